# Optimizing a Trainium2 kernel written in Bass

```python
import math
import jax, jax.numpy as jnp
from jax import lax
import numpy as np

D_MODEL = 1024
BATCH = 4
SEQ = 4096
DEPTH = 4

HEAD_DIM = 64
N_Q_HEADS = 8
N_KV_HEADS = 2
GQA_GROUPS = N_Q_HEADS // N_KV_HEADS
ATTN_WIDTH = N_Q_HEADS * HEAD_DIM
KV_WIDTH = N_KV_HEADS * HEAD_DIM
MIX_WIDTH = D_MODEL // 2
CONV_CH = MIX_WIDTH
CONV_WIDTH = 31
SGU_CH = MIX_WIDTH
SGU_GROUPS = 4
SGU_GROUP_CH = SGU_CH // SGU_GROUPS
CHUNK = 128
WINDOW = 128
BLOCK_Q = 128
GRID_W = 64
ROPE_THETA = 10000.0
IN_WIDTH = 2 * MIX_WIDTH + ATTN_WIDTH + 2 * KV_WIDTH
OUT_WIDTH = 2 * MIX_WIDTH
PEER_HEADS = 8
PEER_NKEYS = 128
PEER_EXPERTS = PEER_NKEYS * PEER_NKEYS
PEER_DKEY = 128
PEER_TOPK = 16
PEER_TOK_CHUNK = 128
EPS = 1e-6
NEG = -1e30
N_EVEN = (DEPTH + 1) // 2
N_ODD = DEPTH // 2

kernel_name = 'hybrid_conv_swa_axial_sgu_peer_encoder'


def rmsnorm(x, g):
    xf = x.astype(jnp.float32)
    y = xf * lax.rsqrt(jnp.mean(xf * xf, axis=-1, keepdims=True) + EPS)
    return (y * g.astype(jnp.float32)).astype(x.dtype)


def layernorm(x, g, b):
    xf = x.astype(jnp.float32)
    mu = jnp.mean(xf, axis=-1, keepdims=True)
    xc = xf - mu
    var = jnp.mean(xc * xc, axis=-1, keepdims=True)
    return (xc * lax.rsqrt(var + EPS) * g.astype(jnp.float32) + b.astype(jnp.float32)).astype(x.dtype)


def rope_angles(pos, dim):
    inv = ROPE_THETA ** (-jnp.arange(0, dim, 2, dtype=jnp.float32) / dim)
    ang = pos.astype(jnp.float32)[:, None] * inv[None, :]
    ang = jnp.concatenate([ang, ang], axis=-1)
    return jnp.cos(ang), jnp.sin(ang)


def apply_rope(x, cos, sin):
    xf = x.astype(jnp.float32)
    x1, x2 = jnp.split(xf, 2, axis=-1)
    rot = jnp.concatenate([-x2, x1], axis=-1)
    return (xf * cos[:, None, :] + rot * sin[:, None, :]).astype(x.dtype)


def apply_axial_rope(x, cos_r, sin_r, cos_c, sin_c):
    half = x.shape[-1] // 2
    return jnp.concatenate([apply_rope(x[..., :half], cos_r, sin_r),
                            apply_rope(x[..., half:], cos_c, sin_c)], axis=-1)


def conformer_conv(a_in, conv_w, conv_b, ln_g, ln_b):
    a = a_in[..., :CONV_CH] * jax.nn.sigmoid(a_in[..., CONV_CH:])
    a = lax.conv_general_dilated(a, conv_w.astype(a.dtype), window_strides=(1,),
                                 padding=[(CONV_WIDTH // 2, CONV_WIDTH // 2)],
                                 dimension_numbers=('NWC', 'WIO', 'NWC'),
                                 feature_group_count=CONV_CH)
    a = a + conv_b.astype(a.dtype)
    a = layernorm(a, ln_g, ln_b)
    return jax.nn.silu(a)


def windowed_sink_attention(q, k, v, sink):
    B, S, _, Dh = q.shape
    nb = S // BLOCK_Q
    qb = q.reshape(B, nb, BLOCK_Q, N_KV_HEADS, GQA_GROUPS, Dh)
    pad = ((0, 0), (BLOCK_Q, BLOCK_Q), (0, 0), (0, 0))
    kp = jnp.pad(k, pad).reshape(B, nb + 2, BLOCK_Q, N_KV_HEADS, Dh)
    vp = jnp.pad(v, pad).reshape(B, nb + 2, BLOCK_Q, N_KV_HEADS, Dh)
    kb = jnp.concatenate([kp[:, :-2], kp[:, 1:-1], kp[:, 2:]], axis=2)
    vb = jnp.concatenate([vp[:, :-2], vp[:, 1:-1], vp[:, 2:]], axis=2)
    s = jnp.einsum('bnqhgd,bnkhd->bhgnqk', qb, kb).astype(jnp.float32) * (Dh ** -0.5)
    blk = jnp.arange(nb)[:, None, None]
    qpos = blk * BLOCK_Q + jnp.arange(BLOCK_Q)[None, :, None]
    kpos = (blk - 1) * BLOCK_Q + jnp.arange(3 * BLOCK_Q)[None, None, :]
    mask = (jnp.abs(kpos - qpos) <= WINDOW) & (kpos >= 0) & (kpos < S)
    s = jnp.where(mask, s, NEG)
    sink_b = sink.astype(jnp.float32).reshape(1, N_KV_HEADS, GQA_GROUPS, 1, 1, 1)
    m = jnp.maximum(jnp.max(s, axis=-1, keepdims=True), sink_b)
    p = jnp.exp(s - m)
    denom = jnp.sum(p, axis=-1, keepdims=True) + jnp.exp(sink_b - m)
    p = (p / denom).astype(v.dtype)
    o = jnp.einsum('bhgnqk,bnkhd->bnqhgd', p, vb)
    return o.reshape(B, S, N_Q_HEADS * Dh)


def dense_block_attention(q, k, v):
    B, S, _, Dh = q.shape
    nb = S // BLOCK_Q
    qb = q.reshape(B, nb, BLOCK_Q, N_KV_HEADS, GQA_GROUPS, Dh).transpose(1, 0, 2, 3, 4, 5)
    scale = Dh ** -0.5

    def one_block(qblk):
        s = jnp.einsum('bqhgd,bkhd->bhgqk', qblk, k).astype(jnp.float32) * scale
        p = jax.nn.softmax(s, axis=-1).astype(v.dtype)
        return jnp.einsum('bhgqk,bkhd->bqhgd', p, v)

    o = lax.map(one_block, qb)
    return o.transpose(1, 0, 2, 3, 4, 5).reshape(B, S, N_Q_HEADS * Dh)


def spatial_gating(d_in, ln_g, ln_b, w_s, b_s):
    z = jax.nn.gelu(d_in)
    u, v = z[..., :SGU_CH], z[..., SGU_CH:]
    v = layernorm(v, ln_g, ln_b)
    B, S, _ = v.shape
    nc = S // CHUNK
    vc = v.reshape(B, nc, CHUNK, SGU_GROUPS, SGU_GROUP_CH)
    mixed = jnp.einsum('gpq,bnqgc->bnpgc', w_s.astype(v.dtype), vc) + b_s.T.astype(v.dtype)[:, :, None]
    return u * mixed.reshape(B, S, SGU_CH)


def peer(h, wq, subkeys, u_tab, v_tab):
    B, S, D = h.shape
    T = B * S
    ht = h.reshape(T, D)
    q = (ht @ wq).reshape(T, PEER_HEADS, 2, PEER_DKEY // 2)
    s = jnp.einsum('thpd,hpkd->thpk', q, subkeys.astype(q.dtype)).astype(jnp.float32)
    sv, si = lax.top_k(s, PEER_TOPK)
    cand = (sv[:, :, 0, :, None] + sv[:, :, 1, None, :]).reshape(T, PEER_HEADS, PEER_TOPK * PEER_TOPK)
    fv, fi = lax.top_k(cand, PEER_TOPK)
    i1 = jnp.take_along_axis(si[:, :, 0, :], fi // PEER_TOPK, axis=-1)
    i2 = jnp.take_along_axis(si[:, :, 1, :], fi % PEER_TOPK, axis=-1)
    eidx = i1 * PEER_NKEYS + i2
    gate = jax.nn.softmax(fv, axis=-1)
    nchunk = T // PEER_TOK_CHUNK

    def chunk_fn(args):
        xc, ec, gc = args
        uc = u_tab[ec]
        a = jnp.einsum('chkd,cd->chk', uc, xc).astype(jnp.float32)
        a = (jax.nn.gelu(a) * gc).astype(xc.dtype)
        vc = v_tab[ec]
        return jnp.einsum('chk,chkd->cd', a, vc)

    out = lax.map(chunk_fn, (ht.reshape(nchunk, PEER_TOK_CHUNK, D),
                             eidx.reshape(nchunk, PEER_TOK_CHUNK, PEER_HEADS, PEER_TOPK),
                             gate.reshape(nchunk, PEER_TOK_CHUNK, PEER_HEADS, PEER_TOPK)))
    return out.reshape(B, S, D)


def setup_inputs(seed: int = 0) -> dict:
    key = jax.random.key(seed)
    ks = jax.random.split(key, 24)
    f32 = jnp.float32

    def nrm(k, shape, scale):
        return jax.random.normal(k, shape, f32) * scale

    return {
        'x': nrm(ks[0], (BATCH, SEQ, D_MODEL), 1.0),
        'mix_norm_g': 1.0 + nrm(ks[1], (DEPTH, D_MODEL), 0.02),
        'ffn_norm_g': 1.0 + nrm(ks[2], (DEPTH, D_MODEL), 0.02),
        'final_norm_g': 1.0 + nrm(ks[3], (D_MODEL,), 0.02),
        'even_w_in': nrm(ks[4], (N_EVEN, D_MODEL, IN_WIDTH), D_MODEL ** -0.5),
        'even_w_out': nrm(ks[5], (N_EVEN, OUT_WIDTH, D_MODEL), OUT_WIDTH ** -0.5),
        'conv_w': nrm(ks[6], (N_EVEN, CONV_WIDTH, 1, CONV_CH), CONV_WIDTH ** -0.5),
        'conv_b': nrm(ks[7], (N_EVEN, CONV_CH), 0.02),
        'conv_ln_g': 1.0 + nrm(ks[8], (N_EVEN, CONV_CH), 0.02),
        'conv_ln_b': nrm(ks[9], (N_EVEN, CONV_CH), 0.02),
        'sink_logits': nrm(ks[10], (N_EVEN, N_Q_HEADS), 0.5),
        'odd_w_in': nrm(ks[11], (N_ODD, D_MODEL, IN_WIDTH), D_MODEL ** -0.5),
        'odd_w_out': nrm(ks[12], (N_ODD, OUT_WIDTH, D_MODEL), OUT_WIDTH ** -0.5),
        'q_norm_g': 1.0 + nrm(ks[13], (N_ODD, HEAD_DIM), 0.02),
        'k_norm_g': 1.0 + nrm(ks[14], (N_ODD, HEAD_DIM), 0.02),
        'sgu_ln_g': 1.0 + nrm(ks[15], (N_ODD, SGU_CH), 0.02),
        'sgu_ln_b': nrm(ks[16], (N_ODD, SGU_CH), 0.02),
        'sgu_w': nrm(ks[17], (N_ODD, SGU_GROUPS, CHUNK, CHUNK), CHUNK ** -0.5),
        'sgu_b': 1.0 + nrm(ks[18], (N_ODD, SGU_GROUPS, CHUNK), 0.02),
        'peer_wq': nrm(ks[19], (DEPTH, D_MODEL, PEER_HEADS * PEER_DKEY), D_MODEL ** -0.5),
        'peer_subkeys': nrm(ks[20], (DEPTH, PEER_HEADS, 2, PEER_NKEYS, PEER_DKEY // 2), (PEER_DKEY // 2) ** -0.5),
        'peer_u': nrm(ks[21], (DEPTH, PEER_EXPERTS, D_MODEL), D_MODEL ** -0.5),
        'peer_v': nrm(ks[22], (DEPTH, PEER_EXPERTS, D_MODEL), (PEER_HEADS * PEER_TOPK) ** -0.5),
    }


def reference(x, mix_norm_g, ffn_norm_g, final_norm_g, even_w_in, even_w_out, conv_w, conv_b,
              conv_ln_g, conv_ln_b, sink_logits, odd_w_in, odd_w_out, q_norm_g, k_norm_g,
              sgu_ln_g, sgu_ln_b, sgu_w, sgu_b, peer_wq, peer_subkeys, peer_u, peer_v):
    B, S, _ = x.shape
    ROWS = S // GRID_W
    pos = jnp.arange(S)
    cos1, sin1 = rope_angles(pos, HEAD_DIM)
    rows = jnp.repeat(jnp.arange(ROWS), GRID_W, total_repeat_length=S)
    cols = jnp.tile(jnp.arange(GRID_W), ROWS)
    cos_r, sin_r = rope_angles(rows, HEAD_DIM // 2)
    cos_c, sin_c = rope_angles(cols, HEAD_DIM // 2)
    o1 = 2 * MIX_WIDTH
    o2 = o1 + ATTN_WIDTH
    o3 = o2 + KV_WIDTH
    for layer in range(DEPTH):
        i = layer // 2
        h = rmsnorm(x, mix_norm_g[layer])
        if layer % 2 == 0:
            z = h @ even_w_in[i]
            a_out = conformer_conv(z[..., :o1], conv_w[i], conv_b[i], conv_ln_g[i], conv_ln_b[i])
            q = apply_rope(z[..., o1:o2].reshape(B, S, N_Q_HEADS, HEAD_DIM), cos1, sin1)
            k = apply_rope(z[..., o2:o3].reshape(B, S, N_KV_HEADS, HEAD_DIM), cos1, sin1)
            v = z[..., o3:].reshape(B, S, N_KV_HEADS, HEAD_DIM)
            b_out = windowed_sink_attention(q, k, v, sink_logits[i])
            y = jnp.concatenate([a_out, b_out], axis=-1) @ even_w_out[i]
        else:
            z = h @ odd_w_in[i]
            c1 = ATTN_WIDTH
            c2 = c1 + KV_WIDTH
            c3 = c2 + KV_WIDTH
            q = rmsnorm(z[..., :c1].reshape(B, S, N_Q_HEADS, HEAD_DIM), q_norm_g[i])
            k = rmsnorm(z[..., c1:c2].reshape(B, S, N_KV_HEADS, HEAD_DIM), k_norm_g[i])
            v = z[..., c2:c3].reshape(B, S, N_KV_HEADS, HEAD_DIM)
            q = apply_axial_rope(q, cos_r, sin_r, cos_c, sin_c)
            k = apply_axial_rope(k, cos_r, sin_r, cos_c, sin_c)
            c_out = dense_block_attention(q, k, v)
            d_out = spatial_gating(z[..., c3:], sgu_ln_g[i], sgu_ln_b[i], sgu_w[i], sgu_b[i])
            y = jnp.concatenate([c_out, d_out], axis=-1) @ odd_w_out[i]
        x = x + y
        h = rmsnorm(x, ffn_norm_g[layer])
        x = x + peer(h, peer_wq[layer], peer_subkeys[layer], peer_u[layer], peer_v[layer])
    return rmsnorm(x, final_norm_g)
```

```python
import numpy as np
from contextlib import ExitStack
import ml_dtypes
import concourse.bass as bass
import concourse.mybir as mybir
from concourse.bass_utils import run_bass_kernel_spmd

F32 = mybir.dt.float32
BF16 = mybir.dt.bfloat16
U32 = mybir.dt.uint32
ALU = mybir.AluOpType
AF = mybir.ActivationFunctionType
AX = mybir.AxisListType

D = 1024
NT = 2048
SEQ = 4096
EPS = 1e-6
ENGS = ['pe', 'act', 'dve', 'pool', 'sp']
NRING = 12
SB_BASE = 16512
SB_TOP = 229344
NEGM = -30000.0


class Sched:
    def __init__(self, nc, stack, self_sync=True):
        self.nc = nc
        self.sem = {e: stack.enter_context(nc.semaphore('c_' + e)) for e in ENGS}
        self.ring = [stack.enter_context(nc.semaphore('d_%d' % i)) for i in range(NRING)]
        self.cnt = {e: 0 for e in ENGS}
        self.ndma = 0
        self.ops = {e: [] for e in ENGS}
        self.seen = {e: {} for e in ENGS}
        self.lastw = {}
        self.readers = {}
        self.self_sync = self_sync
        self.n_instr = 0

    def new_epoch(self, stack):
        self.barrier()
        nc = self.nc
        self.epoch = getattr(self, 'epoch', 0) + 1
        self.sem = {e: stack.enter_context(nc.semaphore('c%d_%s' % (self.epoch, e))) for e in ENGS}
        self.ring = [stack.enter_context(nc.semaphore('d%d_%d' % (self.epoch, i))) for i in range(NRING)]
        self.cnt = {e: 0 for e in ENGS}
        self.ndma = 0
        self.seen = {e: {} for e in ENGS}

    def _tok_wait(self, tok):
        if tok[0] == 'c':
            return (('c', tok[1]), self.sem[tok[1]], tok[2])
        n = tok[1]
        return (('d', n % NRING), self.ring[n % NRING], 16 * (n // NRING + 1))

    def _collect(self, eng, reads, writes):
        need = {}
        toks = []
        for r in reads:
            t = self.lastw.get(r)
            if t is not None:
                toks.append(t)
        for w in writes:
            t = self.lastw.get(w)
            if t is not None:
                toks.append(t)
            toks.extend(self.readers.get(w, ()))
        for t in toks:
            if t[0] == 'c' and t[1] == eng:
                if eng == 'pe' or not self.self_sync:
                    continue
            key, sem, val = self._tok_wait(t)
            if self.seen[eng].get(key, 0) >= val:
                continue
            if key not in need or need[key][1] < val:
                need[key] = (sem, val)
        for key, (sem, val) in need.items():
            self.seen[eng][key] = val
        return list(need.values())

    def _commit(self, tok, reads, writes):
        for r in reads:
            self.readers.setdefault(r, []).append(tok)
        for w in writes:
            self.lastw[w] = tok
            self.readers[w] = []

    def op(self, eng, fn, reads=(), writes=()):
        waits = self._collect(eng, reads, writes)
        self.cnt[eng] += 1
        idx = self.cnt[eng]
        sem = self.sem[eng]

        def emit(e):
            for (s, v) in waits:
                e.wait_ge(s, v)
            fn(e).then_inc(sem, 1)
        self.ops[eng].append(emit)
        self.n_instr += 1 + len(waits)
        tok = ('c', eng, idx)
        self._commit(tok, reads, writes)
        return tok

    def dma(self, out, in_, reads=(), writes=(), q='sp', **kw):
        return self.dmalike(q, lambda e: e.dma_start(out=out, in_=in_, **kw), reads, writes)

    def dmalike(self, q, fn, reads=(), writes=()):
        n = self.ndma
        self.ndma += 1
        waits = self._collect(q, reads, writes)
        if n >= NRING:
            key, sem, val = self._tok_wait(('d', n - NRING))
            if self.seen[q].get(key, 0) < val:
                self.seen[q][key] = val
                waits.append((sem, val))
        rs = self.ring[n % NRING]

        def emit(e):
            for (s, v) in waits:
                e.wait_ge(s, v)
            fn(e).then_inc(rs, 16)
        self.ops[q].append(emit)
        self.n_instr += 1 + len(waits)
        tok = ('d', n)
        self._commit(tok, reads, writes)
        return tok

    def barrier(self):
        for e in ENGS:
            waits = []
            for o in ENGS:
                if o == e or self.cnt[o] == 0:
                    continue
                key = ('c', o)
                if self.seen[e].get(key, 0) < self.cnt[o]:
                    self.seen[e][key] = self.cnt[o]
                    waits.append((self.sem[o], self.cnt[o]))
            for n in range(max(0, self.ndma - NRING), self.ndma):
                key, sem, val = self._tok_wait(('d', n))
                if self.seen[e].get(key, 0) < val:
                    self.seen[e][key] = val
                    waits.append((sem, val))
            if waits:
                def emit(en, waits=waits):
                    for (s, v) in waits:
                        en.wait_ge(s, v)
                self.ops[e].append(emit)
                self.n_instr += len(waits)
        self.lastw.clear()
        self.readers.clear()

    def emit_all(self):
        self.barrier()
        nc = self.nc
        ops = self.ops
        with nc.Block() as block:
            @block.tensor
            def _(e):
                for f in ops['pe']:
                    f(e)

            @block.scalar
            def _(e):
                for f in ops['act']:
                    f(e)

            @block.vector
            def _(e):
                for f in ops['dve']:
                    f(e)

            @block.gpsimd
            def _(e):
                for f in ops['pool']:
                    f(e)

            @block.sync
            def _(e):
                for f in ops['sp']:
                    f(e)


_DT_SIZE = {F32: 4, BF16: 2, U32: 4}


class Bld:
    def __init__(self, nc, st):
        self.nc = nc
        self.S = Sched(nc, st)
        self.banks = [st.enter_context(nc.psum_tensor("pb%d" % i, [128, 512], F32)) for i in range(7)]
        self.bankb = st.enter_context(nc.psum_tensor("pb7", [128, 1024], BF16))
        self.p = SB_BASE
        self.nalloc = 0
        self.rot = 0

    def sb(self, shape, dt, name):
        size = _DT_SIZE[dt]
        for s in shape[1:]:
            size *= s
        off = (self.p + 63) // 64 * 64
        assert off + size <= SB_TOP, ("SBUF overflow", name, off + size - SB_TOP)
        self.p = off + size
        self.nalloc += 1
        return self.nc.alloc_sbuf_tensor_at("%s_%d" % (name, self.nalloc), list(shape), dt, offset=off)

    def mark(self):
        return self.p

    def reset(self, m):
        self.S.barrier()
        self.p = m

    def nb(self, lst):
        i = lst[self.rot % len(lst)]
        self.rot += 1
        return self.banks[i], 'pb%d' % i

    def mm(self, out, lhsT, rhs, start, stop, r, w):
        self.S.op('pe', lambda e: e.matmul(out, lhsT=lhsT, rhs=rhs, start=start, stop=stop), r, w)

    def tr(self, out, in_, ident, r, w):
        self.S.op('pe', lambda e: e.transpose(out=out, in_=in_, identity=ident), r, w)

    def act(self, out, in_, func, r, w, **kw):
        self.S.op('act', lambda e: e.activation(out=out, in_=in_, func=func, **kw), r, w)

    def tt(self, eng, out, in0, in1, op, r, w):
        self.S.op(eng, lambda e: e.tensor_tensor(out=out, in0=in0, in1=in1, op=op), r, w)

    def ts(self, eng, out, in0, s1, s2, op0, op1, r, w):
        if op1 is None:
            self.S.op(eng, lambda e: e.tensor_scalar(out=out, in0=in0, scalar1=s1, scalar2=None, op0=op0), r, w)
        else:
            self.S.op(eng, lambda e: e.tensor_scalar(out=out, in0=in0, scalar1=s1, scalar2=s2, op0=op0, op1=op1), r, w)

    def stt(self, out, in0, scalar, in1, op0, op1, r, w):
        self.S.op('dve', lambda e: e.scalar_tensor_tensor(out=out, in0=in0, scalar=scalar, in1=in1, op0=op0, op1=op1), r, w)

    def cp(self, eng, out, in_, r, w):
        if eng == 'act':
            self.S.op('act', lambda e: e.copy(out=out, in_=in_), r, w)
        else:
            self.S.op(eng, lambda e: e.tensor_copy(out=out, in_=in_), r, w)

    def recip(self, out, in_, r, w):
        self.S.op('dve', lambda e: e.reciprocal(out=out, in_=in_), r, w)

    def consts(self):
        S = self.S
        self.identf = self.sb([128, 128], F32, 'identf')
        self.identb = self.sb([128, 128], BF16, 'identb')
        self.iota = self.sb([128, 128], F32, 'iota')
        self.onesf = self.sb([128, 128], F32, 'onesf')
        self.onesb = self.sb([128, 128], BF16, 'onesb')
        self.iotab = self.sb([128, 128], BF16, 'iotab')
        S.op('pool', lambda e: e.iota(self.iota[:], pattern=[[1, 128]], base=0, channel_multiplier=0,
                                      allow_small_or_imprecise_dtypes=True), (), ['iota'])
        S.op('pool', lambda e: e.iota(self.identf[:], pattern=[[1, 128]], base=0, channel_multiplier=-1,
                                      allow_small_or_imprecise_dtypes=True), (), ['identf'])
        self.ts('dve', self.identf[:], self.identf[:], 0.0, None, ALU.is_equal, None, ['identf'], ['identf'])
        self.cp('dve', self.identb[:], self.identf[:], ['identf'], ['identb'])
        self.cp('dve', self.iotab[:], self.iota[:], ['iota'], ['iotab'])
        S.op('pool', lambda e: e.memset(self.onesf[:], 1.0), (), ['onesf'])
        S.op('pool', lambda e: e.memset(self.onesb[:], 1.0), (), ['onesb'])


def rms_rstd(b, xt, xkey, ssq, sd, rstd, col, junk, junkkey, tag):
    b.act(junk, xt, AF.Square, [xkey], [junkkey, tag + 'ssq'], accum_out=ssq[:, col:col + 1])
    b.act(sd[:, col:col + 1], ssq[:, col:col + 1], AF.Sqrt, [tag + 'ssq'], [tag + 'sd'], scale=1.0 / D, bias=EPS)
    b.recip(rstd[:, col:col + 1], sd[:, col:col + 1], [tag + 'sd'], [tag + 'rstd'])


def row_tiles(ap, n):
    return [ap[t * 128:(t + 1) * 128, :] for t in range(n)]


def norm_tiles_bf16(b, src_tiles, gcol, hT, hkey, W):
    S = b.S
    for tt in range(len(src_tiles)):
        sl = W['xslot']
        W['xslot'] ^= 1
        xt = W['xt'][sl]
        xn = W['xn'][sl]
        S.dma(xt[:], src_tiles[tt], (), ['xt%d' % sl])
        rms_rstd(b, xt[:], 'xt%d' % sl, W['ssq'], W['sd'], W['rstd'], sl, xn[:], 'xn%d' % sl, 'n%d' % sl)
        b.act(xn[:], xt[:], AF.Copy, ['xt%d' % sl, 'n%drstd' % sl], ['xn%d' % sl], scale=W['rstd'][:, sl:sl + 1])
        for c in range(8):
            b.tr(b.bankb[:, c * 128:(c + 1) * 128], xn[:, c * 128:(c + 1) * 128], b.identb[:], ['xn%d' % sl, 'identb'], ['pb7'])
        b.tt('dve', hT[:, :, tt * 128:(tt + 1) * 128], b.bankb[:].rearrange("p (c t) -> p c t", c=8),
             gcol.unsqueeze(2).to_broadcast([128, 8, 128]), ALU.mult, ['pb7', 'gcol'], [hkey])


def proj_fm(b, bank, bkey, M, wT, wkey, col0, hT, hkey, t0, ntok):
    for kc in range(8):
        b.mm(bank[0:M, 0:ntok], wT[:, kc, col0:col0 + M], hT[:, kc, t0:t0 + ntok], kc == 0, kc == 7, [wkey, hkey], [bkey])


def load_cast(b, dst, dkey, src, shape, stage, skey, eng='pool'):
    b.S.dma(stage, src, (), [skey])
    b.cp(eng, dst, stage, [skey], [dkey])


TB = 256
JG = 2


def emit_peer_prep(b, d):
    S = b.S
    m = b.mark()
    st = [b.sb([128, JG, 1024], F32, 'pst%d' % i) for i in range(4)]
    ob = [b.sb([128, JG, 1024], BF16, 'pob%d' % i) for i in range(4)]
    n = 0
    for g in range(128 // JG):
        for (src, dst, eng) in ((d['uth'], d['uts'], 'act'), (d['vbh'], d['vbs'], 'pool')):
            sl = n % 4
            n += 1
            S.dma(st[sl][:], src[g * JG:(g + 1) * JG].rearrange("j p f -> p j f"), (), ['pst%d' % sl])
            b.cp(eng, ob[sl][:], st[sl][:], ['pst%d' % sl], ['pob%d' % sl])
            S.dma(dst[g], ob[sl][:], ['pob%d' % sl], ())
    b.reset(m)


def emit_peer(b, d, last, x1, xout, ntok):
    S = b.S
    m = b.mark()
    B = b.banks
    NTT = TB // 128
    wq = b.sb([128, 8, 1024], F32, 'wq')
    skbd = b.sb([128, 8, 256], F32, 'skbd')
    gffn = b.sb([128, 8], F32, 'gffn')
    S.dma(wq[:], d['wq'], (), ['wq'])
    S.dma(skbd[:], d['skbd'], (), ['skbd'])
    S.dma(gffn[:], d['gffn'], (), ['gffn'])
    if last:
        gfin = b.sb([128, 1024], F32, 'gfin')
        S.dma(gfin[:], d['gfin'].partition_broadcast(128), (), ['gfin'])
    xt = [b.sb([128, 1024], F32, 'pxt%d' % i) for i in range(NTT)]
    xn = b.sb([128, 1024], F32, 'pxn')
    ssq = b.sb([128, 2], F32, 'pssq')
    sd = b.sb([128, 2], F32, 'psd')
    rstd = b.sb([128, 2], F32, 'prstd')
    h2T = b.sb([128, 8, TB], F32, 'h2T')
    h2Tb = b.sb([128, 8, TB], BF16, 'h2Tb')
    qh = [b.sb([128, TB], F32, 'qh%d' % i) for i in range(2)]
    ssb = [b.sb([128, 8, 2, 128], F32, 'ssb%d' % i) for i in range(NTT)]
    v16 = b.sb([128, 8, 2, 16], F32, 'v16')
    i16 = b.sb([128, 8, 2, 16], U32, 'i16')
    i16f = b.sb([128, 8, 2, 16], F32, 'i16f')
    cand = b.sb([128, 8, 16, 16], F32, 'cand')
    fv = b.sb([128, 8, 16], F32, 'fv')
    fi = b.sb([128, 8, 16], U32, 'fi')
    fa = b.sb([128, 8, 16], U32, 'fa')
    fb = b.sb([128, 8, 16], U32, 'fb')
    faf = b.sb([128, 8, 16], F32, 'faf')
    fbf = b.sb([128, 8, 16], F32, 'fbf')
    gate = b.sb([128, 8, 16], F32, 'gate')
    gsum = b.sb([128, 8], F32, 'gsum')
    i12 = b.sb([128, 2, 8, 16], F32, 'i12')
    T3 = b.sb([128, 3, TB], F32, 'T3')
    Pt = [b.sb([128, 128], BF16, 'Pt%d' % i) for i in range(4)]
    Qt = [b.sb([128, 128], BF16, 'Qt%d' % i) for i in range(4)]
    Gs = b.sb([128, TB, 128], BF16, 'Gs')
    ut = [b.sb([128, JG, 8, 128], BF16, 'ut%d' % i) for i in range(2)]
    vb = [b.sb([128, JG, 1024], BF16, 'vb%d' % i) for i in range(2)]
    gl = [b.sb([128, TB], F32, 'gl%d' % i) for i in range(3)]
    wT = [b.sb([128, TB], BF16, 'wT%d' % i) for i in range(3)]
    x2 = b.sb([128, 1024], F32, 'x2')
    y2 = b.sb([128, 1024], F32, 'y2')
    iota16 = b.iota[:, 0:16]
    print('PEER sbuf end', b.p, SB_TOP)

    nblk = ntok // TB
    for blk in range(nblk):
        for tt in range(NTT):
            T = blk * NTT + tt
            S.dma(xt[tt][:], x1[T * 128:(T + 1) * 128, :], (), ['pxt%d' % tt])
            rms_rstd(b, xt[tt][:], 'pxt%d' % tt, ssq, sd, rstd, 0, xn[:], 'pxn', 'pn')
            b.act(xn[:], xt[tt][:], AF.Copy, ['pxt%d' % tt, 'pnrstd'], ['pxn'], scale=rstd[:, 0:1])
            for c in range(8):
                b.tr(B[c // 4][:, (c % 4) * 128:(c % 4 + 1) * 128], xn[:, c * 128:(c + 1) * 128], b.identf[:],
                     ['pxn', 'identf'], ['pb%d' % (c // 4)])
            for hb in range(2):
                b.tt('dve', h2T[:, hb * 4:(hb + 1) * 4, tt * 128:(tt + 1) * 128], B[hb][:].rearrange("p (c t) -> p c t", c=4),
                     gffn[:, hb * 4:(hb + 1) * 4].unsqueeze(2).to_broadcast([128, 4, 128]), ALU.mult, ['pb%d' % hb, 'gffn'], ['h2T'])
        b.cp('pool', h2Tb[:], h2T[:], ['h2T'], ['h2Tb'])
        for h in range(8):
            bq, kq = b.nb([2, 3])
            for kc in range(8):
                b.mm(bq[:, 0:TB], wq[:, kc, h * 128:(h + 1) * 128], h2T[:, kc, :], kc == 0, kc == 7, ['wq', 'h2T'], [kq])
            q = qh[h % 2]
            b.cp('act', q[:], bq[:, 0:TB], [kq], ['qh%d' % (h % 2)])
            for tt in range(NTT):
                bs_, ks = b.nb([4, 5, 6])
                b.mm(bs_[:, 0:256], q[:, tt * 128:(tt + 1) * 128], skbd[:, h, :], True, True, ['qh%d' % (h % 2), 'skbd'], [ks])
                b.cp('act', ssb[tt][:, h, :, :], bs_[:, 0:256].rearrange("p (s k) -> p s k", s=2), [ks], ['ss%d_%d_0' % (tt, h), 'ss%d_%d_1' % (tt, h)])
        for tt in range(NTT):
            HP = [(h, p) for h in range(8) for p in range(2)]
            sk = lambda h, p: 'ss%d_%d_%d' % (tt, h, p)
            for (h, p) in HP:
                S.op('dve', lambda e, h=h, p=p, tt=tt: e.max(out=v16[:, h, p, 0:8], in_=ssb[tt][:, h, p, :]), [sk(h, p)], ['va%d_%d' % (h, p)])
            for (h, p) in HP:
                S.op('dve', lambda e, h=h, p=p, tt=tt: e.max_index(out=i16[:, h, p, 0:8], in_max=v16[:, h, p, 0:8], in_values=ssb[tt][:, h, p, :]),
                     [sk(h, p), 'va%d_%d' % (h, p)], ['ia%d_%d' % (h, p)])
            for (h, p) in HP:
                S.op('dve', lambda e, h=h, p=p, tt=tt: e.match_replace(out=ssb[tt][:, h, p, :], in_to_replace=v16[:, h, p, 0:8], in_values=ssb[tt][:, h, p, :], imm_value=-1e30),
                     ['va%d_%d' % (h, p)], [sk(h, p)])
            for (h, p) in HP:
                S.op('dve', lambda e, h=h, p=p, tt=tt: e.max(out=v16[:, h, p, 8:16], in_=ssb[tt][:, h, p, :]), [sk(h, p)], ['vb%d_%d' % (h, p)])
            for (h, p) in HP:
                S.op('dve', lambda e, h=h, p=p, tt=tt: e.max_index(out=i16[:, h, p, 8:16], in_max=v16[:, h, p, 8:16], in_values=ssb[tt][:, h, p, :]),
                     [sk(h, p), 'vb%d_%d' % (h, p)], ['ib%d_%d' % (h, p)])
            VK = ['va%d_%d' % hp for hp in HP] + ['vb%d_%d' % hp for hp in HP]
            IK = ['ia%d_%d' % hp for hp in HP] + ['ib%d_%d' % hp for hp in HP]
            CK = ['cand%d' % h for h in range(8)]
            FVK = ['fva%d' % h for h in range(8)] + ['fvb%d' % h for h in range(8)]
            FIK = ['fia%d' % h for h in range(8)] + ['fib%d' % h for h in range(8)]
            b.cp('dve', i16f[:], i16[:], IK, ['i16f'])
            b.tt('dve', cand[:], v16[:, :, 0, :].unsqueeze(3).to_broadcast([128, 8, 16, 16]),
                 v16[:, :, 1, :].unsqueeze(2).to_broadcast([128, 8, 16, 16]), ALU.add, VK, CK)
            cs = lambda h: cand[:, h, :, :].rearrange("p a b -> p (a b)")
            for h in range(8):
                S.op('dve', lambda e, h=h: e.max(out=fv[:, h, 0:8], in_=cs(h)), ['cand%d' % h], ['fva%d' % h])
            for h in range(8):
                S.op('dve', lambda e, h=h: e.max_index(out=fi[:, h, 0:8], in_max=fv[:, h, 0:8], in_values=cs(h)), ['cand%d' % h, 'fva%d' % h], ['fia%d' % h])
            for h in range(8):
                S.op('dve', lambda e, h=h: e.match_replace(out=cs(h), in_to_replace=fv[:, h, 0:8], in_values=cs(h), imm_value=-1e30), ['fva%d' % h, 'cand%d' % h], ['cand%d' % h])
            for h in range(8):
                S.op('dve', lambda e, h=h: e.max(out=fv[:, h, 8:16], in_=cs(h)), ['cand%d' % h], ['fvb%d' % h])
            for h in range(8):
                S.op('dve', lambda e, h=h: e.max_index(out=fi[:, h, 8:16], in_max=fv[:, h, 8:16], in_values=cs(h)), ['cand%d' % h, 'fvb%d' % h], ['fib%d' % h])
            b.tt('dve', gate[:], fv[:], fv[:, :, 0:1].to_broadcast([128, 8, 16]), ALU.subtract, FVK, ['gate'])
            b.act(gate[:], gate[:], AF.Exp, ['gate'], ['gate'])
            S.op('dve', lambda e: e.tensor_reduce(out=gsum[:], in_=gate[:], axis=AX.X, op=ALU.add), ['gate'], ['gsum'])
            b.recip(gsum[:], gsum[:], ['gsum'], ['gsum'])
            b.tt('dve', gate[:], gate[:], gsum[:].unsqueeze(2).to_broadcast([128, 8, 16]), ALU.mult, ['gate', 'gsum'], ['gate'])
            S.op('dve', lambda e: e.tensor_single_scalar(out=fa[:], in_=fi[:], scalar=4, op=ALU.logical_shift_right), FIK, ['fa'])
            S.op('dve', lambda e: e.tensor_single_scalar(out=fb[:], in_=fi[:], scalar=15, op=ALU.bitwise_and), FIK, ['fb'])
            b.cp('dve', faf[:], fa[:], ['fa'], ['faf'])
            b.cp('dve', fbf[:], fb[:], ['fb'], ['fbf'])
            for p, pos in ((0, faf), (1, fbf)):
                b.tt('dve', cand[:].rearrange("p h k a -> p (h k) a"), pos[:].rearrange("p h k -> p (h k)").unsqueeze(2).to_broadcast([128, 128, 16]),
                     iota16.unsqueeze(1).to_broadcast([128, 128, 16]), ALU.is_equal, ['faf', 'fbf'] + CK, CK)
                b.tt('dve', cand[:], cand[:], i16f[:, :, p, :].unsqueeze(2).to_broadcast([128, 8, 16, 16]), ALU.mult, CK + ['i16f'], CK)
                S.op('dve', lambda e, p=p: e.tensor_reduce(out=i12[:, p, :, :], in_=cand[:], axis=AX.X, op=ALU.add), CK, ['i12'])
            bt, kt = b.nb([2, 3])
            b.tr(bt[:, 0:128], i12[:, 0, :, :].rearrange("p h k -> p (h k)"), b.identf[:], ['i12', 'identf'], [kt])
            b.tr(bt[:, 128:256], i12[:, 1, :, :].rearrange("p h k -> p (h k)"), b.identf[:], ['i12', 'identf'], [kt])
            b.tr(bt[:, 256:384], gate[:].rearrange("p h k -> p (h k)"), b.identf[:], ['gate', 'identf'], [kt])
            b.cp('act', T3[:, :, tt * 128:(tt + 1) * 128], bt[:, 0:384].rearrange("p (a t) -> p a t", a=3), [kt], ['T3'])
        for t in range(TB):
            sl = t % 4
            b.ts('dve', Pt[sl][:], b.iotab[:], T3[:, 0, t:t + 1], T3[:, 2, t:t + 1], ALU.is_equal, ALU.mult, ['iotab', 'T3'], ['Pt%d' % sl])
            b.ts('dve', Qt[sl][:], b.iotab[:], T3[:, 1, t:t + 1], None, ALU.is_equal, None, ['iotab', 'T3'], ['Qt%d' % sl])
            bi = (t // 4) % 2
            b.mm(B[bi][:, (t % 4) * 128:(t % 4 + 1) * 128], Pt[sl][:], Qt[sl][:], True, True, ['Pt%d' % sl, 'Qt%d' % sl], ['pb%d' % bi])
            if t % 4 == 3:
                b.cp('act', Gs[:, t - 3:t + 1, :], B[bi][:].rearrange("p (t j) -> p t j", t=4), ['pb%d' % bi], ['Gs'])
        for g in range(128 // JG):
            sl = g % 2
            S.dma(ut[sl][:].rearrange("p j c i -> p (j c i)"), d['uts'][g].rearrange("p j f -> p (j f)"), (), ['ut%d' % sl])
            S.dma(vb[sl][:].rearrange("p j f -> p (j f)"), d['vbs'][g].rearrange("p j f -> p (j f)"), (), ['vb%d' % sl])
            for jj in range(JG):
                j = g * JG + jj
                ba, ka = B[j % 3], 'pb%d' % (j % 3)
                for dc in range(8):
                    b.mm(ba[:, 0:TB], ut[sl][:, jj, dc, :], h2Tb[:, dc, :], dc == 0, dc == 7, ['ut%d' % sl, 'h2Tb'], [ka])
                w = j % 3
                b.act(gl[w][:], ba[:, 0:TB], AF.Gelu_apprx_tanh, [ka], ['gl%d' % w])
                b.tt('dve', wT[w][:], gl[w][:], Gs[:, :, j], ALU.mult, ['gl%d' % w, 'Gs'], ['wT%d' % w])
                for tt in range(NTT):
                    for hh in range(2):
                        bo = 3 + tt * 2 + hh
                        b.mm(B[bo][:, :], wT[w][:, tt * 128:(tt + 1) * 128], vb[sl][:, jj, hh * 512:(hh + 1) * 512], j == 0, j == 127,
                             ['wT%d' % w, 'vb%d' % sl], ['pb%d' % bo])
        for tt in range(NTT):
            T = blk * NTT + tt
            for hh in range(2):
                bo = 3 + tt * 2 + hh
                b.tt('dve', x2[:, hh * 512:(hh + 1) * 512], B[bo][:, :], xt[tt][:, hh * 512:(hh + 1) * 512], ALU.add,
                     ['pb%d' % bo, 'pxt%d' % tt], ['x2'])
            if last:
                rms_rstd(b, x2[:], 'x2', ssq, sd, rstd, 1, y2[:], 'y2', 'fn')
                b.act(y2[:], x2[:], AF.Copy, ['x2', 'fnrstd'], ['y2'], scale=rstd[:, 1:2])
                b.tt('pool', y2[:], y2[:], gfin[:], ALU.mult, ['y2', 'gfin'], ['y2'])
                S.dma(xout[T * 128:(T + 1) * 128, :], y2[:], ['y2'], ())
            else:
                S.dma(xout[T * 128:(T + 1) * 128, :], x2[:], ['x2'], ())
    b.reset(m)


def mixer_common_alloc(b, d, nctx_cols, rope_src):
    S = b.S
    W = {'xslot': 0}
    W['xt'] = [b.sb([128, 1024], F32, 'mxt%d' % i) for i in range(2)]
    W['xn'] = [b.sb([128, 1024], BF16, 'mxn%d' % i) for i in range(2)]
    W['ssq'] = b.sb([128, 2], F32, 'mssq')
    W['sd'] = b.sb([128, 2], F32, 'msd')
    W['rstd'] = b.sb([128, 2], F32, 'mrstd')
    W['gcol'] = b.sb([128, 8], F32, 'gcol')
    S.dma(W['gcol'][:], d['gmix'], (), ['gcol'])
    W['win'] = b.sb([128, 8, 1792], BF16, 'win')
    W['wpm'] = b.sb([128, 8, 640], BF16, 'wpm')
    wst = b.sb([128, 1792], F32, 'wst')
    for kc in range(8):
        S.dma(wst[:], d['w_in'][:, kc, :], (), ['wst'])
        b.cp('pool', W['win'][:, kc, :], wst[:], ['wst'], ['win'])
        S.dma(wst[:, 0:640], d['w_perm'][:, kc, :], (), ['wst'])
        b.cp('pool', W['wpm'][:, kc, :], wst[:, 0:640], ['wst'], ['wpm'])
    W['rope'] = b.sb([64, 2, nctx_cols], F32, 'rope')
    if rope_src is not None:
        S.dma(W['rope'][:], rope_src, (), ['rope'])
    W['hT'] = [b.sb([128, 8, 512], BF16, 'hT%d' % i) for i in range(2)]
    W['t1'] = [b.sb([128, 512], F32, 't1_%d' % i) for i in range(2)]
    W['t2'] = [b.sb([128, 512], F32, 't2_%d' % i) for i in range(2)]
    W['hs'] = 0
    return W


def out_proj(b, d, c, catA, catH):
    S = b.S
    B = b.banks
    woA = b.sb([128, 4, 1024], BF16, 'woA')
    woH = b.sb([64, 8, 1024], BF16, 'woH')
    wst = b.sb([128, 2, 1024], F32, 'wost')
    for c2_ in range(2):
        S.dma(wst[:], d['w_outA'][:, c2_ * 2:(c2_ + 1) * 2, :], (), ['wost'])
        b.cp('pool', woA[:, c2_ * 2:(c2_ + 1) * 2, :], wst[:], ['wost'], ['woA'])
    for c4 in range(4):
        S.dma(wst[0:64], d['w_outH'][:, c4 * 2:(c4 + 1) * 2, :], (), ['wost'])
        b.cp('pool', woH[:, c4 * 2:(c4 + 1) * 2, :], wst[0:64], ['wost'], ['woH'])
    xt = [b.sb([128, 1024], F32, 'oxt%d' % i) for i in range(2)]
    xo = [b.sb([128, 1024], F32, 'oxo%d' % i) for i in range(2)]
    for T in range(NT // 128):
        sl = T % 2
        S.dma(xt[sl][:], c['x_own'][T * 128:(T + 1) * 128, :], (), ['oxt%d' % sl])
        for hh in range(2):
            bo, ko = b.nb([0, 1, 2, 3])
            n = 0
            for cc in range(4):
                b.mm(bo[:, :], catA[:, cc, T * 128:(T + 1) * 128], woA[:, cc, hh * 512:(hh + 1) * 512], n == 0, False, ['catA', 'woA'], [ko])
                n += 1
            for h in range(8):
                b.mm(bo[:, :], catH[:, h, T * 128:(T + 1) * 128], woH[:, h, hh * 512:(hh + 1) * 512], False, h == 7, ['catH', 'woH'], [ko])
            b.tt('dve', xo[sl][:, hh * 512:(hh + 1) * 512], bo[:, :], xt[sl][:, hh * 512:(hh + 1) * 512], ALU.add, [ko, 'oxt%d' % sl], ['oxo%d' % sl])
        S.dma(c['x1'][T * 128:(T + 1) * 128, :], xo[sl][:], ['oxo%d' % sl], ())


def rope_epilogue(b, W, pq, kq, pp, kp, c0, n, out, okey, rstd=None, rkey=None):
    i = W['hs']
    W['hs'] ^= 1
    t1 = W['t1'][i]
    t2 = W['t2'][i]
    b.tt('dve', t1[0:64, 0:n], pq[0:64, 0:n], W['rope'][:, 0, c0:c0 + n], ALU.mult, [kq, 'rope'], ['t1_%d' % i])
    b.tt('dve', t2[0:64, 0:n], pp[0:64, 0:n], W['rope'][:, 1, c0:c0 + n], ALU.mult, [kp, 'rope'], ['t2_%d' % i])
    if rstd is None:
        b.tt('pool', out, t1[0:64, 0:n], t2[0:64, 0:n], ALU.add, ['t1_%d' % i, 't2_%d' % i], [okey])
    else:
        b.tt('pool', t1[0:64, 0:n], t1[0:64, 0:n], t2[0:64, 0:n], ALU.add, ['t1_%d' % i, 't2_%d' % i], ['t1_%d' % i])
        b.tt('pool', out, t1[0:64, 0:n], rstd, ALU.mult, ['t1_%d' % i, rkey], [okey])


def emit_mixer_even(b, d, c):
    S = b.S
    B = b.banks
    m0 = b.mark()
    aT = b.sb([128, 4, NT + 30], F32, 'aT')
    qT = b.sb([64, 8, NT], BF16, 'qT')
    kT = b.sb([64, 2, NT + 256], BF16, 'kT')
    Vt = b.sb([128, 18, 128], BF16, 'Vt')
    m1 = b.mark()
    W = mixer_common_alloc(b, d, NT + 256, c['rope'])
    sig = [b.sb([128, 512], F32, 'sig%d' % i) for i in range(2)]
    atmp = b.sb([128, 256], F32, 'atmp')
    win, wpm = W['win'], W['wpm']

    def kproj(hT, hk, ntok, tcol, kcol):
        for j in range(2):
            pq, kq = b.nb([0, 1, 2, 3, 4, 5])
            proj_fm(b, pq, kq, 64, win, 'win', 1536 + j * 64, hT, hk, 0, ntok)
            pp, kp = b.nb([0, 1, 2, 3, 4, 5])
            proj_fm(b, pp, kp, 64, wpm, 'wpm', 512 + j * 64, hT, hk, 0, ntok)
            rope_epilogue(b, W, pq, kq, pp, kp, tcol, ntok, kT[:, j, kcol:kcol + ntok], 'kT')

    def vproj(hT, hk, ntiles, vt0):
        for tt in range(ntiles):
            for kc in range(8):
                b.mm(B[6][:, 0:128], hT[:, kc, tt * 128:(tt + 1) * 128], win[:, kc, 1664:1792], kc == 0, kc == 7, ['win', hk], ['pb6'])
            b.cp('act', Vt[:, vt0 + tt, :], B[6][:, 0:128], ['pb6'], ['Vt'])

    hT = W['hT'][0]
    norm_tiles_bf16(b, [c['ctxL'], c['ctxR']], W['gcol'][:], hT, 'hT0', W)
    kproj(hT, 'hT0', 128, NT, 0)
    for j in range(2):
        pq, kq = b.nb([0, 1, 2, 3, 4, 5])
        for kc in range(8):
            b.mm(pq[0:64, 0:128], win[:, kc, 1536 + j * 64:1536 + (j + 1) * 64], hT[:, kc, 128:256], kc == 0, kc == 7, ['win', 'hT0'], [kq])
        pp, kp = b.nb([0, 1, 2, 3, 4, 5])
        for kc in range(8):
            b.mm(pp[0:64, 0:128], wpm[:, kc, 512 + j * 64:512 + (j + 1) * 64], hT[:, kc, 128:256], kc == 0, kc == 7, ['wpm', 'hT0'], [kp])
        rope_epilogue(b, W, pq, kq, pp, kp, NT + 128, 128, kT[:, j, NT + 128:NT + 256], 'kT')
    vproj(hT, 'hT0', 1, 0)
    for kc in range(8):
        b.mm(B[6][:, 0:128], hT[:, kc, 128:256], win[:, kc, 1664:1792], kc == 0, kc == 7, ['win', 'hT0'], ['pb6'])
    b.cp('act', Vt[:, 17, :], B[6][:, 0:128], ['pb6'], ['Vt'])
    for cc in range(4):
        pa, ka = b.nb([0, 1, 2, 3, 4, 5])
        proj_fm(b, pa, ka, 128, win, 'win', cc * 128, hT, 'hT0', 0, 256)
        pg, kg = b.nb([0, 1, 2, 3, 4, 5])
        proj_fm(b, pg, kg, 128, win, 'win', 512 + cc * 128, hT, 'hT0', 0, 256)
        b.act(sig[0][:, 0:256], pg[:, 0:256], AF.Sigmoid, [kg], ['sig0'])
        b.tt('dve', atmp[:], pa[:, 0:256], sig[0][:, 0:256], ALU.mult, [ka, 'sig0'], ['atmp'])
        b.cp('act', aT[:, cc, 0:15], atmp[:, 113:128], ['atmp'], ['aT'])
        b.cp('act', aT[:, cc, NT + 15:NT + 30], atmp[:, 128:143], ['atmp'], ['aT'])
    for g in range(4):
        hi = g % 2
        hT = W['hT'][hi]
        hk = 'hT%d' % hi
        norm_tiles_bf16(b, row_tiles(c['x_own'][g * 512:(g + 1) * 512, :], 4), W['gcol'][:], hT, hk, W)
        for cc in range(4):
            pa, ka = b.nb([0, 1, 2, 3, 4, 5])
            proj_fm(b, pa, ka, 128, win, 'win', cc * 128, hT, hk, 0, 512)
            pg, kg = b.nb([0, 1, 2, 3, 4, 5])
            proj_fm(b, pg, kg, 128, win, 'win', 512 + cc * 128, hT, hk, 0, 512)
            si = cc % 2
            b.act(sig[si][:], pg[:, :], AF.Sigmoid, [kg], ['sig%d' % si])
            b.tt('dve', aT[:, cc, 15 + g * 512:15 + (g + 1) * 512], pa[:, :], sig[si][:], ALU.mult, [ka, 'sig%d' % si], ['aT'])
        for h in range(8):
            pq, kq = b.nb([0, 1, 2, 3, 4, 5])
            proj_fm(b, pq, kq, 64, win, 'win', 1024 + h * 64, hT, hk, 0, 512)
            pp, kp = b.nb([0, 1, 2, 3, 4, 5])
            proj_fm(b, pp, kp, 64, wpm, 'wpm', h * 64, hT, hk, 0, 512)
            rope_epilogue(b, W, pq, kq, pp, kp, g * 512, 512, qT[:, h, g * 512:(g + 1) * 512], 'qT')
        for j in range(2):
            pq, kq = b.nb([0, 1, 2, 3, 4, 5])
            proj_fm(b, pq, kq, 64, win, 'win', 1536 + j * 64, hT, hk, 0, 512)
            pp, kp = b.nb([0, 1, 2, 3, 4, 5])
            proj_fm(b, pp, kp, 64, wpm, 'wpm', 512 + j * 64, hT, hk, 0, 512)
            rope_epilogue(b, W, pq, kq, pp, kp, g * 512, 512, kT[:, j, 128 + g * 512:128 + (g + 1) * 512], 'kT')
        vproj(hT, hk, 4, 1 + g * 4)
    b.reset(m1)
    catA = b.sb([128, 4, NT], BF16, 'catA')
    catH = b.sb([64, 8, NT], BF16, 'catH')
    m2_ = b.mark()
    mask = b.sb([128, 3, 384], BF16, 'mask')
    S.dma(mask[:], c['mask3'], (), ['mask'])
    esink = b.sb([64, 8], F32, 'esink')
    S.dma(esink[:], d['sink'].partition_broadcast(64), (), ['esink'])
    b.act(esink[:], esink[:], AF.Exp, ['esink'], ['esink'])
    pT = [b.sb([128, 384], BF16, 'pT%d' % i) for i in range(3)]
    dn = [b.sb([64, 128], F32, 'dn%d' % i) for i in range(2)]
    it = 0
    for h in range(8):
        kv = h // 4
        for qb in range(16):
            msel = 1 if qb == 0 else (2 if qb == 15 else 0)
            bs_, ks = b.nb([0, 1, 2])
            b.mm(bs_[:, 0:384], b.identb[:], mask[:, msel, :], True, False, ['identb', 'mask'], [ks])
            for j in range(3):
                b.mm(bs_[:, j * 128:(j + 1) * 128], kT[:, kv, (qb + j) * 128:(qb + j + 1) * 128], qT[:, h, qb * 128:(qb + 1) * 128],
                     False, j == 2, ['kT', 'qT'], [ks])
            pi = it % 3
            b.act(pT[pi][:], bs_[:, 0:384], AF.Exp, [ks], ['pT%d' % pi], scale=0.125)
            bv, kvk = b.nb([3, 4, 5])
            for j in range(3):
                b.mm(bv[0:64, 0:128], Vt[:, qb + j, kv * 64:(kv + 1) * 64], pT[pi][:, j * 128:(j + 1) * 128], j == 0, j == 2, ['Vt', 'pT%d' % pi], [kvk])
            for j in range(3):
                b.mm(bv[0:64, 128:256], b.onesb[:, 0:64], pT[pi][:, j * 128:(j + 1) * 128], j == 0, j == 2, ['onesb', 'pT%d' % pi], [kvk])
            di = it % 2
            b.ts('dve', dn[di][:], bv[0:64, 128:256], esink[:, h:h + 1], None, ALU.add, None, [kvk, 'esink'], ['dn%d' % di])
            b.recip(dn[di][:], dn[di][:], ['dn%d' % di], ['dn%d' % di])
            b.tt('dve', catH[:, h, qb * 128:(qb + 1) * 128], bv[0:64, 0:128], dn[di][:], ALU.mult, [kvk, 'dn%d' % di], ['catH'])
            it += 1
    cw = b.sb([128, 4, 31], F32, 'cw')
    cvec = b.sb([128, 3, 4], F32, 'cvec')
    S.dma(cw[:], d['cw'], (), ['cw'])
    S.dma(cvec[:], d['cvec'], (), ['cvec'])
    acc = b.sb([128, 4, 512], F32, 'acc')
    sq = [b.sb([128, 512], F32, 'sq%d' % i) for i in range(2)]
    mean = b.sb([128, 512], F32, 'mean')
    m2 = b.sb([128, 512], F32, 'm2')
    var = b.sb([128, 512], F32, 'var')
    yb = [b.sb([128, 512], F32, 'yb%d' % i) for i in range(2)]
    for tg in range(4):
        for cc in range(4):
            a0 = tg * 512
            b.ts('dve', acc[:, cc, :], aT[:, cc, a0:a0 + 512], cw[:, cc, 0:1], cvec[:, 0, cc:cc + 1], ALU.mult, ALU.add, ['aT', 'cw', 'cvec'], ['acc%d' % cc])
            for j in range(1, 31):
                b.stt(acc[:, cc, :], aT[:, cc, a0 + j:a0 + j + 512], cw[:, cc, j:j + 1], acc[:, cc, :], ALU.mult, ALU.add, ['aT', 'cw', 'acc%d' % cc], ['acc%d' % cc])
        for cc in range(4):
            si = cc % 2
            b.act(sq[si][:], acc[:, cc, :], AF.Square, ['acc%d' % cc], ['sq%d' % si])
            b.mm(B[6][:, :], b.onesf[:], acc[:, cc, :], cc == 0, cc == 3, ['onesf', 'acc%d' % cc], ['pb6'])
            b.mm(B[0][:, :], b.onesf[:], sq[si][:], cc == 0, cc == 3, ['onesf', 'sq%d' % si], ['pb0'])
        b.act(mean[:], B[6][:, :], AF.Copy, ['pb6'], ['mean'], scale=1.0 / 512)
        b.act(m2[:], B[6][:, :], AF.Square, ['pb6'], ['m2'], scale=1.0 / 512)
        b.stt(var[:], B[0][:, :], 1.0 / 512, m2[:], ALU.mult, ALU.subtract, ['pb0', 'm2'], ['var'])
        b.act(var[:], var[:], AF.Sqrt, ['var'], ['var'], bias=EPS)
        b.recip(var[:], var[:], ['var'], ['var'])
        for cc in range(4):
            yi = cc % 2
            b.tt('dve', yb[yi][:], acc[:, cc, :], mean[:], ALU.subtract, ['acc%d' % cc, 'mean'], ['yb%d' % yi])
            b.tt('pool', yb[yi][:], yb[yi][:], var[:], ALU.mult, ['yb%d' % yi, 'var'], ['yb%d' % yi])
            b.ts('pool', yb[yi][:], yb[yi][:], cvec[:, 1, cc:cc + 1], cvec[:, 2, cc:cc + 1], ALU.mult, ALU.add, ['yb%d' % yi, 'cvec'], ['yb%d' % yi])
            b.act(catA[:, cc, tg * 512:(tg + 1) * 512], yb[yi][:], AF.Silu, ['yb%d' % yi], ['catA'])
    b.reset(m2_)
    out_proj(b, d, c, catA, catH)
    b.reset(m0)


def emit_mixer_odd(b, d, c):
    S = b.S
    B = b.banks
    m0 = b.mark()
    NA = 2 * NT
    qT = b.sb([64, 8, NT], BF16, 'qT')
    kT = b.sb([64, 2, NA], BF16, 'kT')
    Vt = b.sb([128, 32, 128], BF16, 'Vt')
    uT = b.sb([128, 4, NT], BF16, 'uT')
    vn = b.sb([128, 16, 512], BF16, 'vn')
    m1 = b.mark()
    W = mixer_common_alloc(b, d, 512, None)
    win, wpm = W['win'], W['wpm']
    gq = b.sb([64, 4], F32, 'gq')
    S.dma(gq[:], d['gqk'], (), ['gq'])
    sqb = [b.sb([64, 512], F32, 'sqb%d' % i) for i in range(2)]
    rsb = [b.sb([64, 512], F32, 'rsb%d' % i) for i in range(2)]
    vg = [b.sb([128, 512], F32, 'vg%d' % i) for i in range(2)]
    st6 = b.sb([128, 6], F32, 'st6')
    mv = b.sb([128, 2], F32, 'mv')
    lnr = b.sb([128, 2], F32, 'lnr')
    gbc = b.sb([128, 2, 512], F32, 'gbc')
    S.dma(gbc[:, 0, :], d['sgu_g'].partition_broadcast(128), (), ['gbc'])
    S.dma(gbc[:, 1, :], d['sgu_b'].partition_broadcast(128), (), ['gbc'])
    cnt = [0]

    def qk_head(hT, hk, wcol, pcol, gcol_i, out, okey):
        i = cnt[0] % 2
        cnt[0] += 1
        pq, kq = b.nb([0, 1, 2, 3, 4, 5])
        proj_fm(b, pq, kq, 64, win, 'win', wcol, hT, hk, 0, 512)
        pp, kp = b.nb([0, 1, 2, 3, 4, 5])
        proj_fm(b, pp, kp, 64, wpm, 'wpm', pcol, hT, hk, 0, 512)
        b.act(sqb[i][:], pq[0:64, :], AF.Square, [kq], ['sqb%d' % i])
        ps_, kss = b.nb([0, 1, 2, 3, 4, 5])
        b.mm(ps_[0:64, :], b.onesf[0:64, 0:64], sqb[i][:], True, True, ['onesf', 'sqb%d' % i], [kss])
        b.act(rsb[i][:], ps_[0:64, :], AF.Sqrt, [kss], ['rsb%d' % i], scale=1.0 / 64, bias=EPS)
        b.recip(rsb[i][:], rsb[i][:], ['rsb%d' % i], ['rsb%d' % i])
        t1 = W['t1'][i]
        t2 = W['t2'][i]
        b.stt(t1[0:64, :], pq[0:64, :], gq[:, gcol_i:gcol_i + 1], W['rope'][:, 0, :], ALU.mult, ALU.mult, [kq, 'gq', 'rope'], ['t1_%d' % i])
        b.stt(t2[0:64, :], pp[0:64, :], gq[:, gcol_i + 1:gcol_i + 2], W['rope'][:, 1, :], ALU.mult, ALU.mult, [kp, 'gq', 'rope'], ['t2_%d' % i])
        b.tt('pool', t1[0:64, :], t1[0:64, :], t2[0:64, :], ALU.add, ['t1_%d' % i, 't2_%d' % i], ['t1_%d' % i])
        b.tt('pool', out, t1[0:64, :], rsb[i][:], ALU.mult, ['t1_%d' % i, 'rsb%d' % i], [okey])

    for g in range(8):
        own = g < 4
        hi = g % 2
        hT = W['hT'][hi]
        hk = 'hT%d' % hi
        src = c['x_own'][g * 512:(g + 1) * 512, :] if own else c['x_ctx'][(g - 4) * 512:(g - 3) * 512, :]
        S.dma(W['rope'][:], c['rope'][:, :, g * 512:(g + 1) * 512], (), ['rope'])
        norm_tiles_bf16(b, row_tiles(src, 4), W['gcol'][:], hT, hk, W)
        if own:
            for h in range(8):
                qk_head(hT, hk, h * 64, h * 64, 0, qT[:, h, g * 512:(g + 1) * 512], 'qT')
        for j in range(2):
            qk_head(hT, hk, 512 + j * 64, 512 + j * 64, 2, kT[:, j, g * 512:(g + 1) * 512], 'kT')
        for tt in range(4):
            for kc in range(8):
                b.mm(B[6][:, 0:128], hT[:, kc, tt * 128:(tt + 1) * 128], win[:, kc, 640:768], kc == 0, kc == 7, ['win', hk], ['pb6'])
            b.cp('act', Vt[:, g * 4 + tt, :], B[6][:, 0:128], ['pb6'], ['Vt'])
        if own:
            for cc in range(4):
                pu, ku = b.nb([0, 1, 2, 3, 4, 5])
                proj_fm(b, pu, ku, 128, win, 'win', 768 + cc * 128, hT, hk, 0, 512)
                b.act(uT[:, cc, g * 512:(g + 1) * 512], pu[:, :], AF.Gelu_apprx_tanh, [ku], ['uT'])
            for tt in range(4):
                T = g * 4 + tt
                vi = tt % 2
                pv, kv_ = b.nb([0, 1, 2, 3, 4, 5])
                for kc in range(8):
                    b.mm(pv[:, :], hT[:, kc, tt * 128:(tt + 1) * 128], win[:, kc, 1280:1792], kc == 0, kc == 7, ['win', hk], [kv_])
                b.act(vg[vi][:], pv[:, :], AF.Gelu_apprx_tanh, [kv_], ['vg%d' % vi])
                S.op('dve', lambda e, vi=vi: e.bn_stats(out=st6[:], in_=vg[vi][:]), ['vg%d' % vi], ['st6'])
                S.op('dve', lambda e: e.bn_aggr(out=mv[:], in_=st6[:]), ['st6'], ['mv'])
                b.act(lnr[:, 0:1], mv[:, 1:2], AF.Sqrt, ['mv'], ['lnr'], bias=EPS)
                b.recip(lnr[:, 1:2], lnr[:, 0:1], ['lnr'], ['lnr'])
                b.ts('dve', vg[vi][:], vg[vi][:], mv[:, 0:1], lnr[:, 1:2], ALU.subtract, ALU.mult, ['vg%d' % vi, 'mv', 'lnr'], ['vg%d' % vi])
                b.tt('pool', vg[vi][:], vg[vi][:], gbc[:, 0, :], ALU.mult, ['vg%d' % vi, 'gbc'], ['vg%d' % vi])
                b.tt('pool', vn[:, T, :], vg[vi][:], gbc[:, 1, :], ALU.add, ['vg%d' % vi, 'gbc'], ['vn'])
    b.reset(m1)
    catA = b.sb([128, 4, NT], BF16, 'catA')
    catH = b.sb([64, 8, NT], BF16, 'catH')
    m2_ = b.mark()
    pT = [b.sb([128, 512], BF16, 'pT%d' % i) for i in range(3)]
    rden = [b.sb([64, 512], F32, 'rden%d' % i) for i in range(2)]
    it = 0
    n = 0
    for h in range(8):
        kv = h // 4
        for qg in range(4):
            bv = 3 + (it % 2) * 2
            bd = 4 + (it % 2) * 2
            for kb in range(32):
                bs_, ks = b.nb([0, 1, 2])
                b.mm(bs_[:, :], kT[:, kv, kb * 128:(kb + 1) * 128], qT[:, h, qg * 512:(qg + 1) * 512], True, True, ['kT', 'qT'], [ks])
                pi = n % 3
                n += 1
                b.act(pT[pi][:], bs_[:, :], AF.Exp, [ks], ['pT%d' % pi], scale=0.125)
                b.mm(B[bv][0:64, :], Vt[:, kb, kv * 64:(kv + 1) * 64], pT[pi][:], kb == 0, kb == 31, ['Vt', 'pT%d' % pi], ['pb%d' % bv])
                b.mm(B[bd][0:64, :], b.onesb[:, 0:64], pT[pi][:], kb == 0, kb == 31, ['onesb', 'pT%d' % pi], ['pb%d' % bd])
            ri = it % 2
            b.recip(rden[ri][:], B[bd][0:64, :], ['pb%d' % bd], ['rden%d' % ri])
            b.tt('dve', catH[:, h, qg * 512:(qg + 1) * 512], B[bv][0:64, :], rden[ri][:], ALU.mult, ['pb%d' % bv, 'rden%d' % ri], ['catH'])
            it += 1
    wsT = b.sb([128, 4, 128], BF16, 'wsT')
    wsst = b.sb([128, 4, 128], F32, 'wsst')
    S.dma(wsst[:], d['wsT'], (), ['wsst'])
    b.cp('pool', wsT[:], wsst[:], ['wsst'], ['wsT'])
    bsb = b.sb([128, 512], F32, 'bsb')
    S.dma(bsb[:], d['sgu_bs'].partition_broadcast(128), (), ['bsb'])
    tmx = [b.sb([128, 512], F32, 'tmx%d' % i) for i in range(2)]
    for T in range(16):
        bm, km = b.nb([0, 1, 2])
        for g in range(4):
            b.mm(bm[:, g * 128:(g + 1) * 128], vn[:, T, g * 128:(g + 1) * 128], wsT[:, g, :], True, True, ['vn', 'wsT'], [km])
        ti = T % 2
        b.tt('dve', tmx[ti][:], bm[:, :], bsb[:], ALU.add, [km, 'bsb'], ['tmx%d' % ti])
        b.tt('pool', catA[:, :, T * 128:(T + 1) * 128], tmx[ti][:].rearrange("p (g t) -> p g t", g=4), uT[:, :, T * 128:(T + 1) * 128],
             ALU.mult, ['tmx%d' % ti, 'uT'], ['catA'])
    b.reset(m2_)
    out_proj(b, d, c, catA, catH)
    b.reset(m0)


LAYER_SHAPES = {
    'gmix': [128, 8], 'w_in': [128, 8, 1792], 'w_perm': [128, 8, 640], 'w_outA': [128, 4, 1024], 'w_outH': [64, 8, 1024],
    'gffn': [128, 8], 'wq': [128, 8, 1024], 'skbd': [128, 8, 256], 'uth': [128, 128, 1024], 'vbh': [128, 128, 1024],
}
EVEN_SHAPES = {'sink': [8], 'cw': [128, 4, 31], 'cvec': [128, 3, 4]}
ODD_SHAPES = {'gqk': [64, 4], 'sgu_g': [512], 'sgu_b': [512], 'wsT': [128, 4, 128], 'sgu_bs': [512]}


def build_fused_program(nlayers=4):
    nc = bass.Bass("TRN2", target_bir_lowering=False)

    def inp(name, shape, dt=F32):
        return nc.dram_tensor(name, list(shape), dt, kind="ExternalInput").ap()
    x_in = inp('x', [SEQ, D])
    rope_e = inp('rope_e', [2, 64, 2, NT + 256])
    rope_o = inp('rope_o', [2, 64, 2, 2 * NT])
    mask3 = inp('mask3', [2, 128, 3, 384], BF16)
    zrow = inp('zrow', [128, D])
    gfin = inp('gfin', [D])
    out = nc.dram_tensor('out', [SEQ, D], F32, kind="ExternalOutput").ap()
    xa = nc.dram_tensor('xa', [SEQ, D], F32).ap()
    xb = nc.dram_tensor('xb', [SEQ, D], F32).ap()
    x1 = nc.dram_tensor('x1', [SEQ, D], F32).ap()
    uts = nc.dram_tensor('uts', [128 // JG, 128, JG, 1024], BF16).ap()
    vbs = nc.dram_tensor('vbs', [128 // JG, 128, JG, 1024], BF16).ap()
    with ExitStack() as st:
        b = Bld(nc, st)
        b.consts()
        cur = x_in
        for L in range(nlayers):
            even = L % 2 == 0
            last = L == nlayers - 1
            d = {'uts': uts, 'vbs': vbs, 'gfin': gfin}
            shapes = dict(LAYER_SHAPES)
            shapes.update(EVEN_SHAPES if even else ODD_SHAPES)
            for k, shp in shapes.items():
                d[k] = inp('L%d_%s' % (L, k), shp)
            nxt = out if last else (xa if L % 2 == 0 else xb)
            emit_peer_prep(b, d)
            for half in range(2):
                o0 = half * NT
                o1 = (1 - half) * NT
                c = {'x_own': cur[o0:o0 + NT, :], 'x1': x1[o0:o0 + NT, :]}
                if even:
                    c['ctxL'] = cur[o0 - 128:o0, :] if half == 1 else zrow
                    c['ctxR'] = cur[o0 + NT:o0 + NT + 128, :] if half == 0 else zrow
                    c['rope'] = rope_e[half]
                    c['mask3'] = mask3[half]
                    emit_mixer_even(b, d, c)
                else:
                    c['x_ctx'] = cur[o1:o1 + NT, :]
                    c['rope'] = rope_o[half]
                    emit_mixer_odd(b, d, c)
            emit_peer(b, d, last, x1, nxt, SEQ)
            cur = nxt
            print('layer', L, 'instructions', b.S.n_instr, {e: b.S.cnt[e] for e in ENGS}, 'dmas', b.S.ndma, flush=True)
            if not last:
                b.S.new_epoch(st)
        b.S.emit_all()
    return nc


PERM_1D = np.concatenate([np.arange(32, 64), np.arange(0, 32)])
PERM_AX = np.concatenate([np.arange(16, 32), np.arange(0, 16), np.arange(48, 64), np.arange(32, 48)])
_f = np.float32


def _fm(v, c):
    return np.ascontiguousarray(v.reshape(c, 128).T)


def host_layer_inputs(layer, inp):
    i = layer // 2
    even = layer % 2 == 0
    o = {}
    o['gmix'] = _fm(inp['mix_norm_g'][layer], 8)
    o['gffn'] = _fm(inp['ffn_norm_g'][layer], 8)
    Win = inp['even_w_in'][i] if even else inp['odd_w_in'][i]
    Wout = inp['even_w_out'][i] if even else inp['odd_w_out'][i]
    o['w_in'] = np.ascontiguousarray(Win.reshape(8, 128, 1792).transpose(1, 0, 2))
    qk0 = 1024 if even else 0
    perm = PERM_1D if even else PERM_AX
    qk = Win[:, qk0:qk0 + 640].reshape(1024, 10, 64)[:, :, perm].reshape(1024, 640)
    o['w_perm'] = np.ascontiguousarray(qk.reshape(8, 128, 640).transpose(1, 0, 2))
    A0, H0 = (0, 512) if even else (512, 0)
    o['w_outA'] = np.ascontiguousarray(Wout[A0:A0 + 512].reshape(4, 128, 1024).transpose(1, 0, 2))
    o['w_outH'] = np.ascontiguousarray(Wout[H0:H0 + 512].reshape(8, 64, 1024).transpose(1, 0, 2))
    if even:
        o['sink'] = np.ascontiguousarray(inp['sink_logits'][i])
        o['cw'] = np.ascontiguousarray(inp['conv_w'][i][:, 0, :].reshape(31, 4, 128).transpose(2, 1, 0))
        o['cvec'] = np.ascontiguousarray(np.stack([_fm(inp['conv_b'][i], 4), _fm(inp['conv_ln_g'][i], 4), _fm(inp['conv_ln_b'][i], 4)], axis=1))
    else:
        qg, kg = inp['q_norm_g'][i], inp['k_norm_g'][i]
        o['gqk'] = np.ascontiguousarray(np.stack([qg, qg[PERM_AX], kg, kg[PERM_AX]], axis=1))
        o['sgu_g'] = np.ascontiguousarray(inp['sgu_ln_g'][i])
        o['sgu_b'] = np.ascontiguousarray(inp['sgu_ln_b'][i])
        o['wsT'] = np.ascontiguousarray(inp['sgu_w'][i].transpose(2, 0, 1))
        o['sgu_bs'] = np.ascontiguousarray(inp['sgu_b'][i].reshape(512))
    o['wq'] = np.ascontiguousarray(inp['peer_wq'][layer].reshape(8, 128, 1024).transpose(1, 0, 2))
    sk = inp['peer_subkeys'][layer]
    skbd = np.zeros((128, 8, 256), _f)
    skbd[0:64, :, 0:128] = sk[:, 0].transpose(2, 0, 1)
    skbd[64:128, :, 128:256] = sk[:, 1].transpose(2, 0, 1)
    o['skbd'] = skbd
    U = inp['peer_u'][layer]
    V = inp['peer_v'][layer]
    o['uth'] = np.ascontiguousarray(U.reshape(128, 128, 8, 128).transpose(1, 3, 2, 0)).reshape(128, 128, 1024)
    o['vbh'] = np.ascontiguousarray(V.reshape(128, 128, 1024).transpose(1, 0, 2))
    return o


def rope_tables_even(pos):
    inv = (_f(10000.0) ** (-np.arange(0, 64, 2, dtype=_f) / _f(64))).astype(_f)
    ang = pos.astype(_f)[None, :] * inv[:, None]
    ang = np.concatenate([ang, ang], axis=0)
    sgn = np.concatenate([-np.ones(32, _f), np.ones(32, _f)])[:, None]
    return np.stack([np.cos(ang), np.sin(ang) * sgn], axis=1).astype(_f)


def rope_tables_axial(pos):
    inv = (_f(10000.0) ** (-np.arange(0, 32, 2, dtype=_f) / _f(32))).astype(_f)
    rows = (pos // 64).astype(_f)
    cols = (pos % 64).astype(_f)
    ar = rows[None, :] * inv[:, None]
    ac = cols[None, :] * inv[:, None]
    ang = np.concatenate([ar, ar, ac, ac], axis=0)
    sgn = np.concatenate([-np.ones(16, _f), np.ones(16, _f), -np.ones(16, _f), np.ones(16, _f)])[:, None]
    return np.stack([np.cos(ang), np.sin(ang) * sgn], axis=1).astype(_f)


def window_masks(half):
    k = np.arange(128)[:, None]
    q = np.arange(128)[None, :]
    left = np.where(k >= q, 0.0, NEGM)
    mid = np.zeros((128, 128))
    right = np.where(k <= q, 0.0, NEGM)
    full = np.full((128, 128), NEGM)
    m_mid = np.concatenate([left, mid, right], axis=1)
    m_first = np.concatenate([full if half == 0 else left, mid, right], axis=1)
    m_last = np.concatenate([left, mid, full if half == 1 else right], axis=1)
    return np.stack([m_mid, m_first, m_last], axis=1).astype(ml_dtypes.bfloat16)


def host_const_inputs():
    o = {}
    re, ro, mk = [], [], []
    for half in range(2):
        o0 = half * NT
        o1 = (1 - half) * NT
        pos = np.concatenate([o0 + np.arange(NT), o0 - 128 + np.arange(128), o0 + NT + np.arange(128)])
        re.append(rope_tables_even(pos))
        pos = np.concatenate([o0 + np.arange(NT), o1 + np.arange(NT)])
        ro.append(rope_tables_axial(pos))
        mk.append(window_masks(half))
    o['rope_e'] = np.stack(re)
    o['rope_o'] = np.stack(ro)
    o['mask3'] = np.stack(mk)
    o['zrow'] = np.zeros((128, D), _f)
    return o


_PROG = {}


def kernel(**inputs):
    inp = {k: np.asarray(v) for k, v in inputs.items()}
    x = np.ascontiguousarray(inp['x'], dtype=_f)
    if 'nc' not in _PROG:
        _PROG['nc'] = build_fused_program()
    nc = _PROG['nc']
    shared = host_const_inputs()
    shared['gfin'] = np.ascontiguousarray(inp['final_norm_g'])
    for L in range(4):
        for k, v in host_layer_inputs(L, inp).items():
            shared['L%d_%s' % (L, k)] = v
    in_maps = []
    for c in range(8):
        m = dict(shared)
        m['x'] = np.ascontiguousarray(x[c % 4])
        in_maps.append(m)
    res = run_bass_kernel_spmd(nc, in_maps, core_ids=list(range(8)))
    return np.stack([res.results[c]['out'] for c in range(4)], axis=0)
```

```python
import numpy as np
from contextlib import ExitStack
import ml_dtypes
import concourse.bass as bass
import concourse.mybir as mybir
from concourse.bass_utils import run_bass_kernel_spmd

F32 = mybir.dt.float32
BF16 = mybir.dt.bfloat16
U32 = mybir.dt.uint32
ALU = mybir.AluOpType
AF = mybir.ActivationFunctionType
AX = mybir.AxisListType

D = 1024
NT = 2048
SEQ = 4096
EPS = 1e-6
ENGS = ['pe', 'act', 'dve', 'pool', 'sp']
NRING = 12
SB_BASE = 16512
SB_TOP = 229344
NEGM = -30000.0


class Sched:
    def __init__(self, nc, stack, self_sync=True):
        self.nc = nc
        self.sem = {e: stack.enter_context(nc.semaphore('c_' + e)) for e in ENGS}
        self.ring = [stack.enter_context(nc.semaphore('d_%d' % i)) for i in range(NRING)]
        self.cnt = {e: 0 for e in ENGS}
        self.ndma = 0
        self.ops = {e: [] for e in ENGS}
        self.seen = {e: {} for e in ENGS}
        self.lastw = {}
        self.readers = {}
        self.self_sync = self_sync
        self.n_instr = 0
        self.bgsem = stack.enter_context(nc.semaphore('bg'))
        self.bgcnt = 0

    def new_epoch(self, stack):
        self.barrier()
        nc = self.nc
        self.epoch = getattr(self, 'epoch', 0) + 1
        self.sem = {e: stack.enter_context(nc.semaphore('c%d_%s' % (self.epoch, e))) for e in ENGS}
        self.ring = [stack.enter_context(nc.semaphore('d%d_%d' % (self.epoch, i))) for i in range(NRING)]
        self.cnt = {e: 0 for e in ENGS}
        self.ndma = 0
        self.seen = {e: {} for e in ENGS}

    def _tok_wait(self, tok):
        if tok[0] == 'c':
            return (('c', tok[1]), self.sem[tok[1]], tok[2])
        n = tok[1]
        return (('d', n % NRING), self.ring[n % NRING], 16 * (n // NRING + 1))

    def _collect(self, eng, reads, writes):
        need = {}
        toks = []
        for r in reads:
            t = self.lastw.get(r)
            if t is not None:
                toks.append(t)
        for w in writes:
            t = self.lastw.get(w)
            if t is not None:
                toks.append(t)
            toks.extend(self.readers.get(w, ()))
        for t in toks:
            if t[0] == 'c' and t[1] == eng:
                if eng == 'pe' or not self.self_sync:
                    continue
            key, sem, val = self._tok_wait(t)
            if self.seen[eng].get(key, 0) >= val:
                continue
            if key not in need or need[key][1] < val:
                need[key] = (sem, val)
        for key, (sem, val) in need.items():
            self.seen[eng][key] = val
        return list(need.values())

    def _commit(self, tok, reads, writes):
        for r in reads:
            self.readers.setdefault(r, []).append(tok)
        for w in writes:
            self.lastw[w] = tok
            self.readers[w] = []

    def op(self, eng, fn, reads=(), writes=()):
        waits = self._collect(eng, reads, writes)
        self.cnt[eng] += 1
        idx = self.cnt[eng]
        sem = self.sem[eng]

        def emit(e):
            for (s, v) in waits:
                e.wait_ge(s, v)
            fn(e).then_inc(sem, 1)
        self.ops[eng].append(emit)
        self.n_instr += 1 + len(waits)
        tok = ('c', eng, idx)
        self._commit(tok, reads, writes)
        return tok

    def dma(self, out, in_, reads=(), writes=(), q='sp', **kw):
        return self.dmalike(q, lambda e: e.dma_start(out=out, in_=in_, **kw), reads, writes)

    def dmalike(self, q, fn, reads=(), writes=()):
        n = self.ndma
        self.ndma += 1
        waits = self._collect(q, reads, writes)
        if n >= NRING:
            key, sem, val = self._tok_wait(('d', n - NRING))
            if self.seen[q].get(key, 0) < val:
                self.seen[q][key] = val
                waits.append((sem, val))
        rs = self.ring[n % NRING]

        def emit(e):
            for (s, v) in waits:
                e.wait_ge(s, v)
            fn(e).then_inc(rs, 16)
        self.ops[q].append(emit)
        self.n_instr += 1 + len(waits)
        tok = ('d', n)
        self._commit(tok, reads, writes)
        return tok

    def bg_dma(self, q, fn):
        self.bgcnt += 1
        sem = self.bgsem
        self.ops[q].append(lambda e: fn(e).then_inc(sem, 16))
        self.n_instr += 1

    def bg_wait(self, engs):
        v = 16 * self.bgcnt
        sem = self.bgsem
        for q in engs:
            self.ops[q].append(lambda e: e.wait_ge(sem, v))

    def barrier(self):
        for e in ENGS:
            waits = []
            for o in ENGS:
                if o == e or self.cnt[o] == 0:
                    continue
                key = ('c', o)
                if self.seen[e].get(key, 0) < self.cnt[o]:
                    self.seen[e][key] = self.cnt[o]
                    waits.append((self.sem[o], self.cnt[o]))
            for n in range(max(0, self.ndma - NRING), self.ndma):
                key, sem, val = self._tok_wait(('d', n))
                if self.seen[e].get(key, 0) < val:
                    self.seen[e][key] = val
                    waits.append((sem, val))
            if waits:
                def emit(en, waits=waits):
                    for (s, v) in waits:
                        en.wait_ge(s, v)
                self.ops[e].append(emit)
                self.n_instr += len(waits)
        self.lastw.clear()
        self.readers.clear()

    def emit_all(self):
        self.barrier()
        nc = self.nc
        ops = self.ops
        with nc.Block() as block:
            @block.tensor
            def _(e):
                for f in ops['pe']:
                    f(e)

            @block.scalar
            def _(e):
                for f in ops['act']:
                    f(e)

            @block.vector
            def _(e):
                for f in ops['dve']:
                    f(e)

            @block.gpsimd
            def _(e):
                for f in ops['pool']:
                    f(e)

            @block.sync
            def _(e):
                for f in ops['sp']:
                    f(e)


_DT_SIZE = {F32: 4, BF16: 2, U32: 4}


class Bld:
    def __init__(self, nc, st):
        self.nc = nc
        self.S = Sched(nc, st)
        self.banks = [st.enter_context(nc.psum_tensor("pb%d" % i, [128, 512], F32)) for i in range(7)]
        self.bankb = st.enter_context(nc.psum_tensor("pb7", [128, 1024], BF16))
        self.p = SB_BASE
        self.nalloc = 0
        self.rot = 0

    def sb(self, shape, dt, name):
        size = _DT_SIZE[dt]
        for s in shape[1:]:
            size *= s
        off = (self.p + 63) // 64 * 64
        assert off + size <= SB_TOP, ("SBUF overflow", name, off + size - SB_TOP)
        self.p = off + size
        self.nalloc += 1
        return self.nc.alloc_sbuf_tensor_at("%s_%d" % (name, self.nalloc), list(shape), dt, offset=off)

    def mark(self):
        return self.p

    def reset(self, m):
        self.S.barrier()
        self.p = m

    def nb(self, lst):
        i = lst[self.rot % len(lst)]
        self.rot += 1
        return self.banks[i], 'pb%d' % i

    def mm(self, out, lhsT, rhs, start, stop, r, w):
        self.S.op('pe', lambda e: e.matmul(out, lhsT=lhsT, rhs=rhs, start=start, stop=stop), r, w)

    def tr(self, out, in_, ident, r, w):
        self.S.op('pe', lambda e: e.transpose(out=out, in_=in_, identity=ident), r, w)

    def act(self, out, in_, func, r, w, **kw):
        self.S.op('act', lambda e: e.activation(out=out, in_=in_, func=func, **kw), r, w)

    def tt(self, eng, out, in0, in1, op, r, w):
        self.S.op(eng, lambda e: e.tensor_tensor(out=out, in0=in0, in1=in1, op=op), r, w)

    def ts(self, eng, out, in0, s1, s2, op0, op1, r, w):
        if op1 is None:
            self.S.op(eng, lambda e: e.tensor_scalar(out=out, in0=in0, scalar1=s1, scalar2=None, op0=op0), r, w)
        else:
            self.S.op(eng, lambda e: e.tensor_scalar(out=out, in0=in0, scalar1=s1, scalar2=s2, op0=op0, op1=op1), r, w)

    def stt(self, out, in0, scalar, in1, op0, op1, r, w):
        self.S.op('dve', lambda e: e.scalar_tensor_tensor(out=out, in0=in0, scalar=scalar, in1=in1, op0=op0, op1=op1), r, w)

    def cp(self, eng, out, in_, r, w):
        if eng == 'act':
            self.S.op('act', lambda e: e.copy(out=out, in_=in_), r, w)
        else:
            self.S.op(eng, lambda e: e.tensor_copy(out=out, in_=in_), r, w)

    def recip(self, out, in_, r, w):
        self.S.op('dve', lambda e: e.reciprocal(out=out, in_=in_), r, w)

    def consts(self):
        S = self.S
        self.identf = self.sb([128, 128], F32, 'identf')
        self.identb = self.sb([128, 128], BF16, 'identb')
        self.iota = self.sb([128, 128], F32, 'iota')
        self.onesf = self.sb([128, 128], F32, 'onesf')
        self.onesb = self.sb([128, 128], BF16, 'onesb')
        self.iotab = self.sb([128, 128], BF16, 'iotab')
        S.op('pool', lambda e: e.iota(self.iota[:], pattern=[[1, 128]], base=0, channel_multiplier=0,
                                      allow_small_or_imprecise_dtypes=True), (), ['iota'])
        S.op('pool', lambda e: e.iota(self.identf[:], pattern=[[1, 128]], base=0, channel_multiplier=-1,
                                      allow_small_or_imprecise_dtypes=True), (), ['identf'])
        self.ts('dve', self.identf[:], self.identf[:], 0.0, None, ALU.is_equal, None, ['identf'], ['identf'])
        self.cp('dve', self.identb[:], self.identf[:], ['identf'], ['identb'])
        self.cp('dve', self.iotab[:], self.iota[:], ['iota'], ['iotab'])
        S.op('pool', lambda e: e.memset(self.onesf[:], 1.0), (), ['onesf'])
        S.op('pool', lambda e: e.memset(self.onesb[:], 1.0), (), ['onesb'])


def rms_rstd(b, xt, xkey, ssq, sd, rstd, col, junk, junkkey, tag):
    b.act(junk, xt, AF.Square, [xkey], [junkkey, tag + 'ssq'], accum_out=ssq[:, col:col + 1])
    b.act(sd[:, col:col + 1], ssq[:, col:col + 1], AF.Sqrt, [tag + 'ssq'], [tag + 'sd'], scale=1.0 / D, bias=EPS)
    b.recip(rstd[:, col:col + 1], sd[:, col:col + 1], [tag + 'sd'], [tag + 'rstd'])


def row_tiles(ap, n):
    return [ap[t * 128:(t + 1) * 128, :] for t in range(n)]


def norm_tiles_bf16(b, src_tiles, gcol, hT, hkey, W):
    S = b.S
    for tt in range(len(src_tiles)):
        sl = W['xslot']
        W['xslot'] ^= 1
        xt = W['xt'][sl]
        xn = W['xn'][sl]
        S.dma(xt[:], src_tiles[tt], (), ['xt%d' % sl])
        rms_rstd(b, xt[:], 'xt%d' % sl, W['ssq'], W['sd'], W['rstd'], sl, xn[:], 'xn%d' % sl, 'n%d' % sl)
        b.act(xn[:], xt[:], AF.Copy, ['xt%d' % sl, 'n%drstd' % sl], ['xn%d' % sl], scale=W['rstd'][:, sl:sl + 1])
        for c in range(8):
            b.tr(b.bankb[:, c * 128:(c + 1) * 128], xn[:, c * 128:(c + 1) * 128], b.identb[:], ['xn%d' % sl, 'identb'], ['pb7'])
        b.tt('dve', hT[:, :, tt * 128:(tt + 1) * 128], b.bankb[:].rearrange("p (c t) -> p c t", c=8),
             gcol.unsqueeze(2).to_broadcast([128, 8, 128]), ALU.mult, ['pb7', 'gcol'], [hkey])


def proj_fm(b, bank, bkey, M, wT, wkey, col0, hT, hkey, t0, ntok):
    for kc in range(8):
        b.mm(bank[0:M, 0:ntok], wT[:, kc, col0:col0 + M], hT[:, kc, t0:t0 + ntok], kc == 0, kc == 7, [wkey, hkey], [bkey])


def load_cast(b, dst, dkey, src, shape, stage, skey, eng='pool'):
    b.S.dma(stage, src, (), [skey])
    b.cp(eng, dst, stage, [skey], [dkey])


TB = 256
JG = 2


def emit_peer_prep(b, d):
    S = b.S
    m = b.mark()
    st = [b.sb([128, JG, 1024], F32, 'pst%d' % i) for i in range(4)]
    ob = [b.sb([128, JG, 1024], BF16, 'pob%d' % i) for i in range(4)]
    n = 0
    for g in range(128 // JG):
        for (src, dst, eng) in ((d['uth'], d['uts'], 'act'), (d['vbh'], d['vbs'], 'pool')):
            sl = n % 4
            n += 1
            S.dma(st[sl][:], src[g * JG:(g + 1) * JG].rearrange("j p f -> p j f"), (), ['pst%d' % sl])
            b.cp(eng, ob[sl][:], st[sl][:], ['pst%d' % sl], ['pob%d' % sl])
            S.dma(dst[g], ob[sl][:], ['pob%d' % sl], ())
    b.reset(m)


def prep_dma_list(b, d):
    lst = []
    for g in range(128 // JG):
        for (src, dst) in ((d['uth'], d['uts']), (d['vbh'], d['vbs'])):
            lst.append(lambda e, src=src, dst=dst, g=g: e.dma_start(out=dst[g], in_=src[g * JG:(g + 1) * JG].rearrange("j p f -> p j f")))
    return lst


def emit_peer(b, d, last, x1, xout, ntok):
    S = b.S
    m = b.mark()
    B = b.banks
    NTT = TB // 128
    HX = B[2]
    HY = b.bankb[:].bitcast(F32)
    wq = b.sb([128, 8, 1024], F32, 'wq')
    skbd = b.sb([128, 8, 256], F32, 'skbd')
    gffn = b.sb([128, 8], F32, 'gffn')
    S.dma(wq[:], d['wq'], (), ['wq'])
    S.dma(skbd[:], d['skbd'], (), ['skbd'])
    S.dma(gffn[:], d['gffn'], (), ['gffn'])
    if last:
        gfin = b.sb([128, 1024], F32, 'gfin')
        S.dma(gfin[:], d['gfin'].partition_broadcast(128), (), ['gfin'])
    xt = [[b.sb([128, 1024], F32, 'pxt%d_%d' % (s_, i)) for i in range(NTT)] for s_ in range(2)]
    xn = b.sb([128, 1024], F32, 'pxn')
    ssq = b.sb([128, 2], F32, 'pssq')
    sd = b.sb([128, 2], F32, 'psd')
    rstd = b.sb([128, 2], F32, 'prstd')
    h2T = b.sb([128, 8, TB], F32, 'h2T')
    h2Tb = [b.sb([128, 8, TB], BF16, 'h2Tb%d' % i) for i in range(2)]
    qh = [b.sb([128, TB], F32, 'qh%d' % i) for i in range(2)]
    ssb = [b.sb([128, 8, 2, 128], F32, 'ssb%d' % i) for i in range(NTT)]
    v16 = b.sb([128, 8, 2, 16], F32, 'v16')
    i16 = b.sb([128, 8, 2, 16], U32, 'i16')
    i16f = b.sb([128, 8, 2, 16], F32, 'i16f')
    cand = b.sb([128, 8, 16, 16], F32, 'cand')
    fv = b.sb([128, 8, 16], F32, 'fv')
    fi = b.sb([128, 8, 16], U32, 'fi')
    fa = b.sb([128, 8, 16], U32, 'fa')
    fb = b.sb([128, 8, 16], U32, 'fb')
    faf = b.sb([128, 8, 16], F32, 'faf')
    fbf = b.sb([128, 8, 16], F32, 'fbf')
    gate = b.sb([128, 8, 16], F32, 'gate')
    gsum = b.sb([128, 8], F32, 'gsum')
    i12 = b.sb([128, 2, 8, 16], F32, 'i12')
    T3 = b.sb([128, 3, TB], F32, 'T3')
    Pt = [b.sb([128, 128], BF16, 'Pt%d' % i) for i in range(4)]
    Qt = [b.sb([128, 128], BF16, 'Qt%d' % i) for i in range(4)]
    Gs = b.sb([128, TB, 128], BF16, 'Gs')
    ut = [b.sb([128, 8, 128], BF16, 'ut%d' % i) for i in range(3)]
    vb = [b.sb([128, 1024], BF16, 'vb%d' % i) for i in range(3)]
    gl = [b.sb([128, TB], F32, 'gl%d' % i) for i in range(2)]
    wT = [b.sb([128, TB], BF16, 'wT%d' % i) for i in range(2)]
    x2 = b.sb([128, 1024], F32, 'x2')
    iota16 = b.iota[:, 0:16]
    print('PEER sbuf end', b.p, SB_TOP)
    nblk = ntok // TB

    def h_phase(blk):
        L = []
        st_ = blk % 2
        for tt in range(NTT):
            T = blk * NTT + tt
            xk = 'pxt%d_%d' % (st_, tt)
            x_ = xt[st_][tt]
            L.append(lambda x_=x_, T=T, xk=xk: S.dma(x_[:], x1[T * 128:(T + 1) * 128, :], (), [xk]))
            L.append(lambda x_=x_, xk=xk: b.act(xn[:], x_[:], AF.Square, [xk], ['pxn', 'pnssq'], accum_out=ssq[:, 0:1]))
            L.append(lambda: b.act(sd[:, 0:1], ssq[:, 0:1], AF.Sqrt, ['pnssq'], ['pnsd'], scale=1.0 / D, bias=EPS))
            L.append(lambda: b.recip(rstd[:, 0:1], sd[:, 0:1], ['pnsd'], ['pnrstd']))
            L.append(lambda x_=x_, xk=xk: b.act(xn[:], x_[:], AF.Copy, [xk, 'pnrstd'], ['pxn'], scale=rstd[:, 0:1]))
            for hb in range(2):
                bank, bk = (HX, 'pb2') if hb == 0 else (HY, 'pb7')
                for c4 in range(4):
                    c = hb * 4 + c4
                    L.append(lambda bank=bank, bk=bk, c=c, c4=c4: b.tr(bank[:, c4 * 128:(c4 + 1) * 128], xn[:, c * 128:(c + 1) * 128], b.identf[:], ['pxn', 'identf'], [bk]))
                L.append(lambda bank=bank, bk=bk, hb=hb, tt=tt: b.tt('dve', h2T[:, hb * 4:(hb + 1) * 4, tt * 128:(tt + 1) * 128], bank[:, :].rearrange("p (c t) -> p c t", c=4),
                                                              gffn[:, hb * 4:(hb + 1) * 4].unsqueeze(2).to_broadcast([128, 4, 128]), ALU.mult, [bk, 'gffn'], ['h2T']))
        hb_ = h2Tb[st_]
        hbk = 'h2Tb%d' % st_
        L.append(lambda: b.cp('pool', hb_[:], h2T[:], ['h2T'], [hbk]))
        for h in range(8):
            for kc in range(8):
                L.append(lambda h=h, kc=kc: b.mm(HX[:, 0:TB], wq[:, kc, h * 128:(h + 1) * 128], h2T[:, kc, :], kc == 0, kc == 7, ['wq', 'h2T'], ['pb2']))
            q = qh[h % 2]
            qk = 'qh%d' % (h % 2)
            L.append(lambda q=q, qk=qk: b.cp('act', q[:], HX[:, 0:TB], ['pb2'], [qk]))
            for tt in range(NTT):
                L.append(lambda q=q, qk=qk, tt=tt, h=h: b.mm(HY[:, tt * 256:(tt + 1) * 256], q[:, tt * 128:(tt + 1) * 128], skbd[:, h, :], True, True, [qk, 'skbd'], ['pb7']))
            for tt in range(NTT):
                L.append(lambda tt=tt, h=h: b.cp('act', ssb[tt][:, h, :, :], HY[:, tt * 256:(tt + 1) * 256].rearrange("p (s k) -> p s k", s=2), ['pb7'],
                                                 ['ss%d_%d_0' % (tt, h), 'ss%d_%d_1' % (tt, h)]))
        for tt in range(NTT):
            HP = [(h, p) for h in range(8) for p in range(2)]
            sk = lambda h, p, tt=tt: 'ss%d_%d_%d' % (tt, h, p)
            for (h, p) in HP:
                L.append(lambda h=h, p=p, tt=tt: S.op('dve', lambda e: e.max(out=v16[:, h, p, 0:8], in_=ssb[tt][:, h, p, :]), [sk(h, p, tt)], ['va%d_%d' % (h, p)]))
            for (h, p) in HP:
                L.append(lambda h=h, p=p, tt=tt: S.op('dve', lambda e: e.max_index(out=i16[:, h, p, 0:8], in_max=v16[:, h, p, 0:8], in_values=ssb[tt][:, h, p, :]),
                                                      [sk(h, p, tt), 'va%d_%d' % (h, p)], ['ia%d_%d' % (h, p)]))
            for (h, p) in HP:
                L.append(lambda h=h, p=p, tt=tt: S.op('dve', lambda e: e.match_replace(out=ssb[tt][:, h, p, :], in_to_replace=v16[:, h, p, 0:8], in_values=ssb[tt][:, h, p, :], imm_value=-1e30),
                                                      ['va%d_%d' % (h, p)], [sk(h, p, tt)]))
            for (h, p) in HP:
                L.append(lambda h=h, p=p, tt=tt: S.op('dve', lambda e: e.max(out=v16[:, h, p, 8:16], in_=ssb[tt][:, h, p, :]), [sk(h, p, tt)], ['vb%d_%d' % (h, p)]))
            for (h, p) in HP:
                L.append(lambda h=h, p=p, tt=tt: S.op('dve', lambda e: e.max_index(out=i16[:, h, p, 8:16], in_max=v16[:, h, p, 8:16], in_values=ssb[tt][:, h, p, :]),
                                                      [sk(h, p, tt), 'vb%d_%d' % (h, p)], ['ib%d_%d' % (h, p)]))
            VK = ['va%d_%d' % hp for hp in HP] + ['vb%d_%d' % hp for hp in HP]
            IK = ['ia%d_%d' % hp for hp in HP] + ['ib%d_%d' % hp for hp in HP]
            CK = ['cand%d' % h for h in range(8)]
            FVK = ['fva%d' % h for h in range(8)] + ['fvb%d' % h for h in range(8)]
            FIK = ['fia%d' % h for h in range(8)] + ['fib%d' % h for h in range(8)]
            L.append(lambda IK=IK: b.cp('dve', i16f[:], i16[:], IK, ['i16f']))
            L.append(lambda VK=VK, CK=CK: b.tt('dve', cand[:], v16[:, :, 0, :].unsqueeze(3).to_broadcast([128, 8, 16, 16]),
                                               v16[:, :, 1, :].unsqueeze(2).to_broadcast([128, 8, 16, 16]), ALU.add, VK, CK))
            cs = lambda h: cand[:, h, :, :].rearrange("p a b -> p (a b)")
            for h in range(8):
                L.append(lambda h=h: S.op('dve', lambda e: e.max(out=fv[:, h, 0:8], in_=cs(h)), ['cand%d' % h], ['fva%d' % h]))
            for h in range(8):
                L.append(lambda h=h: S.op('dve', lambda e: e.max_index(out=fi[:, h, 0:8], in_max=fv[:, h, 0:8], in_values=cs(h)), ['cand%d' % h, 'fva%d' % h], ['fia%d' % h]))
            for h in range(8):
                L.append(lambda h=h: S.op('dve', lambda e: e.match_replace(out=cs(h), in_to_replace=fv[:, h, 0:8], in_values=cs(h), imm_value=-1e30), ['fva%d' % h, 'cand%d' % h], ['cand%d' % h]))
            for h in range(8):
                L.append(lambda h=h: S.op('dve', lambda e: e.max(out=fv[:, h, 8:16], in_=cs(h)), ['cand%d' % h], ['fvb%d' % h]))
            for h in range(8):
                L.append(lambda h=h: S.op('dve', lambda e: e.max_index(out=fi[:, h, 8:16], in_max=fv[:, h, 8:16], in_values=cs(h)), ['cand%d' % h, 'fvb%d' % h], ['fib%d' % h]))
            L.append(lambda FVK=FVK: b.tt('dve', gate[:], fv[:], fv[:, :, 0:1].to_broadcast([128, 8, 16]), ALU.subtract, FVK, ['gate']))
            L.append(lambda: b.act(gate[:], gate[:], AF.Exp, ['gate'], ['gate']))
            L.append(lambda: S.op('dve', lambda e: e.tensor_reduce(out=gsum[:], in_=gate[:], axis=AX.X, op=ALU.add), ['gate'], ['gsum']))
            L.append(lambda: b.recip(gsum[:], gsum[:], ['gsum'], ['gsum']))
            L.append(lambda: b.tt('dve', gate[:], gate[:], gsum[:].unsqueeze(2).to_broadcast([128, 8, 16]), ALU.mult, ['gate', 'gsum'], ['gate']))
            L.append(lambda FIK=FIK: S.op('dve', lambda e: e.tensor_single_scalar(out=fa[:], in_=fi[:], scalar=4, op=ALU.logical_shift_right), FIK, ['fa']))
            L.append(lambda FIK=FIK: S.op('dve', lambda e: e.tensor_single_scalar(out=fb[:], in_=fi[:], scalar=15, op=ALU.bitwise_and), FIK, ['fb']))
            L.append(lambda: b.cp('dve', faf[:], fa[:], ['fa'], ['faf']))
            L.append(lambda: b.cp('dve', fbf[:], fb[:], ['fb'], ['fbf']))
            for p, pos in ((0, faf), (1, fbf)):
                L.append(lambda pos=pos, CK=CK: b.tt('dve', cand[:].rearrange("p h k a -> p (h k) a"), pos[:].rearrange("p h k -> p (h k)").unsqueeze(2).to_broadcast([128, 128, 16]),
                                                     iota16.unsqueeze(1).to_broadcast([128, 128, 16]), ALU.is_equal, ['faf', 'fbf'] + CK, CK))
                L.append(lambda p=p, CK=CK: b.tt('dve', cand[:], cand[:], i16f[:, :, p, :].unsqueeze(2).to_broadcast([128, 8, 16, 16]), ALU.mult, CK + ['i16f'], CK))
                L.append(lambda p=p, CK=CK: S.op('dve', lambda e: e.tensor_reduce(out=i12[:, p, :, :], in_=cand[:], axis=AX.X, op=ALU.add), CK, ['i12']))
            L.append(lambda: b.tr(HX[:, 0:128], i12[:, 0, :, :].rearrange("p h k -> p (h k)"), b.identf[:], ['i12', 'identf'], ['pb2']))
            L.append(lambda: b.tr(HX[:, 128:256], i12[:, 1, :, :].rearrange("p h k -> p (h k)"), b.identf[:], ['i12', 'identf'], ['pb2']))
            L.append(lambda: b.tr(HX[:, 256:384], gate[:].rearrange("p h k -> p (h k)"), b.identf[:], ['gate', 'identf'], ['pb2']))
            L.append(lambda tt=tt: b.cp('act', T3[:, :, tt * 128:(tt + 1) * 128], HX[:, 0:384].rearrange("p (a t) -> p a t", a=3), ['pb2'], ['T3']))
        return L

    def g_build():
        for t in range(TB):
            sl = t % 4
            b.ts('dve', Pt[sl][:], b.iotab[:], T3[:, 0, t:t + 1], T3[:, 2, t:t + 1], ALU.is_equal, ALU.mult, ['iotab', 'T3'], ['Pt%d' % sl])
            b.ts('dve', Qt[sl][:], b.iotab[:], T3[:, 1, t:t + 1], None, ALU.is_equal, None, ['iotab', 'T3'], ['Qt%d' % sl])
            bi = (t // 4) % 2
            b.mm(B[bi][:, (t % 4) * 128:(t % 4 + 1) * 128], Pt[sl][:], Qt[sl][:], True, True, ['Pt%d' % sl, 'Qt%d' % sl], ['pb%d' % bi])
            if t % 4 == 3:
                b.cp('act', Gs[:, t - 3:t + 1, :], B[bi][:].rearrange("p (t j) -> p t j", t=4), ['pb%d' % bi], ['Gs'])

    def main_loop(blk, inter):
        st_ = blk % 2
        hb_ = h2Tb[st_]
        hbk = 'h2Tb%d' % st_
        ni = len(inter)
        done = 0
        for j in range(128):
            sl = j % 3
            S.dma(ut[sl][:].rearrange("p c i -> p (c i)"), d['uts'][j // JG][:, j % JG, :], (), ['ut%d' % sl])
            S.dma(vb[sl][:], d['vbs'][j // JG][:, j % JG, :], (), ['vb%d' % sl])
            ba, ka = B[j % 2], 'pb%d' % (j % 2)
            for dc in range(8):
                b.mm(ba[:, 0:TB], ut[sl][:, dc, :], hb_[:, dc, :], dc == 0, dc == 7, ['ut%d' % sl, hbk], [ka])
            w = j % 2
            b.act(gl[w][:], ba[:, 0:TB], AF.Gelu_apprx_tanh, [ka], ['gl%d' % w])
            b.tt('dve', wT[w][:], gl[w][:], Gs[:, :, j], ALU.mult, ['gl%d' % w, 'Gs'], ['wT%d' % w])
            for tt in range(NTT):
                for hh in range(2):
                    bo = 3 + tt * 2 + hh
                    b.mm(B[bo][:, :], wT[w][:, tt * 128:(tt + 1) * 128], vb[sl][:, hh * 512:(hh + 1) * 512], j == 0, j == 127,
                         ['wT%d' % w, 'vb%d' % sl], ['pb%d' % bo])
            tgt = (ni * (j + 1) + 119) // 120 if j < 120 else ni
            tgt = min(tgt, ni)
            while done < tgt:
                inter[done]()
                done += 1

    def epilogue(blk):
        st_ = blk % 2
        for tt in range(NTT):
            T = blk * NTT + tt
            xk = 'pxt%d_%d' % (st_, tt)
            for hh in range(2):
                bo = 3 + tt * 2 + hh
                b.tt('dve', x2[:, hh * 512:(hh + 1) * 512], B[bo][:, :], xt[st_][tt][:, hh * 512:(hh + 1) * 512], ALU.add,
                     ['pb%d' % bo, xk], ['x2'])
            if last:
                y2 = xt[st_][tt]
                rms_rstd(b, x2[:], 'x2', ssq, sd, rstd, 1, y2[:], xk, 'fn')
                b.act(y2[:], x2[:], AF.Copy, ['x2', 'fnrstd'], [xk], scale=rstd[:, 1:2])
                b.tt('pool', y2[:], y2[:], gfin[:], ALU.mult, [xk, 'gfin'], [xk])
                S.dma(xout[T * 128:(T + 1) * 128, :], y2[:], [xk], ())
            else:
                S.dma(xout[T * 128:(T + 1) * 128, :], x2[:], ['x2'], ())

    for f in h_phase(0):
        f()
    S.bg_wait(['sp'])
    import os
    dbg = os.environ.get('PEER_DBG', '')
    for blk in range(nblk):
        if 'nog' not in dbg or blk == 0:
            g_build()
        inter = h_phase(blk + 1) if (blk + 1 < nblk and 'noh' not in dbg) else []
        main_loop(blk, inter)
        epilogue(blk)
    b.reset(m)


def mixer_common_alloc(b, d, nctx_cols, rope_src):
    S = b.S
    W = {'xslot': 0}
    W['xt'] = [b.sb([128, 1024], F32, 'mxt%d' % i) for i in range(2)]
    W['xn'] = [b.sb([128, 1024], BF16, 'mxn%d' % i) for i in range(2)]
    W['ssq'] = b.sb([128, 2], F32, 'mssq')
    W['sd'] = b.sb([128, 2], F32, 'msd')
    W['rstd'] = b.sb([128, 2], F32, 'mrstd')
    W['gcol'] = b.sb([128, 8], F32, 'gcol')
    S.dma(W['gcol'][:], d['gmix'], (), ['gcol'])
    W['win'] = b.sb([128, 8, 1792], BF16, 'win')
    W['wpm'] = b.sb([128, 8, 640], BF16, 'wpm')
    wst = b.sb([128, 1792], F32, 'wst')
    for kc in range(8):
        S.dma(wst[:], d['w_in'][:, kc, :], (), ['wst'])
        b.cp('pool', W['win'][:, kc, :], wst[:], ['wst'], ['win'])
        S.dma(wst[:, 0:640], d['w_perm'][:, kc, :], (), ['wst'])
        b.cp('pool', W['wpm'][:, kc, :], wst[:, 0:640], ['wst'], ['wpm'])
    W['rope'] = b.sb([64, 2, nctx_cols], F32, 'rope')
    if rope_src is not None:
        S.dma(W['rope'][:], rope_src, (), ['rope'])
    W['hT'] = [b.sb([128, 8, 512], BF16, 'hT%d' % i) for i in range(2)]
    W['t1'] = [b.sb([128, 512], F32, 't1_%d' % i) for i in range(2)]
    W['t2'] = [b.sb([128, 512], F32, 't2_%d' % i) for i in range(2)]
    W['hs'] = 0
    return W


def out_proj(b, d, c, catA, catH):
    S = b.S
    B = b.banks
    woA = b.sb([128, 4, 1024], BF16, 'woA')
    woH = b.sb([64, 8, 1024], BF16, 'woH')
    wst = b.sb([128, 2, 1024], F32, 'wost')
    for c2_ in range(2):
        S.dma(wst[:], d['w_outA'][:, c2_ * 2:(c2_ + 1) * 2, :], (), ['wost'])
        b.cp('pool', woA[:, c2_ * 2:(c2_ + 1) * 2, :], wst[:], ['wost'], ['woA'])
    for c4 in range(4):
        S.dma(wst[0:64], d['w_outH'][:, c4 * 2:(c4 + 1) * 2, :], (), ['wost'])
        b.cp('pool', woH[:, c4 * 2:(c4 + 1) * 2, :], wst[0:64], ['wost'], ['woH'])
    xt = [b.sb([128, 1024], F32, 'oxt%d' % i) for i in range(2)]
    xo = [b.sb([128, 1024], F32, 'oxo%d' % i) for i in range(2)]
    for T in range(NT // 128):
        sl = T % 2
        S.dma(xt[sl][:], c['x_own'][T * 128:(T + 1) * 128, :], (), ['oxt%d' % sl])
        for hh in range(2):
            bo, ko = b.nb([0, 1, 2, 3])
            n = 0
            for cc in range(4):
                b.mm(bo[:, :], catA[:, cc, T * 128:(T + 1) * 128], woA[:, cc, hh * 512:(hh + 1) * 512], n == 0, False, ['catA', 'woA'], [ko])
                n += 1
            for h in range(8):
                b.mm(bo[:, :], catH[:, h, T * 128:(T + 1) * 128], woH[:, h, hh * 512:(hh + 1) * 512], False, h == 7, ['catH', 'woH'], [ko])
            b.tt('dve', xo[sl][:, hh * 512:(hh + 1) * 512], bo[:, :], xt[sl][:, hh * 512:(hh + 1) * 512], ALU.add, [ko, 'oxt%d' % sl], ['oxo%d' % sl])
        S.dma(c['x1'][T * 128:(T + 1) * 128, :], xo[sl][:], ['oxo%d' % sl], ())


def rope_epilogue(b, W, pq, kq, pp, kp, c0, n, out, okey, rstd=None, rkey=None):
    i = W['hs']
    W['hs'] ^= 1
    t1 = W['t1'][i]
    t2 = W['t2'][i]
    b.tt('dve', t1[0:64, 0:n], pq[0:64, 0:n], W['rope'][:, 0, c0:c0 + n], ALU.mult, [kq, 'rope'], ['t1_%d' % i])
    b.tt('dve', t2[0:64, 0:n], pp[0:64, 0:n], W['rope'][:, 1, c0:c0 + n], ALU.mult, [kp, 'rope'], ['t2_%d' % i])
    if rstd is None:
        b.tt('pool', out, t1[0:64, 0:n], t2[0:64, 0:n], ALU.add, ['t1_%d' % i, 't2_%d' % i], [okey])
    else:
        b.tt('pool', t1[0:64, 0:n], t1[0:64, 0:n], t2[0:64, 0:n], ALU.add, ['t1_%d' % i, 't2_%d' % i], ['t1_%d' % i])
        b.tt('pool', out, t1[0:64, 0:n], rstd, ALU.mult, ['t1_%d' % i, rkey], [okey])


def emit_mixer_even(b, d, c):
    S = b.S
    B = b.banks
    m0 = b.mark()
    aT = b.sb([128, 4, NT + 30], F32, 'aT')
    qT = b.sb([64, 8, NT], BF16, 'qT')
    kT = b.sb([64, 2, NT + 256], BF16, 'kT')
    Vt = b.sb([128, 18, 128], BF16, 'Vt')
    m1 = b.mark()
    W = mixer_common_alloc(b, d, NT + 256, c['rope'])
    sig = [b.sb([128, 512], F32, 'sig%d' % i) for i in range(2)]
    atmp = b.sb([128, 256], F32, 'atmp')
    win, wpm = W['win'], W['wpm']

    def kproj(hT, hk, ntok, tcol, kcol):
        for j in range(2):
            pq, kq = b.nb([0, 1, 2, 3, 4, 5])
            proj_fm(b, pq, kq, 64, win, 'win', 1536 + j * 64, hT, hk, 0, ntok)
            pp, kp = b.nb([0, 1, 2, 3, 4, 5])
            proj_fm(b, pp, kp, 64, wpm, 'wpm', 512 + j * 64, hT, hk, 0, ntok)
            rope_epilogue(b, W, pq, kq, pp, kp, tcol, ntok, kT[:, j, kcol:kcol + ntok], 'kT')

    def vproj(hT, hk, ntiles, vt0):
        for tt in range(ntiles):
            for kc in range(8):
                b.mm(B[6][:, 0:128], hT[:, kc, tt * 128:(tt + 1) * 128], win[:, kc, 1664:1792], kc == 0, kc == 7, ['win', hk], ['pb6'])
            b.cp('act', Vt[:, vt0 + tt, :], B[6][:, 0:128], ['pb6'], ['Vt'])

    hT = W['hT'][0]
    norm_tiles_bf16(b, [c['ctxL'], c['ctxR']], W['gcol'][:], hT, 'hT0', W)
    kproj(hT, 'hT0', 128, NT, 0)
    for j in range(2):
        pq, kq = b.nb([0, 1, 2, 3, 4, 5])
        for kc in range(8):
            b.mm(pq[0:64, 0:128], win[:, kc, 1536 + j * 64:1536 + (j + 1) * 64], hT[:, kc, 128:256], kc == 0, kc == 7, ['win', 'hT0'], [kq])
        pp, kp = b.nb([0, 1, 2, 3, 4, 5])
        for kc in range(8):
            b.mm(pp[0:64, 0:128], wpm[:, kc, 512 + j * 64:512 + (j + 1) * 64], hT[:, kc, 128:256], kc == 0, kc == 7, ['wpm', 'hT0'], [kp])
        rope_epilogue(b, W, pq, kq, pp, kp, NT + 128, 128, kT[:, j, NT + 128:NT + 256], 'kT')
    vproj(hT, 'hT0', 1, 0)
    for kc in range(8):
        b.mm(B[6][:, 0:128], hT[:, kc, 128:256], win[:, kc, 1664:1792], kc == 0, kc == 7, ['win', 'hT0'], ['pb6'])
    b.cp('act', Vt[:, 17, :], B[6][:, 0:128], ['pb6'], ['Vt'])
    for cc in range(4):
        pa, ka = b.nb([0, 1, 2, 3, 4, 5])
        proj_fm(b, pa, ka, 128, win, 'win', cc * 128, hT, 'hT0', 0, 256)
        pg, kg = b.nb([0, 1, 2, 3, 4, 5])
        proj_fm(b, pg, kg, 128, win, 'win', 512 + cc * 128, hT, 'hT0', 0, 256)
        b.act(sig[0][:, 0:256], pg[:, 0:256], AF.Sigmoid, [kg], ['sig0'])
        b.tt('dve', atmp[:], pa[:, 0:256], sig[0][:, 0:256], ALU.mult, [ka, 'sig0'], ['atmp'])
        b.cp('act', aT[:, cc, 0:15], atmp[:, 113:128], ['atmp'], ['aT'])
        b.cp('act', aT[:, cc, NT + 15:NT + 30], atmp[:, 128:143], ['atmp'], ['aT'])
    for g in range(4):
        hi = g % 2
        hT = W['hT'][hi]
        hk = 'hT%d' % hi
        norm_tiles_bf16(b, row_tiles(c['x_own'][g * 512:(g + 1) * 512, :], 4), W['gcol'][:], hT, hk, W)
        for cc in range(4):
            pa, ka = b.nb([0, 1, 2, 3, 4, 5])
            proj_fm(b, pa, ka, 128, win, 'win', cc * 128, hT, hk, 0, 512)
            pg, kg = b.nb([0, 1, 2, 3, 4, 5])
            proj_fm(b, pg, kg, 128, win, 'win', 512 + cc * 128, hT, hk, 0, 512)
            si = cc % 2
            b.act(sig[si][:], pg[:, :], AF.Sigmoid, [kg], ['sig%d' % si])
            b.tt('dve', aT[:, cc, 15 + g * 512:15 + (g + 1) * 512], pa[:, :], sig[si][:], ALU.mult, [ka, 'sig%d' % si], ['aT'])
        for h in range(8):
            pq, kq = b.nb([0, 1, 2, 3, 4, 5])
            proj_fm(b, pq, kq, 64, win, 'win', 1024 + h * 64, hT, hk, 0, 512)
            pp, kp = b.nb([0, 1, 2, 3, 4, 5])
            proj_fm(b, pp, kp, 64, wpm, 'wpm', h * 64, hT, hk, 0, 512)
            rope_epilogue(b, W, pq, kq, pp, kp, g * 512, 512, qT[:, h, g * 512:(g + 1) * 512], 'qT')
        for j in range(2):
            pq, kq = b.nb([0, 1, 2, 3, 4, 5])
            proj_fm(b, pq, kq, 64, win, 'win', 1536 + j * 64, hT, hk, 0, 512)
            pp, kp = b.nb([0, 1, 2, 3, 4, 5])
            proj_fm(b, pp, kp, 64, wpm, 'wpm', 512 + j * 64, hT, hk, 0, 512)
            rope_epilogue(b, W, pq, kq, pp, kp, g * 512, 512, kT[:, j, 128 + g * 512:128 + (g + 1) * 512], 'kT')
        vproj(hT, hk, 4, 1 + g * 4)
    b.reset(m1)
    catA = b.sb([128, 4, NT], BF16, 'catA')
    catH = b.sb([64, 8, NT], BF16, 'catH')
    m2_ = b.mark()
    mask = b.sb([128, 3, 384], BF16, 'mask')
    S.dma(mask[:], c['mask3'], (), ['mask'])
    esink = b.sb([64, 8], F32, 'esink')
    S.dma(esink[:], d['sink'].partition_broadcast(64), (), ['esink'])
    b.act(esink[:], esink[:], AF.Exp, ['esink'], ['esink'])
    pT = [b.sb([128, 384], BF16, 'pT%d' % i) for i in range(3)]
    dn = [b.sb([64, 128], F32, 'dn%d' % i) for i in range(2)]
    it = 0
    for h in range(8):
        kv = h // 4
        for qb in range(16):
            msel = 1 if qb == 0 else (2 if qb == 15 else 0)
            bs_, ks = b.nb([0, 1, 2])
            b.mm(bs_[:, 0:384], b.identb[:], mask[:, msel, :], True, False, ['identb', 'mask'], [ks])
            for j in range(3):
                b.mm(bs_[:, j * 128:(j + 1) * 128], kT[:, kv, (qb + j) * 128:(qb + j + 1) * 128], qT[:, h, qb * 128:(qb + 1) * 128],
                     False, j == 2, ['kT', 'qT'], [ks])
            pi = it % 3
            b.act(pT[pi][:], bs_[:, 0:384], AF.Exp, [ks], ['pT%d' % pi], scale=0.125)
            bv, kvk = b.nb([3, 4, 5])
            for j in range(3):
                b.mm(bv[0:64, 0:128], Vt[:, qb + j, kv * 64:(kv + 1) * 64], pT[pi][:, j * 128:(j + 1) * 128], j == 0, j == 2, ['Vt', 'pT%d' % pi], [kvk])
            for j in range(3):
                b.mm(bv[0:64, 128:256], b.onesb[:, 0:64], pT[pi][:, j * 128:(j + 1) * 128], j == 0, j == 2, ['onesb', 'pT%d' % pi], [kvk])
            di = it % 2
            b.ts('dve', dn[di][:], bv[0:64, 128:256], esink[:, h:h + 1], None, ALU.add, None, [kvk, 'esink'], ['dn%d' % di])
            b.recip(dn[di][:], dn[di][:], ['dn%d' % di], ['dn%d' % di])
            b.tt('dve', catH[:, h, qb * 128:(qb + 1) * 128], bv[0:64, 0:128], dn[di][:], ALU.mult, [kvk, 'dn%d' % di], ['catH'])
            it += 1
    cw = b.sb([128, 4, 31], F32, 'cw')
    cvec = b.sb([128, 3, 4], F32, 'cvec')
    S.dma(cw[:], d['cw'], (), ['cw'])
    S.dma(cvec[:], d['cvec'], (), ['cvec'])
    acc = b.sb([128, 4, 512], F32, 'acc')
    sq = [b.sb([128, 512], F32, 'sq%d' % i) for i in range(2)]
    mean = b.sb([128, 512], F32, 'mean')
    m2 = b.sb([128, 512], F32, 'm2')
    var = b.sb([128, 512], F32, 'var')
    yb = [b.sb([128, 512], F32, 'yb%d' % i) for i in range(2)]
    for tg in range(4):
        for cc in range(4):
            a0 = tg * 512
            b.ts('dve', acc[:, cc, :], aT[:, cc, a0:a0 + 512], cw[:, cc, 0:1], cvec[:, 0, cc:cc + 1], ALU.mult, ALU.add, ['aT', 'cw', 'cvec'], ['acc%d' % cc])
            for j in range(1, 31):
                b.stt(acc[:, cc, :], aT[:, cc, a0 + j:a0 + j + 512], cw[:, cc, j:j + 1], acc[:, cc, :], ALU.mult, ALU.add, ['aT', 'cw', 'acc%d' % cc], ['acc%d' % cc])
        for cc in range(4):
            si = cc % 2
            b.act(sq[si][:], acc[:, cc, :], AF.Square, ['acc%d' % cc], ['sq%d' % si])
            b.mm(B[6][:, :], b.onesf[:], acc[:, cc, :], cc == 0, cc == 3, ['onesf', 'acc%d' % cc], ['pb6'])
            b.mm(B[0][:, :], b.onesf[:], sq[si][:], cc == 0, cc == 3, ['onesf', 'sq%d' % si], ['pb0'])
        b.act(mean[:], B[6][:, :], AF.Copy, ['pb6'], ['mean'], scale=1.0 / 512)
        b.act(m2[:], B[6][:, :], AF.Square, ['pb6'], ['m2'], scale=1.0 / 512)
        b.stt(var[:], B[0][:, :], 1.0 / 512, m2[:], ALU.mult, ALU.subtract, ['pb0', 'm2'], ['var'])
        b.act(var[:], var[:], AF.Sqrt, ['var'], ['var'], bias=EPS)
        b.recip(var[:], var[:], ['var'], ['var'])
        for cc in range(4):
            yi = cc % 2
            b.tt('dve', yb[yi][:], acc[:, cc, :], mean[:], ALU.subtract, ['acc%d' % cc, 'mean'], ['yb%d' % yi])
            b.tt('pool', yb[yi][:], yb[yi][:], var[:], ALU.mult, ['yb%d' % yi, 'var'], ['yb%d' % yi])
            b.ts('pool', yb[yi][:], yb[yi][:], cvec[:, 1, cc:cc + 1], cvec[:, 2, cc:cc + 1], ALU.mult, ALU.add, ['yb%d' % yi, 'cvec'], ['yb%d' % yi])
            b.act(catA[:, cc, tg * 512:(tg + 1) * 512], yb[yi][:], AF.Silu, ['yb%d' % yi], ['catA'])
    b.reset(m2_)
    out_proj(b, d, c, catA, catH)
    b.reset(m0)


def emit_mixer_odd(b, d, c):
    S = b.S
    B = b.banks
    m0 = b.mark()
    NA = 2 * NT
    qT = b.sb([64, 8, NT], BF16, 'qT')
    kT = b.sb([64, 2, NA], BF16, 'kT')
    Vt = b.sb([128, 32, 128], BF16, 'Vt')
    uT = b.sb([128, 4, NT], BF16, 'uT')
    vn = b.sb([128, 16, 512], BF16, 'vn')
    m1 = b.mark()
    W = mixer_common_alloc(b, d, 512, None)
    win, wpm = W['win'], W['wpm']
    gq = b.sb([64, 4], F32, 'gq')
    S.dma(gq[:], d['gqk'], (), ['gq'])
    sqb = [b.sb([64, 512], F32, 'sqb%d' % i) for i in range(2)]
    rsb = [b.sb([64, 512], F32, 'rsb%d' % i) for i in range(2)]
    vg = [b.sb([128, 512], F32, 'vg%d' % i) for i in range(2)]
    st6 = b.sb([128, 6], F32, 'st6')
    mv = b.sb([128, 2], F32, 'mv')
    lnr = b.sb([128, 2], F32, 'lnr')
    gbc = b.sb([128, 2, 512], F32, 'gbc')
    S.dma(gbc[:, 0, :], d['sgu_g'].partition_broadcast(128), (), ['gbc'])
    S.dma(gbc[:, 1, :], d['sgu_b'].partition_broadcast(128), (), ['gbc'])
    cnt = [0]

    def qk_head(hT, hk, wcol, pcol, gcol_i, out, okey):
        i = cnt[0] % 2
        cnt[0] += 1
        pq, kq = b.nb([0, 1, 2, 3, 4, 5])
        proj_fm(b, pq, kq, 64, win, 'win', wcol, hT, hk, 0, 512)
        pp, kp = b.nb([0, 1, 2, 3, 4, 5])
        proj_fm(b, pp, kp, 64, wpm, 'wpm', pcol, hT, hk, 0, 512)
        b.act(sqb[i][:], pq[0:64, :], AF.Square, [kq], ['sqb%d' % i])
        ps_, kss = b.nb([0, 1, 2, 3, 4, 5])
        b.mm(ps_[0:64, :], b.onesf[0:64, 0:64], sqb[i][:], True, True, ['onesf', 'sqb%d' % i], [kss])
        b.act(rsb[i][:], ps_[0:64, :], AF.Sqrt, [kss], ['rsb%d' % i], scale=1.0 / 64, bias=EPS)
        b.recip(rsb[i][:], rsb[i][:], ['rsb%d' % i], ['rsb%d' % i])
        t1 = W['t1'][i]
        t2 = W['t2'][i]
        b.stt(t1[0:64, :], pq[0:64, :], gq[:, gcol_i:gcol_i + 1], W['rope'][:, 0, :], ALU.mult, ALU.mult, [kq, 'gq', 'rope'], ['t1_%d' % i])
        b.stt(t2[0:64, :], pp[0:64, :], gq[:, gcol_i + 1:gcol_i + 2], W['rope'][:, 1, :], ALU.mult, ALU.mult, [kp, 'gq', 'rope'], ['t2_%d' % i])
        b.tt('pool', t1[0:64, :], t1[0:64, :], t2[0:64, :], ALU.add, ['t1_%d' % i, 't2_%d' % i], ['t1_%d' % i])
        b.tt('pool', out, t1[0:64, :], rsb[i][:], ALU.mult, ['t1_%d' % i, 'rsb%d' % i], [okey])

    for g in range(8):
        own = g < 4
        hi = g % 2
        hT = W['hT'][hi]
        hk = 'hT%d' % hi
        src = c['x_own'][g * 512:(g + 1) * 512, :] if own else c['x_ctx'][(g - 4) * 512:(g - 3) * 512, :]
        S.dma(W['rope'][:], c['rope'][:, :, g * 512:(g + 1) * 512], (), ['rope'])
        norm_tiles_bf16(b, row_tiles(src, 4), W['gcol'][:], hT, hk, W)
        if own:
            for h in range(8):
                qk_head(hT, hk, h * 64, h * 64, 0, qT[:, h, g * 512:(g + 1) * 512], 'qT')
        for j in range(2):
            qk_head(hT, hk, 512 + j * 64, 512 + j * 64, 2, kT[:, j, g * 512:(g + 1) * 512], 'kT')
        for tt in range(4):
            for kc in range(8):
                b.mm(B[6][:, 0:128], hT[:, kc, tt * 128:(tt + 1) * 128], win[:, kc, 640:768], kc == 0, kc == 7, ['win', hk], ['pb6'])
            b.cp('act', Vt[:, g * 4 + tt, :], B[6][:, 0:128], ['pb6'], ['Vt'])
        if own:
            for cc in range(4):
                pu, ku = b.nb([0, 1, 2, 3, 4, 5])
                proj_fm(b, pu, ku, 128, win, 'win', 768 + cc * 128, hT, hk, 0, 512)
                b.act(uT[:, cc, g * 512:(g + 1) * 512], pu[:, :], AF.Gelu_apprx_tanh, [ku], ['uT'])
            for tt in range(4):
                T = g * 4 + tt
                vi = tt % 2
                pv, kv_ = b.nb([0, 1, 2, 3, 4, 5])
                for kc in range(8):
                    b.mm(pv[:, :], hT[:, kc, tt * 128:(tt + 1) * 128], win[:, kc, 1280:1792], kc == 0, kc == 7, ['win', hk], [kv_])
                b.act(vg[vi][:], pv[:, :], AF.Gelu_apprx_tanh, [kv_], ['vg%d' % vi])
                S.op('dve', lambda e, vi=vi: e.bn_stats(out=st6[:], in_=vg[vi][:]), ['vg%d' % vi], ['st6'])
                S.op('dve', lambda e: e.bn_aggr(out=mv[:], in_=st6[:]), ['st6'], ['mv'])
                b.act(lnr[:, 0:1], mv[:, 1:2], AF.Sqrt, ['mv'], ['lnr'], bias=EPS)
                b.recip(lnr[:, 1:2], lnr[:, 0:1], ['lnr'], ['lnr'])
                b.ts('dve', vg[vi][:], vg[vi][:], mv[:, 0:1], lnr[:, 1:2], ALU.subtract, ALU.mult, ['vg%d' % vi, 'mv', 'lnr'], ['vg%d' % vi])
                b.tt('pool', vg[vi][:], vg[vi][:], gbc[:, 0, :], ALU.mult, ['vg%d' % vi, 'gbc'], ['vg%d' % vi])
                b.tt('pool', vn[:, T, :], vg[vi][:], gbc[:, 1, :], ALU.add, ['vg%d' % vi, 'gbc'], ['vn'])
    b.reset(m1)
    catA = b.sb([128, 4, NT], BF16, 'catA')
    catH = b.sb([64, 8, NT], BF16, 'catH')
    m2_ = b.mark()
    pT = [b.sb([128, 512], BF16, 'pT%d' % i) for i in range(3)]
    rden = [b.sb([64, 512], F32, 'rden%d' % i) for i in range(2)]
    it = 0
    n = 0
    for h in range(8):
        kv = h // 4
        for qg in range(4):
            bv = 3 + (it % 2) * 2
            bd = 4 + (it % 2) * 2
            for kb in range(32):
                bs_, ks = b.nb([0, 1, 2])
                b.mm(bs_[:, :], kT[:, kv, kb * 128:(kb + 1) * 128], qT[:, h, qg * 512:(qg + 1) * 512], True, True, ['kT', 'qT'], [ks])
                pi = n % 3
                n += 1
                b.act(pT[pi][:], bs_[:, :], AF.Exp, [ks], ['pT%d' % pi], scale=0.125)
                b.mm(B[bv][0:64, :], Vt[:, kb, kv * 64:(kv + 1) * 64], pT[pi][:], kb == 0, kb == 31, ['Vt', 'pT%d' % pi], ['pb%d' % bv])
                b.mm(B[bd][0:64, :], b.onesb[:, 0:64], pT[pi][:], kb == 0, kb == 31, ['onesb', 'pT%d' % pi], ['pb%d' % bd])
            ri = it % 2
            b.recip(rden[ri][:], B[bd][0:64, :], ['pb%d' % bd], ['rden%d' % ri])
            b.tt('dve', catH[:, h, qg * 512:(qg + 1) * 512], B[bv][0:64, :], rden[ri][:], ALU.mult, ['pb%d' % bv, 'rden%d' % ri], ['catH'])
            it += 1
    wsT = b.sb([128, 4, 128], BF16, 'wsT')
    wsst = b.sb([128, 4, 128], F32, 'wsst')
    S.dma(wsst[:], d['wsT'], (), ['wsst'])
    b.cp('pool', wsT[:], wsst[:], ['wsst'], ['wsT'])
    bsb = b.sb([128, 512], F32, 'bsb')
    S.dma(bsb[:], d['sgu_bs'].partition_broadcast(128), (), ['bsb'])
    tmx = [b.sb([128, 512], F32, 'tmx%d' % i) for i in range(2)]
    for T in range(16):
        bm, km = b.nb([0, 1, 2])
        for g in range(4):
            b.mm(bm[:, g * 128:(g + 1) * 128], vn[:, T, g * 128:(g + 1) * 128], wsT[:, g, :], True, True, ['vn', 'wsT'], [km])
        ti = T % 2
        b.tt('dve', tmx[ti][:], bm[:, :], bsb[:], ALU.add, [km, 'bsb'], ['tmx%d' % ti])
        b.tt('pool', catA[:, :, T * 128:(T + 1) * 128], tmx[ti][:].rearrange("p (g t) -> p g t", g=4), uT[:, :, T * 128:(T + 1) * 128],
             ALU.mult, ['tmx%d' % ti, 'uT'], ['catA'])
    b.reset(m2_)
    out_proj(b, d, c, catA, catH)
    b.reset(m0)


LAYER_SHAPES = {
    'gmix': [128, 8], 'w_in': [128, 8, 1792], 'w_perm': [128, 8, 640], 'w_outA': [128, 4, 1024], 'w_outH': [64, 8, 1024],
    'gffn': [128, 8], 'wq': [128, 8, 1024], 'skbd': [128, 8, 256], 'uth': [128, 128, 1024], 'vbh': [128, 128, 1024],
}
EVEN_SHAPES = {'sink': [8], 'cw': [128, 4, 31], 'cvec': [128, 3, 4]}
ODD_SHAPES = {'gqk': [64, 4], 'sgu_g': [512], 'sgu_b': [512], 'wsT': [128, 4, 128], 'sgu_bs': [512]}


def build_fused_program(nlayers=4):
    nc = bass.Bass("TRN2", target_bir_lowering=False)

    def inp(name, shape, dt=F32):
        return nc.dram_tensor(name, list(shape), dt, kind="ExternalInput").ap()
    x_in = inp('x', [SEQ, D])
    rope_e = inp('rope_e', [2, 64, 2, NT + 256])
    rope_o = inp('rope_o', [2, 64, 2, 2 * NT])
    mask3 = inp('mask3', [2, 128, 3, 384], BF16)
    zrow = inp('zrow', [128, D])
    gfin = inp('gfin', [D])
    out = nc.dram_tensor('out', [SEQ, D], F32, kind="ExternalOutput").ap()
    xa = nc.dram_tensor('xa', [SEQ, D], F32).ap()
    xb = nc.dram_tensor('xb', [SEQ, D], F32).ap()
    x1 = nc.dram_tensor('x1', [SEQ, D], F32).ap()
    uts = nc.dram_tensor('uts', [128 // JG, 128, JG, 1024], BF16).ap()
    vbs = nc.dram_tensor('vbs', [128 // JG, 128, JG, 1024], BF16).ap()
    with ExitStack() as st:
        b = Bld(nc, st)
        b.consts()
        cur = x_in
        for L in range(nlayers):
            even = L % 2 == 0
            last = L == nlayers - 1
            d = {'uts': uts, 'vbs': vbs, 'gfin': gfin}
            shapes = dict(LAYER_SHAPES)
            shapes.update(EVEN_SHAPES if even else ODD_SHAPES)
            for k, shp in shapes.items():
                d[k] = inp('L%d_%s' % (L, k), shp)
            nxt = out if last else (xa if L % 2 == 0 else xb)
            for f in prep_dma_list(b, d):
                b.S.bg_dma('pool', f)
            for half in range(2):
                o0 = half * NT
                o1 = (1 - half) * NT
                c = {'x_own': cur[o0:o0 + NT, :], 'x1': x1[o0:o0 + NT, :]}
                if even:
                    c['ctxL'] = cur[o0 - 128:o0, :] if half == 1 else zrow
                    c['ctxR'] = cur[o0 + NT:o0 + NT + 128, :] if half == 0 else zrow
                    c['rope'] = rope_e[half]
                    c['mask3'] = mask3[half]
                    emit_mixer_even(b, d, c)
                else:
                    c['x_ctx'] = cur[o1:o1 + NT, :]
                    c['rope'] = rope_o[half]
                    emit_mixer_odd(b, d, c)
            emit_peer(b, d, last, x1, nxt, SEQ)
            cur = nxt
            print('layer', L, 'instructions', b.S.n_instr, {e: b.S.cnt[e] for e in ENGS}, 'dmas', b.S.ndma, flush=True)
            if not last:
                b.S.new_epoch(st)
        b.S.emit_all()
    return nc


PERM_1D = np.concatenate([np.arange(32, 64), np.arange(0, 32)])
PERM_AX = np.concatenate([np.arange(16, 32), np.arange(0, 16), np.arange(48, 64), np.arange(32, 48)])
_f = np.float32


def _fm(v, c):
    return np.ascontiguousarray(v.reshape(c, 128).T)


def host_layer_inputs(layer, inp):
    i = layer // 2
    even = layer % 2 == 0
    o = {}
    o['gmix'] = _fm(inp['mix_norm_g'][layer], 8)
    o['gffn'] = _fm(inp['ffn_norm_g'][layer], 8)
    Win = inp['even_w_in'][i] if even else inp['odd_w_in'][i]
    Wout = inp['even_w_out'][i] if even else inp['odd_w_out'][i]
    o['w_in'] = np.ascontiguousarray(Win.reshape(8, 128, 1792).transpose(1, 0, 2))
    qk0 = 1024 if even else 0
    perm = PERM_1D if even else PERM_AX
    qk = Win[:, qk0:qk0 + 640].reshape(1024, 10, 64)[:, :, perm].reshape(1024, 640)
    o['w_perm'] = np.ascontiguousarray(qk.reshape(8, 128, 640).transpose(1, 0, 2))
    A0, H0 = (0, 512) if even else (512, 0)
    o['w_outA'] = np.ascontiguousarray(Wout[A0:A0 + 512].reshape(4, 128, 1024).transpose(1, 0, 2))
    o['w_outH'] = np.ascontiguousarray(Wout[H0:H0 + 512].reshape(8, 64, 1024).transpose(1, 0, 2))
    if even:
        o['sink'] = np.ascontiguousarray(inp['sink_logits'][i])
        o['cw'] = np.ascontiguousarray(inp['conv_w'][i][:, 0, :].reshape(31, 4, 128).transpose(2, 1, 0))
        o['cvec'] = np.ascontiguousarray(np.stack([_fm(inp['conv_b'][i], 4), _fm(inp['conv_ln_g'][i], 4), _fm(inp['conv_ln_b'][i], 4)], axis=1))
    else:
        qg, kg = inp['q_norm_g'][i], inp['k_norm_g'][i]
        o['gqk'] = np.ascontiguousarray(np.stack([qg, qg[PERM_AX], kg, kg[PERM_AX]], axis=1))
        o['sgu_g'] = np.ascontiguousarray(inp['sgu_ln_g'][i])
        o['sgu_b'] = np.ascontiguousarray(inp['sgu_ln_b'][i])
        o['wsT'] = np.ascontiguousarray(inp['sgu_w'][i].transpose(2, 0, 1))
        o['sgu_bs'] = np.ascontiguousarray(inp['sgu_b'][i].reshape(512))
    o['wq'] = np.ascontiguousarray(inp['peer_wq'][layer].reshape(8, 128, 1024).transpose(1, 0, 2))
    sk = inp['peer_subkeys'][layer]
    skbd = np.zeros((128, 8, 256), _f)
    skbd[0:64, :, 0:128] = sk[:, 0].transpose(2, 0, 1)
    skbd[64:128, :, 128:256] = sk[:, 1].transpose(2, 0, 1)
    o['skbd'] = skbd
    U = inp['peer_u'][layer]
    V = inp['peer_v'][layer]
    o['uth'] = np.ascontiguousarray(U.reshape(128, 128, 8, 128).transpose(1, 3, 2, 0)).reshape(128, 128, 1024)
    o['vbh'] = np.ascontiguousarray(V.reshape(128, 128, 1024).transpose(1, 0, 2))
    return o


def rope_tables_even(pos):
    inv = (_f(10000.0) ** (-np.arange(0, 64, 2, dtype=_f) / _f(64))).astype(_f)
    ang = pos.astype(_f)[None, :] * inv[:, None]
    ang = np.concatenate([ang, ang], axis=0)
    sgn = np.concatenate([-np.ones(32, _f), np.ones(32, _f)])[:, None]
    return np.stack([np.cos(ang), np.sin(ang) * sgn], axis=1).astype(_f)


def rope_tables_axial(pos):
    inv = (_f(10000.0) ** (-np.arange(0, 32, 2, dtype=_f) / _f(32))).astype(_f)
    rows = (pos // 64).astype(_f)
    cols = (pos % 64).astype(_f)
    ar = rows[None, :] * inv[:, None]
    ac = cols[None, :] * inv[:, None]
    ang = np.concatenate([ar, ar, ac, ac], axis=0)
    sgn = np.concatenate([-np.ones(16, _f), np.ones(16, _f), -np.ones(16, _f), np.ones(16, _f)])[:, None]
    return np.stack([np.cos(ang), np.sin(ang) * sgn], axis=1).astype(_f)


def window_masks(half):
    k = np.arange(128)[:, None]
    q = np.arange(128)[None, :]
    left = np.where(k >= q, 0.0, NEGM)
    mid = np.zeros((128, 128))
    right = np.where(k <= q, 0.0, NEGM)
    full = np.full((128, 128), NEGM)
    m_mid = np.concatenate([left, mid, right], axis=1)
    m_first = np.concatenate([full if half == 0 else left, mid, right], axis=1)
    m_last = np.concatenate([left, mid, full if half == 1 else right], axis=1)
    return np.stack([m_mid, m_first, m_last], axis=1).astype(ml_dtypes.bfloat16)


def host_const_inputs():
    o = {}
    re, ro, mk = [], [], []
    for half in range(2):
        o0 = half * NT
        o1 = (1 - half) * NT
        pos = np.concatenate([o0 + np.arange(NT), o0 - 128 + np.arange(128), o0 + NT + np.arange(128)])
        re.append(rope_tables_even(pos))
        pos = np.concatenate([o0 + np.arange(NT), o1 + np.arange(NT)])
        ro.append(rope_tables_axial(pos))
        mk.append(window_masks(half))
    o['rope_e'] = np.stack(re)
    o['rope_o'] = np.stack(ro)
    o['mask3'] = np.stack(mk)
    o['zrow'] = np.zeros((128, D), _f)
    return o


_PROG = {}


def kernel(**inputs):
    inp = {k: np.asarray(v) for k, v in inputs.items()}
    x = np.ascontiguousarray(inp['x'], dtype=_f)
    if 'nc' not in _PROG:
        _PROG['nc'] = build_fused_program()
    nc = _PROG['nc']
    shared = host_const_inputs()
    shared['gfin'] = np.ascontiguousarray(inp['final_norm_g'])
    for L in range(4):
        for k, v in host_layer_inputs(L, inp).items():
            shared['L%d_%s' % (L, k)] = v
    in_maps = []
    for c in range(8):
        m = dict(shared)
        m['x'] = np.ascontiguousarray(x[c % 4])
        in_maps.append(m)
    res = run_bass_kernel_spmd(nc, in_maps, core_ids=list(range(8)))
    return np.stack([res.results[c]['out'] for c in range(4)], axis=0)
```

```python
import numpy as np
from contextlib import ExitStack
import ml_dtypes
import concourse.bass as bass
import concourse.mybir as mybir
from concourse.bass_utils import run_bass_kernel_spmd

F32 = mybir.dt.float32
BF16 = mybir.dt.bfloat16
U32 = mybir.dt.uint32
ALU = mybir.AluOpType
AF = mybir.ActivationFunctionType
AX = mybir.AxisListType

D = 1024
NT = 2048
SEQ = 4096
EPS = 1e-6
ENGS = ['pe', 'act', 'dve', 'pool', 'sp']
NRING = 12
SB_BASE = 16512
SB_TOP = 229344
NEGM = -30000.0


class Sched:
    def __init__(self, nc, stack, self_sync=True):
        self.nc = nc
        self.sem = {e: stack.enter_context(nc.semaphore('c_' + e)) for e in ENGS}
        self.ring = [stack.enter_context(nc.semaphore('d_%d' % i)) for i in range(NRING)]
        self.cnt = {e: 0 for e in ENGS}
        self.ndma = 0
        self.ops = {e: [] for e in ENGS}
        self.seen = {e: {} for e in ENGS}
        self.lastw = {}
        self.readers = {}
        self.self_sync = self_sync
        self.n_instr = 0
        self.bgsem = stack.enter_context(nc.semaphore('bg'))
        self.bgcnt = 0

    def new_epoch(self, stack):
        self.barrier()
        nc = self.nc
        self.epoch = getattr(self, 'epoch', 0) + 1
        self.sem = {e: stack.enter_context(nc.semaphore('c%d_%s' % (self.epoch, e))) for e in ENGS}
        self.ring = [stack.enter_context(nc.semaphore('d%d_%d' % (self.epoch, i))) for i in range(NRING)]
        self.cnt = {e: 0 for e in ENGS}
        self.ndma = 0
        self.seen = {e: {} for e in ENGS}

    def _tok_wait(self, tok):
        if tok[0] == 'c':
            return (('c', tok[1]), self.sem[tok[1]], tok[2])
        n = tok[1]
        return (('d', n % NRING), self.ring[n % NRING], 16 * (n // NRING + 1))

    def _collect(self, eng, reads, writes):
        need = {}
        toks = []
        for r in reads:
            t = self.lastw.get(r)
            if t is not None:
                toks.append(t)
        for w in writes:
            t = self.lastw.get(w)
            if t is not None:
                toks.append(t)
            toks.extend(self.readers.get(w, ()))
        for t in toks:
            if t[0] == 'c' and t[1] == eng:
                if eng == 'pe' or not self.self_sync:
                    continue
            key, sem, val = self._tok_wait(t)
            if self.seen[eng].get(key, 0) >= val:
                continue
            if key not in need or need[key][1] < val:
                need[key] = (sem, val)
        for key, (sem, val) in need.items():
            self.seen[eng][key] = val
        return list(need.values())

    def _commit(self, tok, reads, writes):
        for r in reads:
            self.readers.setdefault(r, []).append(tok)
        for w in writes:
            self.lastw[w] = tok
            self.readers[w] = []

    def op(self, eng, fn, reads=(), writes=()):
        waits = self._collect(eng, reads, writes)
        self.cnt[eng] += 1
        idx = self.cnt[eng]
        sem = self.sem[eng]

        def emit(e):
            for (s, v) in waits:
                e.wait_ge(s, v)
            fn(e).then_inc(sem, 1)
        self.ops[eng].append(emit)
        self.n_instr += 1 + len(waits)
        tok = ('c', eng, idx)
        self._commit(tok, reads, writes)
        return tok

    def dma(self, out, in_, reads=(), writes=(), q='sp', **kw):
        return self.dmalike(q, lambda e: e.dma_start(out=out, in_=in_, **kw), reads, writes)

    def dmalike(self, q, fn, reads=(), writes=()):
        n = self.ndma
        self.ndma += 1
        waits = self._collect(q, reads, writes)
        if n >= NRING:
            key, sem, val = self._tok_wait(('d', n - NRING))
            if self.seen[q].get(key, 0) < val:
                self.seen[q][key] = val
                waits.append((sem, val))
        rs = self.ring[n % NRING]

        def emit(e):
            for (s, v) in waits:
                e.wait_ge(s, v)
            fn(e).then_inc(rs, 16)
        self.ops[q].append(emit)
        self.n_instr += 1 + len(waits)
        tok = ('d', n)
        self._commit(tok, reads, writes)
        return tok

    def bg_dma(self, q, fn):
        self.bgcnt += 1
        sem = self.bgsem
        self.ops[q].append(lambda e: fn(e).then_inc(sem, 16))
        self.n_instr += 1

    def bg_wait(self, engs):
        v = 16 * self.bgcnt
        sem = self.bgsem
        for q in engs:
            self.ops[q].append(lambda e: e.wait_ge(sem, v))

    def barrier(self):
        for e in ENGS:
            waits = []
            for o in ENGS:
                if o == e or self.cnt[o] == 0:
                    continue
                key = ('c', o)
                if self.seen[e].get(key, 0) < self.cnt[o]:
                    self.seen[e][key] = self.cnt[o]
                    waits.append((self.sem[o], self.cnt[o]))
            for n in range(max(0, self.ndma - NRING), self.ndma):
                key, sem, val = self._tok_wait(('d', n))
                if self.seen[e].get(key, 0) < val:
                    self.seen[e][key] = val
                    waits.append((sem, val))
            if waits:
                def emit(en, waits=waits):
                    for (s, v) in waits:
                        en.wait_ge(s, v)
                self.ops[e].append(emit)
                self.n_instr += len(waits)
        self.lastw.clear()
        self.readers.clear()

    def emit_all(self):
        self.barrier()
        nc = self.nc
        ops = self.ops
        with nc.Block() as block:
            @block.tensor
            def _(e):
                for f in ops['pe']:
                    f(e)

            @block.scalar
            def _(e):
                for f in ops['act']:
                    f(e)

            @block.vector
            def _(e):
                for f in ops['dve']:
                    f(e)

            @block.gpsimd
            def _(e):
                for f in ops['pool']:
                    f(e)

            @block.sync
            def _(e):
                for f in ops['sp']:
                    f(e)


_DT_SIZE = {F32: 4, BF16: 2, U32: 4}


class Bld:
    def __init__(self, nc, st):
        self.nc = nc
        self.S = Sched(nc, st)
        self.banks = [st.enter_context(nc.psum_tensor("pb%d" % i, [128, 512], F32)) for i in range(7)]
        self.bankb = st.enter_context(nc.psum_tensor("pb7", [128, 1024], BF16))
        self.p = SB_BASE
        self.nalloc = 0
        self.rot = 0

    def sb(self, shape, dt, name):
        size = _DT_SIZE[dt]
        for s in shape[1:]:
            size *= s
        off = (self.p + 63) // 64 * 64
        assert off + size <= SB_TOP, ("SBUF overflow", name, off + size - SB_TOP)
        self.p = off + size
        self.nalloc += 1
        return self.nc.alloc_sbuf_tensor_at("%s_%d" % (name, self.nalloc), list(shape), dt, offset=off)

    def mark(self):
        return self.p

    def reset(self, m):
        self.S.barrier()
        self.p = m

    def nb(self, lst):
        i = lst[self.rot % len(lst)]
        self.rot += 1
        return self.banks[i], 'pb%d' % i

    def mm(self, out, lhsT, rhs, start, stop, r, w):
        self.S.op('pe', lambda e: e.matmul(out, lhsT=lhsT, rhs=rhs, start=start, stop=stop), r, w)

    def tr(self, out, in_, ident, r, w):
        self.S.op('pe', lambda e: e.transpose(out=out, in_=in_, identity=ident), r, w)

    def act(self, out, in_, func, r, w, **kw):
        self.S.op('act', lambda e: e.activation(out=out, in_=in_, func=func, **kw), r, w)

    def tt(self, eng, out, in0, in1, op, r, w):
        self.S.op(eng, lambda e: e.tensor_tensor(out=out, in0=in0, in1=in1, op=op), r, w)

    def ts(self, eng, out, in0, s1, s2, op0, op1, r, w):
        if op1 is None:
            self.S.op(eng, lambda e: e.tensor_scalar(out=out, in0=in0, scalar1=s1, scalar2=None, op0=op0), r, w)
        else:
            self.S.op(eng, lambda e: e.tensor_scalar(out=out, in0=in0, scalar1=s1, scalar2=s2, op0=op0, op1=op1), r, w)

    def stt(self, out, in0, scalar, in1, op0, op1, r, w):
        self.S.op('dve', lambda e: e.scalar_tensor_tensor(out=out, in0=in0, scalar=scalar, in1=in1, op0=op0, op1=op1), r, w)

    def cp(self, eng, out, in_, r, w):
        if eng == 'act':
            self.S.op('act', lambda e: e.copy(out=out, in_=in_), r, w)
        else:
            self.S.op(eng, lambda e: e.tensor_copy(out=out, in_=in_), r, w)

    def recip(self, out, in_, r, w):
        self.S.op('dve', lambda e: e.reciprocal(out=out, in_=in_), r, w)

    def consts(self):
        S = self.S
        self.identf = self.sb([128, 128], F32, 'identf')
        self.identb = self.sb([128, 128], BF16, 'identb')
        self.iota = self.sb([128, 128], F32, 'iota')
        self.onesf = self.sb([128, 128], F32, 'onesf')
        self.onesb = self.sb([128, 128], BF16, 'onesb')
        self.iotab = self.sb([128, 128], BF16, 'iotab')
        S.op('pool', lambda e: e.iota(self.iota[:], pattern=[[1, 128]], base=0, channel_multiplier=0,
                                      allow_small_or_imprecise_dtypes=True), (), ['iota'])
        S.op('pool', lambda e: e.iota(self.identf[:], pattern=[[1, 128]], base=0, channel_multiplier=-1,
                                      allow_small_or_imprecise_dtypes=True), (), ['identf'])
        self.ts('dve', self.identf[:], self.identf[:], 0.0, None, ALU.is_equal, None, ['identf'], ['identf'])
        self.cp('dve', self.identb[:], self.identf[:], ['identf'], ['identb'])
        self.cp('dve', self.iotab[:], self.iota[:], ['iota'], ['iotab'])
        S.op('pool', lambda e: e.memset(self.onesf[:], 1.0), (), ['onesf'])
        S.op('pool', lambda e: e.memset(self.onesb[:], 1.0), (), ['onesb'])


def rms_rstd(b, xt, xkey, ssq, sd, rstd, col, junk, junkkey, tag):
    b.act(junk, xt, AF.Square, [xkey], [junkkey, tag + 'ssq'], accum_out=ssq[:, col:col + 1])
    b.act(sd[:, col:col + 1], ssq[:, col:col + 1], AF.Sqrt, [tag + 'ssq'], [tag + 'sd'], scale=1.0 / D, bias=EPS)
    b.recip(rstd[:, col:col + 1], sd[:, col:col + 1], [tag + 'sd'], [tag + 'rstd'])


def row_tiles(ap, n):
    return [ap[t * 128:(t + 1) * 128, :] for t in range(n)]


def norm_tiles_bf16(b, src_tiles, gcol, hT, hkey, W):
    S = b.S
    for tt in range(len(src_tiles)):
        sl = W['xslot']
        W['xslot'] ^= 1
        xt = W['xt'][sl]
        xn = W['xn'][sl]
        S.dma(xt[:], src_tiles[tt], (), ['xt%d' % sl])
        rms_rstd(b, xt[:], 'xt%d' % sl, W['ssq'], W['sd'], W['rstd'], sl, xn[:], 'xn%d' % sl, 'n%d' % sl)
        b.act(xn[:], xt[:], AF.Copy, ['xt%d' % sl, 'n%drstd' % sl], ['xn%d' % sl], scale=W['rstd'][:, sl:sl + 1])
        for c in range(8):
            b.tr(b.bankb[:, c * 128:(c + 1) * 128], xn[:, c * 128:(c + 1) * 128], b.identb[:], ['xn%d' % sl, 'identb'], ['pb7'])
        b.tt('dve', hT[:, :, tt * 128:(tt + 1) * 128], b.bankb[:].rearrange("p (c t) -> p c t", c=8),
             gcol.unsqueeze(2).to_broadcast([128, 8, 128]), ALU.mult, ['pb7', 'gcol'], [hkey])


def proj_fm(b, bank, bkey, M, wT, wkey, col0, hT, hkey, t0, ntok):
    for kc in range(8):
        b.mm(bank[0:M, 0:ntok], wT[:, kc, col0:col0 + M], hT[:, kc, t0:t0 + ntok], kc == 0, kc == 7, [wkey, hkey], [bkey])


def load_cast(b, dst, dkey, src, shape, stage, skey, eng='pool'):
    b.S.dma(stage, src, (), [skey])
    b.cp(eng, dst, stage, [skey], [dkey])


TB = 256
JG = 2


def emit_peer_prep(b, d):
    S = b.S
    m = b.mark()
    st = [b.sb([128, JG, 1024], F32, 'pst%d' % i) for i in range(4)]
    ob = [b.sb([128, JG, 1024], BF16, 'pob%d' % i) for i in range(4)]
    n = 0
    for g in range(128 // JG):
        for (src, dst, eng) in ((d['uth'], d['uts'], 'act'), (d['vbh'], d['vbs'], 'pool')):
            sl = n % 4
            n += 1
            S.dma(st[sl][:], src[g * JG:(g + 1) * JG].rearrange("j p f -> p j f"), (), ['pst%d' % sl])
            b.cp(eng, ob[sl][:], st[sl][:], ['pst%d' % sl], ['pob%d' % sl])
            S.dma(dst[g], ob[sl][:], ['pob%d' % sl], ())
    b.reset(m)


def prep_dma_list(b, d):
    lst = []
    for g in range(128 // JG):
        for (src, dst) in ((d['uth'], d['uts']), (d['vbh'], d['vbs'])):
            lst.append(lambda e, src=src, dst=dst, g=g: e.dma_start(out=dst[g], in_=src[g * JG:(g + 1) * JG].rearrange("j p f -> p j f")))
    return lst


def emit_peer(b, d, last, x1, xout, ntok):
    S = b.S
    m = b.mark()
    B = b.banks
    NTT = TB // 128
    HX = B[2]
    HY = b.bankb[:].bitcast(F32)
    wq = b.sb([128, 8, 1024], F32, 'wq')
    skbd = b.sb([128, 8, 256], F32, 'skbd')
    gffn = b.sb([128, 8], F32, 'gffn')
    S.dma(wq[:], d['wq'], (), ['wq'])
    S.dma(skbd[:], d['skbd'], (), ['skbd'])
    S.dma(gffn[:], d['gffn'], (), ['gffn'])
    if last:
        gfin = b.sb([128, 1024], F32, 'gfin')
        S.dma(gfin[:], d['gfin'].partition_broadcast(128), (), ['gfin'])
    xt = [[b.sb([128, 1024], F32, 'pxt%d_%d' % (s_, i)) for i in range(NTT)] for s_ in range(2)]
    xn = b.sb([128, 1024], F32, 'pxn')
    ssq = b.sb([128, 2], F32, 'pssq')
    sd = b.sb([128, 2], F32, 'psd')
    rstd = b.sb([128, 2], F32, 'prstd')
    h2T = b.sb([128, 8, TB], F32, 'h2T')
    h2Tb = [b.sb([128, 8, TB], BF16, 'h2Tb%d' % i) for i in range(2)]
    qh = [b.sb([128, TB], F32, 'qh%d' % i) for i in range(2)]
    ssb = [b.sb([128, 8, 2, 128], F32, 'ssb%d' % i) for i in range(NTT)]
    v16 = b.sb([128, 8, 2, 16], F32, 'v16')
    i16 = b.sb([128, 8, 2, 16], U32, 'i16')
    i16f = b.sb([128, 8, 2, 16], F32, 'i16f')
    cand = b.sb([128, 8, 16, 16], F32, 'cand')
    fv = b.sb([128, 8, 16], F32, 'fv')
    fi = b.sb([128, 8, 16], U32, 'fi')
    fa = b.sb([128, 8, 16], U32, 'fa')
    fb = b.sb([128, 8, 16], U32, 'fb')
    faf = b.sb([128, 8, 16], F32, 'faf')
    fbf = b.sb([128, 8, 16], F32, 'fbf')
    gate = b.sb([128, 8, 16], F32, 'gate')
    gsum = b.sb([128, 8], F32, 'gsum')
    i12 = b.sb([128, 2, 8, 16], F32, 'i12')
    T3 = b.sb([128, 3, TB], F32, 'T3')
    Pt = [b.sb([128, 128], BF16, 'Pt%d' % i) for i in range(4)]
    Qt = [b.sb([128, 128], BF16, 'Qt%d' % i) for i in range(4)]
    Gs = b.sb([128, TB, 128], BF16, 'Gs')
    ut = [b.sb([128, 8, 128], BF16, 'ut%d' % i) for i in range(3)]
    vb = [b.sb([128, 1024], BF16, 'vb%d' % i) for i in range(3)]
    gl = [b.sb([128, TB], F32, 'gl%d' % i) for i in range(2)]
    wT = [b.sb([128, TB], BF16, 'wT%d' % i) for i in range(2)]
    x2 = b.sb([128, 1024], F32, 'x2')
    iota16 = b.iota[:, 0:16]
    print('PEER sbuf end', b.p, SB_TOP)
    nblk = ntok // TB

    def h_phase(blk):
        L = []
        st_ = blk % 2
        for tt in range(NTT):
            T = blk * NTT + tt
            xk = 'pxt%d_%d' % (st_, tt)
            x_ = xt[st_][tt]
            L.append(lambda x_=x_, T=T, xk=xk: S.dma(x_[:], x1[T * 128:(T + 1) * 128, :], (), [xk]))
            L.append(lambda x_=x_, xk=xk: b.act(xn[:], x_[:], AF.Square, [xk], ['pxn', 'pnssq'], accum_out=ssq[:, 0:1]))
            L.append(lambda: b.act(sd[:, 0:1], ssq[:, 0:1], AF.Sqrt, ['pnssq'], ['pnsd'], scale=1.0 / D, bias=EPS))
            L.append(lambda: b.recip(rstd[:, 0:1], sd[:, 0:1], ['pnsd'], ['pnrstd']))
            L.append(lambda x_=x_, xk=xk: b.act(xn[:], x_[:], AF.Copy, [xk, 'pnrstd'], ['pxn'], scale=rstd[:, 0:1]))
            for hb in range(2):
                bank, bk = (HX, 'pb2') if hb == 0 else (HY, 'pb7')
                for c4 in range(4):
                    c = hb * 4 + c4
                    L.append(lambda bank=bank, bk=bk, c=c, c4=c4: b.tr(bank[:, c4 * 128:(c4 + 1) * 128], xn[:, c * 128:(c + 1) * 128], b.identf[:], ['pxn', 'identf'], [bk]))
                L.append(lambda bank=bank, bk=bk, hb=hb, tt=tt: b.tt('dve', h2T[:, hb * 4:(hb + 1) * 4, tt * 128:(tt + 1) * 128], bank[:, :].rearrange("p (c t) -> p c t", c=4),
                                                              gffn[:, hb * 4:(hb + 1) * 4].unsqueeze(2).to_broadcast([128, 4, 128]), ALU.mult, [bk, 'gffn'], ['h2T']))
        hb_ = h2Tb[st_]
        hbk = 'h2Tb%d' % st_
        L.append(lambda: b.cp('pool', hb_[:], h2T[:], ['h2T'], [hbk]))
        for h in range(8):
            for kc in range(8):
                L.append(lambda h=h, kc=kc: b.mm(HX[:, 0:TB], wq[:, kc, h * 128:(h + 1) * 128], h2T[:, kc, :], kc == 0, kc == 7, ['wq', 'h2T'], ['pb2']))
            q = qh[h % 2]
            qk = 'qh%d' % (h % 2)
            L.append(lambda q=q, qk=qk: b.cp('act', q[:], HX[:, 0:TB], ['pb2'], [qk]))
            for tt in range(NTT):
                L.append(lambda q=q, qk=qk, tt=tt, h=h: b.mm(HY[:, tt * 256:(tt + 1) * 256], q[:, tt * 128:(tt + 1) * 128], skbd[:, h, :], True, True, [qk, 'skbd'], ['pb7']))
            for tt in range(NTT):
                L.append(lambda tt=tt, h=h: b.cp('act', ssb[tt][:, h, :, :], HY[:, tt * 256:(tt + 1) * 256].rearrange("p (s k) -> p s k", s=2), ['pb7'],
                                                 ['ss%d_%d_0' % (tt, h), 'ss%d_%d_1' % (tt, h)]))
        for tt in range(NTT):
            HP = [(h, p) for h in range(8) for p in range(2)]
            sk = lambda h, p, tt=tt: 'ss%d_%d_%d' % (tt, h, p)
            for (h, p) in HP:
                L.append(lambda h=h, p=p, tt=tt: S.op('dve', lambda e: e.max(out=v16[:, h, p, 0:8], in_=ssb[tt][:, h, p, :]), [sk(h, p, tt)], ['va%d_%d' % (h, p)]))
            for (h, p) in HP:
                L.append(lambda h=h, p=p, tt=tt: S.op('dve', lambda e: e.max_index(out=i16[:, h, p, 0:8], in_max=v16[:, h, p, 0:8], in_values=ssb[tt][:, h, p, :]),
                                                      [sk(h, p, tt), 'va%d_%d' % (h, p)], ['ia%d_%d' % (h, p)]))
            for (h, p) in HP:
                L.append(lambda h=h, p=p, tt=tt: S.op('dve', lambda e: e.match_replace(out=ssb[tt][:, h, p, :], in_to_replace=v16[:, h, p, 0:8], in_values=ssb[tt][:, h, p, :], imm_value=-1e30),
                                                      ['va%d_%d' % (h, p)], [sk(h, p, tt)]))
            for (h, p) in HP:
                L.append(lambda h=h, p=p, tt=tt: S.op('dve', lambda e: e.max(out=v16[:, h, p, 8:16], in_=ssb[tt][:, h, p, :]), [sk(h, p, tt)], ['vb%d_%d' % (h, p)]))
            for (h, p) in HP:
                L.append(lambda h=h, p=p, tt=tt: S.op('dve', lambda e: e.max_index(out=i16[:, h, p, 8:16], in_max=v16[:, h, p, 8:16], in_values=ssb[tt][:, h, p, :]),
                                                      [sk(h, p, tt), 'vb%d_%d' % (h, p)], ['ib%d_%d' % (h, p)]))
            VK = ['va%d_%d' % hp for hp in HP] + ['vb%d_%d' % hp for hp in HP]
            IK = ['ia%d_%d' % hp for hp in HP] + ['ib%d_%d' % hp for hp in HP]
            CK = ['cand%d' % h for h in range(8)]
            FVK = ['fva%d' % h for h in range(8)] + ['fvb%d' % h for h in range(8)]
            FIK = ['fia%d' % h for h in range(8)] + ['fib%d' % h for h in range(8)]
            L.append(lambda IK=IK: b.cp('dve', i16f[:], i16[:], IK, ['i16f']))
            L.append(lambda VK=VK, CK=CK: b.tt('dve', cand[:], v16[:, :, 0, :].unsqueeze(3).to_broadcast([128, 8, 16, 16]),
                                               v16[:, :, 1, :].unsqueeze(2).to_broadcast([128, 8, 16, 16]), ALU.add, VK, CK))
            cs = lambda h: cand[:, h, :, :].rearrange("p a b -> p (a b)")
            for h in range(8):
                L.append(lambda h=h: S.op('dve', lambda e: e.max(out=fv[:, h, 0:8], in_=cs(h)), ['cand%d' % h], ['fva%d' % h]))
            for h in range(8):
                L.append(lambda h=h: S.op('dve', lambda e: e.max_index(out=fi[:, h, 0:8], in_max=fv[:, h, 0:8], in_values=cs(h)), ['cand%d' % h, 'fva%d' % h], ['fia%d' % h]))
            for h in range(8):
                L.append(lambda h=h: S.op('dve', lambda e: e.match_replace(out=cs(h), in_to_replace=fv[:, h, 0:8], in_values=cs(h), imm_value=-1e30), ['fva%d' % h, 'cand%d' % h], ['cand%d' % h]))
            for h in range(8):
                L.append(lambda h=h: S.op('dve', lambda e: e.max(out=fv[:, h, 8:16], in_=cs(h)), ['cand%d' % h], ['fvb%d' % h]))
            for h in range(8):
                L.append(lambda h=h: S.op('dve', lambda e: e.max_index(out=fi[:, h, 8:16], in_max=fv[:, h, 8:16], in_values=cs(h)), ['cand%d' % h, 'fvb%d' % h], ['fib%d' % h]))
            L.append(lambda FVK=FVK: b.tt('dve', gate[:], fv[:], fv[:, :, 0:1].to_broadcast([128, 8, 16]), ALU.subtract, FVK, ['gate']))
            L.append(lambda: b.act(gate[:], gate[:], AF.Exp, ['gate'], ['gate']))
            L.append(lambda: S.op('dve', lambda e: e.tensor_reduce(out=gsum[:], in_=gate[:], axis=AX.X, op=ALU.add), ['gate'], ['gsum']))
            L.append(lambda: b.recip(gsum[:], gsum[:], ['gsum'], ['gsum']))
            L.append(lambda: b.tt('dve', gate[:], gate[:], gsum[:].unsqueeze(2).to_broadcast([128, 8, 16]), ALU.mult, ['gate', 'gsum'], ['gate']))
            L.append(lambda FIK=FIK: S.op('dve', lambda e: e.tensor_single_scalar(out=fa[:], in_=fi[:], scalar=4, op=ALU.logical_shift_right), FIK, ['fa']))
            L.append(lambda FIK=FIK: S.op('dve', lambda e: e.tensor_single_scalar(out=fb[:], in_=fi[:], scalar=15, op=ALU.bitwise_and), FIK, ['fb']))
            L.append(lambda: b.cp('dve', faf[:], fa[:], ['fa'], ['faf']))
            L.append(lambda: b.cp('dve', fbf[:], fb[:], ['fb'], ['fbf']))
            for p, pos in ((0, faf), (1, fbf)):
                L.append(lambda pos=pos, CK=CK: b.tt('dve', cand[:].rearrange("p h k a -> p (h k) a"), pos[:].rearrange("p h k -> p (h k)").unsqueeze(2).to_broadcast([128, 128, 16]),
                                                     iota16.unsqueeze(1).to_broadcast([128, 128, 16]), ALU.is_equal, ['faf', 'fbf'] + CK, CK))
                L.append(lambda p=p, CK=CK: b.tt('dve', cand[:], cand[:], i16f[:, :, p, :].unsqueeze(2).to_broadcast([128, 8, 16, 16]), ALU.mult, CK + ['i16f'], CK))
                L.append(lambda p=p, CK=CK: S.op('dve', lambda e: e.tensor_reduce(out=i12[:, p, :, :], in_=cand[:], axis=AX.X, op=ALU.add), CK, ['i12']))
            L.append(lambda: b.tr(HX[:, 0:128], i12[:, 0, :, :].rearrange("p h k -> p (h k)"), b.identf[:], ['i12', 'identf'], ['pb2']))
            L.append(lambda: b.tr(HX[:, 128:256], i12[:, 1, :, :].rearrange("p h k -> p (h k)"), b.identf[:], ['i12', 'identf'], ['pb2']))
            L.append(lambda: b.tr(HX[:, 256:384], gate[:].rearrange("p h k -> p (h k)"), b.identf[:], ['gate', 'identf'], ['pb2']))
            L.append(lambda tt=tt: b.cp('act', T3[:, :, tt * 128:(tt + 1) * 128], HX[:, 0:384].rearrange("p (a t) -> p a t", a=3), ['pb2'], ['T3']))
        return L

    def g_build():
        for t in range(TB):
            sl = t % 4
            b.ts('dve', Pt[sl][:], b.iotab[:], T3[:, 0, t:t + 1], T3[:, 2, t:t + 1], ALU.is_equal, ALU.mult, ['iotab', 'T3'], ['Pt%d' % sl])
            b.ts('dve', Qt[sl][:], b.iotab[:], T3[:, 1, t:t + 1], None, ALU.is_equal, None, ['iotab', 'T3'], ['Qt%d' % sl])
            bi = (t // 4) % 2
            b.mm(B[bi][:, (t % 4) * 128:(t % 4 + 1) * 128], Pt[sl][:], Qt[sl][:], True, True, ['Pt%d' % sl, 'Qt%d' % sl], ['pb%d' % bi])
            if t % 4 == 3:
                b.cp('act', Gs[:, t - 3:t + 1, :], B[bi][:].rearrange("p (t j) -> p t j", t=4), ['pb%d' % bi], ['Gs'])

    def main_loop(blk, inter):
        st_ = blk % 2
        hb_ = h2Tb[st_]
        hbk = 'h2Tb%d' % st_
        ni = len(inter)
        done = 0
        def a_T(j):
            sl = j % 3
            S.dma(ut[sl][:].rearrange("p c i -> p (c i)"), d['uts'][j // JG][:, j % JG, :], (), ['ut%d' % sl])
            S.dma(vb[sl][:], d['vbs'][j // JG][:, j % JG, :], (), ['vb%d' % sl])
            ba, ka = B[j % 2], 'pb%d' % (j % 2)
            for dc in range(8):
                b.mm(ba[:, 0:TB], ut[sl][:, dc, :], hb_[:, dc, :], dc == 0, dc == 7, ['ut%d' % sl, hbk], [ka])
        a_T(0)
        for j in range(128):
            sl = j % 3
            ba, ka = B[j % 2], 'pb%d' % (j % 2)
            w = j % 2
            b.act(gl[w][:], ba[:, 0:TB], AF.Gelu_apprx_tanh, [ka], ['gl%d' % w])
            if j + 1 < 128:
                a_T(j + 1)
            b.tt('dve', wT[w][:], gl[w][:], Gs[:, :, j], ALU.mult, ['gl%d' % w, 'Gs'], ['wT%d' % w])
            for tt in range(NTT):
                for hh in range(2):
                    bo = 3 + tt * 2 + hh
                    b.mm(B[bo][:, :], wT[w][:, tt * 128:(tt + 1) * 128], vb[sl][:, hh * 512:(hh + 1) * 512], j == 0, j == 127,
                         ['wT%d' % w, 'vb%d' % sl], ['pb%d' % bo])
            tgt = (ni * (j + 1) + 119) // 120 if j < 120 else ni
            tgt = min(tgt, ni)
            while done < tgt:
                inter[done]()
                done += 1

    def epilogue(blk):
        st_ = blk % 2
        for tt in range(NTT):
            T = blk * NTT + tt
            xk = 'pxt%d_%d' % (st_, tt)
            for hh in range(2):
                bo = 3 + tt * 2 + hh
                b.tt('dve', x2[:, hh * 512:(hh + 1) * 512], B[bo][:, :], xt[st_][tt][:, hh * 512:(hh + 1) * 512], ALU.add,
                     ['pb%d' % bo, xk], ['x2'])
            if last:
                y2 = xt[st_][tt]
                rms_rstd(b, x2[:], 'x2', ssq, sd, rstd, 1, y2[:], xk, 'fn')
                b.act(y2[:], x2[:], AF.Copy, ['x2', 'fnrstd'], [xk], scale=rstd[:, 1:2])
                b.tt('pool', y2[:], y2[:], gfin[:], ALU.mult, [xk, 'gfin'], [xk])
                S.dma(xout[T * 128:(T + 1) * 128, :], y2[:], [xk], ())
            else:
                S.dma(xout[T * 128:(T + 1) * 128, :], x2[:], ['x2'], ())

    for f in h_phase(0):
        f()
    S.bg_wait(['sp'])
    import os
    dbg = os.environ.get('PEER_DBG', '')
    for blk in range(nblk):
        if 'nog' not in dbg or blk == 0:
            g_build()
        inter = h_phase(blk + 1) if (blk + 1 < nblk and 'noh' not in dbg) else []
        main_loop(blk, inter)
        epilogue(blk)
    b.reset(m)


def mixer_common_alloc(b, d, nctx_cols, rope_src):
    S = b.S
    W = {'xslot': 0}
    W['xt'] = [b.sb([128, 1024], F32, 'mxt%d' % i) for i in range(2)]
    W['xn'] = [b.sb([128, 1024], BF16, 'mxn%d' % i) for i in range(2)]
    W['ssq'] = b.sb([128, 2], F32, 'mssq')
    W['sd'] = b.sb([128, 2], F32, 'msd')
    W['rstd'] = b.sb([128, 2], F32, 'mrstd')
    W['gcol'] = b.sb([128, 8], F32, 'gcol')
    S.dma(W['gcol'][:], d['gmix'], (), ['gcol'])
    W['win'] = b.sb([128, 8, 1792], BF16, 'win')
    W['wpm'] = b.sb([128, 8, 640], BF16, 'wpm')
    wst = b.sb([128, 1792], F32, 'wst')
    for kc in range(8):
        S.dma(wst[:], d['w_in'][:, kc, :], (), ['wst'])
        b.cp('pool', W['win'][:, kc, :], wst[:], ['wst'], ['win'])
        S.dma(wst[:, 0:640], d['w_perm'][:, kc, :], (), ['wst'])
        b.cp('pool', W['wpm'][:, kc, :], wst[:, 0:640], ['wst'], ['wpm'])
    W['rope'] = b.sb([64, 2, nctx_cols], F32, 'rope')
    if rope_src is not None:
        S.dma(W['rope'][:], rope_src, (), ['rope'])
    W['hT'] = [b.sb([128, 8, 512], BF16, 'hT%d' % i) for i in range(2)]
    W['t1'] = [b.sb([128, 512], F32, 't1_%d' % i) for i in range(2)]
    W['t2'] = [b.sb([128, 512], F32, 't2_%d' % i) for i in range(2)]
    W['hs'] = 0
    return W


def out_proj(b, d, c, catA, catH):
    S = b.S
    B = b.banks
    woA = b.sb([128, 4, 1024], BF16, 'woA')
    woH = b.sb([64, 8, 1024], BF16, 'woH')
    wst = b.sb([128, 2, 1024], F32, 'wost')
    for c2_ in range(2):
        S.dma(wst[:], d['w_outA'][:, c2_ * 2:(c2_ + 1) * 2, :], (), ['wost'])
        b.cp('pool', woA[:, c2_ * 2:(c2_ + 1) * 2, :], wst[:], ['wost'], ['woA'])
    for c4 in range(4):
        S.dma(wst[0:64], d['w_outH'][:, c4 * 2:(c4 + 1) * 2, :], (), ['wost'])
        b.cp('pool', woH[:, c4 * 2:(c4 + 1) * 2, :], wst[0:64], ['wost'], ['woH'])
    xt = [b.sb([128, 1024], F32, 'oxt%d' % i) for i in range(2)]
    xo = [b.sb([128, 1024], F32, 'oxo%d' % i) for i in range(2)]
    for T in range(NT // 128):
        sl = T % 2
        S.dma(xt[sl][:], c['x_own'][T * 128:(T + 1) * 128, :], (), ['oxt%d' % sl])
        for hh in range(2):
            bo, ko = b.nb([0, 1, 2, 3])
            n = 0
            for cc in range(4):
                b.mm(bo[:, :], catA[:, cc, T * 128:(T + 1) * 128], woA[:, cc, hh * 512:(hh + 1) * 512], n == 0, False, ['catA', 'woA'], [ko])
                n += 1
            for h in range(8):
                b.mm(bo[:, :], catH[:, h, T * 128:(T + 1) * 128], woH[:, h, hh * 512:(hh + 1) * 512], False, h == 7, ['catH', 'woH'], [ko])
            b.tt('dve', xo[sl][:, hh * 512:(hh + 1) * 512], bo[:, :], xt[sl][:, hh * 512:(hh + 1) * 512], ALU.add, [ko, 'oxt%d' % sl], ['oxo%d' % sl])
        S.dma(c['x1'][T * 128:(T + 1) * 128, :], xo[sl][:], ['oxo%d' % sl], ())


def rope_epilogue(b, W, pq, kq, pp, kp, c0, n, out, okey, rstd=None, rkey=None):
    i = W['hs']
    W['hs'] ^= 1
    t1 = W['t1'][i]
    t2 = W['t2'][i]
    b.tt('dve', t1[0:64, 0:n], pq[0:64, 0:n], W['rope'][:, 0, c0:c0 + n], ALU.mult, [kq, 'rope'], ['t1_%d' % i])
    b.tt('dve', t2[0:64, 0:n], pp[0:64, 0:n], W['rope'][:, 1, c0:c0 + n], ALU.mult, [kp, 'rope'], ['t2_%d' % i])
    if rstd is None:
        b.tt('pool', out, t1[0:64, 0:n], t2[0:64, 0:n], ALU.add, ['t1_%d' % i, 't2_%d' % i], [okey])
    else:
        b.tt('pool', t1[0:64, 0:n], t1[0:64, 0:n], t2[0:64, 0:n], ALU.add, ['t1_%d' % i, 't2_%d' % i], ['t1_%d' % i])
        b.tt('pool', out, t1[0:64, 0:n], rstd, ALU.mult, ['t1_%d' % i, rkey], [okey])


def emit_mixer_even(b, d, c):
    S = b.S
    B = b.banks
    m0 = b.mark()
    aT = b.sb([128, 4, NT + 30], F32, 'aT')
    qT = b.sb([64, 8, NT], BF16, 'qT')
    kT = b.sb([64, 2, NT + 256], BF16, 'kT')
    Vt = b.sb([128, 18, 128], BF16, 'Vt')
    m1 = b.mark()
    W = mixer_common_alloc(b, d, NT + 256, c['rope'])
    sig = [b.sb([128, 512], F32, 'sig%d' % i) for i in range(2)]
    atmp = b.sb([128, 256], F32, 'atmp')
    win, wpm = W['win'], W['wpm']

    def kproj(hT, hk, ntok, tcol, kcol):
        for j in range(2):
            pq, kq = b.nb([0, 1, 2, 3, 4, 5])
            proj_fm(b, pq, kq, 64, win, 'win', 1536 + j * 64, hT, hk, 0, ntok)
            pp, kp = b.nb([0, 1, 2, 3, 4, 5])
            proj_fm(b, pp, kp, 64, wpm, 'wpm', 512 + j * 64, hT, hk, 0, ntok)
            rope_epilogue(b, W, pq, kq, pp, kp, tcol, ntok, kT[:, j, kcol:kcol + ntok], 'kT')

    def vproj(hT, hk, ntiles, vt0):
        for tt in range(ntiles):
            for kc in range(8):
                b.mm(B[6][:, 0:128], hT[:, kc, tt * 128:(tt + 1) * 128], win[:, kc, 1664:1792], kc == 0, kc == 7, ['win', hk], ['pb6'])
            b.cp('act', Vt[:, vt0 + tt, :], B[6][:, 0:128], ['pb6'], ['Vt'])

    hT = W['hT'][0]
    norm_tiles_bf16(b, [c['ctxL'], c['ctxR']], W['gcol'][:], hT, 'hT0', W)
    kproj(hT, 'hT0', 128, NT, 0)
    for j in range(2):
        pq, kq = b.nb([0, 1, 2, 3, 4, 5])
        for kc in range(8):
            b.mm(pq[0:64, 0:128], win[:, kc, 1536 + j * 64:1536 + (j + 1) * 64], hT[:, kc, 128:256], kc == 0, kc == 7, ['win', 'hT0'], [kq])
        pp, kp = b.nb([0, 1, 2, 3, 4, 5])
        for kc in range(8):
            b.mm(pp[0:64, 0:128], wpm[:, kc, 512 + j * 64:512 + (j + 1) * 64], hT[:, kc, 128:256], kc == 0, kc == 7, ['wpm', 'hT0'], [kp])
        rope_epilogue(b, W, pq, kq, pp, kp, NT + 128, 128, kT[:, j, NT + 128:NT + 256], 'kT')
    vproj(hT, 'hT0', 1, 0)
    for kc in range(8):
        b.mm(B[6][:, 0:128], hT[:, kc, 128:256], win[:, kc, 1664:1792], kc == 0, kc == 7, ['win', 'hT0'], ['pb6'])
    b.cp('act', Vt[:, 17, :], B[6][:, 0:128], ['pb6'], ['Vt'])
    for cc in range(4):
        pa, ka = b.nb([0, 1, 2, 3, 4, 5])
        proj_fm(b, pa, ka, 128, win, 'win', cc * 128, hT, 'hT0', 0, 256)
        pg, kg = b.nb([0, 1, 2, 3, 4, 5])
        proj_fm(b, pg, kg, 128, win, 'win', 512 + cc * 128, hT, 'hT0', 0, 256)
        b.act(sig[0][:, 0:256], pg[:, 0:256], AF.Sigmoid, [kg], ['sig0'])
        b.tt('dve', atmp[:], pa[:, 0:256], sig[0][:, 0:256], ALU.mult, [ka, 'sig0'], ['atmp'])
        b.cp('act', aT[:, cc, 0:15], atmp[:, 113:128], ['atmp'], ['aT'])
        b.cp('act', aT[:, cc, NT + 15:NT + 30], atmp[:, 128:143], ['atmp'], ['aT'])
    for g in range(4):
        hi = g % 2
        hT = W['hT'][hi]
        hk = 'hT%d' % hi
        norm_tiles_bf16(b, row_tiles(c['x_own'][g * 512:(g + 1) * 512, :], 4), W['gcol'][:], hT, hk, W)
        for cc in range(4):
            pa, ka = b.nb([0, 1, 2, 3, 4, 5])
            proj_fm(b, pa, ka, 128, win, 'win', cc * 128, hT, hk, 0, 512)
            pg, kg = b.nb([0, 1, 2, 3, 4, 5])
            proj_fm(b, pg, kg, 128, win, 'win', 512 + cc * 128, hT, hk, 0, 512)
            si = cc % 2
            b.act(sig[si][:], pg[:, :], AF.Sigmoid, [kg], ['sig%d' % si])
            b.tt('dve', aT[:, cc, 15 + g * 512:15 + (g + 1) * 512], pa[:, :], sig[si][:], ALU.mult, [ka, 'sig%d' % si], ['aT'])
        for h in range(8):
            pq, kq = b.nb([0, 1, 2, 3, 4, 5])
            proj_fm(b, pq, kq, 64, win, 'win', 1024 + h * 64, hT, hk, 0, 512)
            pp, kp = b.nb([0, 1, 2, 3, 4, 5])
            proj_fm(b, pp, kp, 64, wpm, 'wpm', h * 64, hT, hk, 0, 512)
            rope_epilogue(b, W, pq, kq, pp, kp, g * 512, 512, qT[:, h, g * 512:(g + 1) * 512], 'qT')
        for j in range(2):
            pq, kq = b.nb([0, 1, 2, 3, 4, 5])
            proj_fm(b, pq, kq, 64, win, 'win', 1536 + j * 64, hT, hk, 0, 512)
            pp, kp = b.nb([0, 1, 2, 3, 4, 5])
            proj_fm(b, pp, kp, 64, wpm, 'wpm', 512 + j * 64, hT, hk, 0, 512)
            rope_epilogue(b, W, pq, kq, pp, kp, g * 512, 512, kT[:, j, 128 + g * 512:128 + (g + 1) * 512], 'kT')
        vproj(hT, hk, 4, 1 + g * 4)
    b.reset(m1)
    catA = b.sb([128, 4, NT], BF16, 'catA')
    catH = b.sb([64, 8, NT], BF16, 'catH')
    m2_ = b.mark()
    mask = b.sb([128, 3, 384], BF16, 'mask')
    S.dma(mask[:], c['mask3'], (), ['mask'])
    esink = b.sb([64, 8], F32, 'esink')
    S.dma(esink[:], d['sink'].partition_broadcast(64), (), ['esink'])
    b.act(esink[:], esink[:], AF.Exp, ['esink'], ['esink'])
    pT = [b.sb([128, 384], BF16, 'pT%d' % i) for i in range(3)]
    dn = [b.sb([64, 128], F32, 'dn%d' % i) for i in range(2)]
    def w_scores(h, qb):
        kv = h // 4
        msel = 1 if qb == 0 else (2 if qb == 15 else 0)
        bs_, ks = b.nb([0, 1, 2])
        b.mm(bs_[:, 0:384], b.identb[:], mask[:, msel, :], True, False, ['identb', 'mask'], [ks])
        for j in range(3):
            b.mm(bs_[:, j * 128:(j + 1) * 128], kT[:, kv, (qb + j) * 128:(qb + j + 1) * 128], qT[:, h, qb * 128:(qb + 1) * 128],
                 False, j == 2, ['kT', 'qT'], [ks])
        return bs_, ks
    items = [(h, qb) for h in range(8) for qb in range(16)]
    pend = [w_scores(*items[0])]
    for it, (h, qb) in enumerate(items):
        kv = h // 4
        bs_, ks = pend.pop(0)
        pi = it % 3
        b.act(pT[pi][:], bs_[:, 0:384], AF.Exp, [ks], ['pT%d' % pi], scale=0.125)
        if it + 1 < len(items):
            pend.append(w_scores(*items[it + 1]))
        bv, kvk = b.nb([3, 4, 5])
        for j in range(3):
            b.mm(bv[0:64, 0:128], Vt[:, qb + j, kv * 64:(kv + 1) * 64], pT[pi][:, j * 128:(j + 1) * 128], j == 0, j == 2, ['Vt', 'pT%d' % pi], [kvk])
        for j in range(3):
            b.mm(bv[0:64, 128:256], b.onesb[:, 0:64], pT[pi][:, j * 128:(j + 1) * 128], j == 0, j == 2, ['onesb', 'pT%d' % pi], [kvk])
        di = it % 2
        b.ts('dve', dn[di][:], bv[0:64, 128:256], esink[:, h:h + 1], None, ALU.add, None, [kvk, 'esink'], ['dn%d' % di])
        b.recip(dn[di][:], dn[di][:], ['dn%d' % di], ['dn%d' % di])
        b.tt('dve', catH[:, h, qb * 128:(qb + 1) * 128], bv[0:64, 0:128], dn[di][:], ALU.mult, [kvk, 'dn%d' % di], ['catH'])
    cw = b.sb([128, 4, 31], F32, 'cw')
    cvec = b.sb([128, 3, 4], F32, 'cvec')
    S.dma(cw[:], d['cw'], (), ['cw'])
    S.dma(cvec[:], d['cvec'], (), ['cvec'])
    acc = b.sb([128, 4, 512], F32, 'acc')
    sq = [b.sb([128, 512], F32, 'sq%d' % i) for i in range(2)]
    mean = b.sb([128, 512], F32, 'mean')
    m2 = b.sb([128, 512], F32, 'm2')
    var = b.sb([128, 512], F32, 'var')
    yb = [b.sb([128, 512], F32, 'yb%d' % i) for i in range(2)]
    for tg in range(4):
        for cc in range(4):
            a0 = tg * 512
            b.ts('dve', acc[:, cc, :], aT[:, cc, a0:a0 + 512], cw[:, cc, 0:1], cvec[:, 0, cc:cc + 1], ALU.mult, ALU.add, ['aT', 'cw', 'cvec'], ['acc%d' % cc])
            for j in range(1, 31):
                b.stt(acc[:, cc, :], aT[:, cc, a0 + j:a0 + j + 512], cw[:, cc, j:j + 1], acc[:, cc, :], ALU.mult, ALU.add, ['aT', 'cw', 'acc%d' % cc], ['acc%d' % cc])
        for cc in range(4):
            si = cc % 2
            b.act(sq[si][:], acc[:, cc, :], AF.Square, ['acc%d' % cc], ['sq%d' % si])
            b.mm(B[6][:, :], b.onesf[:], acc[:, cc, :], cc == 0, cc == 3, ['onesf', 'acc%d' % cc], ['pb6'])
            b.mm(B[0][:, :], b.onesf[:], sq[si][:], cc == 0, cc == 3, ['onesf', 'sq%d' % si], ['pb0'])
        b.act(mean[:], B[6][:, :], AF.Copy, ['pb6'], ['mean'], scale=1.0 / 512)
        b.act(m2[:], B[6][:, :], AF.Square, ['pb6'], ['m2'], scale=1.0 / 512)
        b.stt(var[:], B[0][:, :], 1.0 / 512, m2[:], ALU.mult, ALU.subtract, ['pb0', 'm2'], ['var'])
        b.act(var[:], var[:], AF.Sqrt, ['var'], ['var'], bias=EPS)
        b.recip(var[:], var[:], ['var'], ['var'])
        for cc in range(4):
            yi = cc % 2
            b.tt('dve', yb[yi][:], acc[:, cc, :], mean[:], ALU.subtract, ['acc%d' % cc, 'mean'], ['yb%d' % yi])
            b.tt('pool', yb[yi][:], yb[yi][:], var[:], ALU.mult, ['yb%d' % yi, 'var'], ['yb%d' % yi])
            b.ts('pool', yb[yi][:], yb[yi][:], cvec[:, 1, cc:cc + 1], cvec[:, 2, cc:cc + 1], ALU.mult, ALU.add, ['yb%d' % yi, 'cvec'], ['yb%d' % yi])
            b.act(catA[:, cc, tg * 512:(tg + 1) * 512], yb[yi][:], AF.Silu, ['yb%d' % yi], ['catA'])
    b.reset(m2_)
    out_proj(b, d, c, catA, catH)
    b.reset(m0)


def emit_mixer_odd(b, d, c):
    S = b.S
    B = b.banks
    m0 = b.mark()
    NA = 2 * NT
    qT = b.sb([64, 8, NT], BF16, 'qT')
    kT = b.sb([64, 2, NA], BF16, 'kT')
    Vt = b.sb([128, 32, 128], BF16, 'Vt')
    uT = b.sb([128, 4, NT], BF16, 'uT')
    vn = b.sb([128, 16, 512], BF16, 'vn')
    m1 = b.mark()
    W = mixer_common_alloc(b, d, 512, None)
    win, wpm = W['win'], W['wpm']
    gq = b.sb([64, 4], F32, 'gq')
    S.dma(gq[:], d['gqk'], (), ['gq'])
    sqb = [b.sb([64, 512], F32, 'sqb%d' % i) for i in range(2)]
    rsb = [b.sb([64, 512], F32, 'rsb%d' % i) for i in range(2)]
    vg = [b.sb([128, 512], F32, 'vg%d' % i) for i in range(2)]
    st6 = b.sb([128, 6], F32, 'st6')
    mv = b.sb([128, 2], F32, 'mv')
    lnr = b.sb([128, 2], F32, 'lnr')
    gbc = b.sb([128, 2, 512], F32, 'gbc')
    S.dma(gbc[:, 0, :], d['sgu_g'].partition_broadcast(128), (), ['gbc'])
    S.dma(gbc[:, 1, :], d['sgu_b'].partition_broadcast(128), (), ['gbc'])
    cnt = [0]

    def qk_head(hT, hk, wcol, pcol, gcol_i, out, okey):
        i = cnt[0] % 2
        cnt[0] += 1
        pq, kq = b.nb([0, 1, 2, 3, 4, 5])
        proj_fm(b, pq, kq, 64, win, 'win', wcol, hT, hk, 0, 512)
        pp, kp = b.nb([0, 1, 2, 3, 4, 5])
        proj_fm(b, pp, kp, 64, wpm, 'wpm', pcol, hT, hk, 0, 512)
        b.act(sqb[i][:], pq[0:64, :], AF.Square, [kq], ['sqb%d' % i])
        ps_, kss = b.nb([0, 1, 2, 3, 4, 5])
        b.mm(ps_[0:64, :], b.onesf[0:64, 0:64], sqb[i][:], True, True, ['onesf', 'sqb%d' % i], [kss])
        b.act(rsb[i][:], ps_[0:64, :], AF.Sqrt, [kss], ['rsb%d' % i], scale=1.0 / 64, bias=EPS)
        b.recip(rsb[i][:], rsb[i][:], ['rsb%d' % i], ['rsb%d' % i])
        t1 = W['t1'][i]
        t2 = W['t2'][i]
        b.stt(t1[0:64, :], pq[0:64, :], gq[:, gcol_i:gcol_i + 1], W['rope'][:, 0, :], ALU.mult, ALU.mult, [kq, 'gq', 'rope'], ['t1_%d' % i])
        b.stt(t2[0:64, :], pp[0:64, :], gq[:, gcol_i + 1:gcol_i + 2], W['rope'][:, 1, :], ALU.mult, ALU.mult, [kp, 'gq', 'rope'], ['t2_%d' % i])
        b.tt('pool', t1[0:64, :], t1[0:64, :], t2[0:64, :], ALU.add, ['t1_%d' % i, 't2_%d' % i], ['t1_%d' % i])
        b.tt('pool', out, t1[0:64, :], rsb[i][:], ALU.mult, ['t1_%d' % i, 'rsb%d' % i], [okey])

    for g in range(8):
        own = g < 4
        hi = g % 2
        hT = W['hT'][hi]
        hk = 'hT%d' % hi
        src = c['x_own'][g * 512:(g + 1) * 512, :] if own else c['x_ctx'][(g - 4) * 512:(g - 3) * 512, :]
        S.dma(W['rope'][:], c['rope'][:, :, g * 512:(g + 1) * 512], (), ['rope'])
        norm_tiles_bf16(b, row_tiles(src, 4), W['gcol'][:], hT, hk, W)
        if own:
            for h in range(8):
                qk_head(hT, hk, h * 64, h * 64, 0, qT[:, h, g * 512:(g + 1) * 512], 'qT')
        for j in range(2):
            qk_head(hT, hk, 512 + j * 64, 512 + j * 64, 2, kT[:, j, g * 512:(g + 1) * 512], 'kT')
        for tt in range(4):
            for kc in range(8):
                b.mm(B[6][:, 0:128], hT[:, kc, tt * 128:(tt + 1) * 128], win[:, kc, 640:768], kc == 0, kc == 7, ['win', hk], ['pb6'])
            b.cp('act', Vt[:, g * 4 + tt, :], B[6][:, 0:128], ['pb6'], ['Vt'])
        if own:
            for cc in range(4):
                pu, ku = b.nb([0, 1, 2, 3, 4, 5])
                proj_fm(b, pu, ku, 128, win, 'win', 768 + cc * 128, hT, hk, 0, 512)
                b.act(uT[:, cc, g * 512:(g + 1) * 512], pu[:, :], AF.Gelu_apprx_tanh, [ku], ['uT'])
            for tt in range(4):
                T = g * 4 + tt
                vi = tt % 2
                pv, kv_ = b.nb([0, 1, 2, 3, 4, 5])
                for kc in range(8):
                    b.mm(pv[:, :], hT[:, kc, tt * 128:(tt + 1) * 128], win[:, kc, 1280:1792], kc == 0, kc == 7, ['win', hk], [kv_])
                b.act(vg[vi][:], pv[:, :], AF.Gelu_apprx_tanh, [kv_], ['vg%d' % vi])
                S.op('dve', lambda e, vi=vi: e.bn_stats(out=st6[:], in_=vg[vi][:]), ['vg%d' % vi], ['st6'])
                S.op('dve', lambda e: e.bn_aggr(out=mv[:], in_=st6[:]), ['st6'], ['mv'])
                b.act(lnr[:, 0:1], mv[:, 1:2], AF.Sqrt, ['mv'], ['lnr'], bias=EPS)
                b.recip(lnr[:, 1:2], lnr[:, 0:1], ['lnr'], ['lnr'])
                b.ts('dve', vg[vi][:], vg[vi][:], mv[:, 0:1], lnr[:, 1:2], ALU.subtract, ALU.mult, ['vg%d' % vi, 'mv', 'lnr'], ['vg%d' % vi])
                b.tt('pool', vg[vi][:], vg[vi][:], gbc[:, 0, :], ALU.mult, ['vg%d' % vi, 'gbc'], ['vg%d' % vi])
                b.tt('pool', vn[:, T, :], vg[vi][:], gbc[:, 1, :], ALU.add, ['vg%d' % vi, 'gbc'], ['vn'])
    b.reset(m1)
    catA = b.sb([128, 4, NT], BF16, 'catA')
    catH = b.sb([64, 8, NT], BF16, 'catH')
    m2_ = b.mark()
    pT = [b.sb([128, 512], BF16, 'pT%d' % i) for i in range(3)]
    rden = [b.sb([64, 512], F32, 'rden%d' % i) for i in range(2)]
    it = 0
    n = 0
    for h in range(8):
        kv = h // 4
        for qg in range(4):
            bv = 3 + (it % 2) * 2
            bd = 4 + (it % 2) * 2
            sb_ = {}

            def s_mm(kb, h=h, kv=kv, qg=qg):
                bs_, ks = b.nb([0, 1, 2])
                sb_[kb] = (bs_, ks)
                b.mm(bs_[:, :], kT[:, kv, kb * 128:(kb + 1) * 128], qT[:, h, qg * 512:(qg + 1) * 512], True, True, ['kT', 'qT'], [ks])
            s_mm(0)
            s_mm(1)
            for kb in range(32):
                bs_, ks = sb_[kb]
                pi = n % 3
                n += 1
                b.act(pT[pi][:], bs_[:, :], AF.Exp, [ks], ['pT%d' % pi], scale=0.125)
                if kb + 2 < 32:
                    s_mm(kb + 2)
                b.mm(B[bv][0:64, :], Vt[:, kb, kv * 64:(kv + 1) * 64], pT[pi][:], kb == 0, kb == 31, ['Vt', 'pT%d' % pi], ['pb%d' % bv])
                b.mm(B[bd][0:64, :], b.onesb[:, 0:64], pT[pi][:], kb == 0, kb == 31, ['onesb', 'pT%d' % pi], ['pb%d' % bd])
            ri = it % 2
            b.recip(rden[ri][:], B[bd][0:64, :], ['pb%d' % bd], ['rden%d' % ri])
            b.tt('dve', catH[:, h, qg * 512:(qg + 1) * 512], B[bv][0:64, :], rden[ri][:], ALU.mult, ['pb%d' % bv, 'rden%d' % ri], ['catH'])
            it += 1
    wsT = b.sb([128, 4, 128], BF16, 'wsT')
    wsst = b.sb([128, 4, 128], F32, 'wsst')
    S.dma(wsst[:], d['wsT'], (), ['wsst'])
    b.cp('pool', wsT[:], wsst[:], ['wsst'], ['wsT'])
    bsb = b.sb([128, 512], F32, 'bsb')
    S.dma(bsb[:], d['sgu_bs'].partition_broadcast(128), (), ['bsb'])
    tmx = [b.sb([128, 512], F32, 'tmx%d' % i) for i in range(2)]
    for T in range(16):
        bm, km = b.nb([0, 1, 2])
        for g in range(4):
            b.mm(bm[:, g * 128:(g + 1) * 128], vn[:, T, g * 128:(g + 1) * 128], wsT[:, g, :], True, True, ['vn', 'wsT'], [km])
        ti = T % 2
        b.tt('dve', tmx[ti][:], bm[:, :], bsb[:], ALU.add, [km, 'bsb'], ['tmx%d' % ti])
        b.tt('pool', catA[:, :, T * 128:(T + 1) * 128], tmx[ti][:].rearrange("p (g t) -> p g t", g=4), uT[:, :, T * 128:(T + 1) * 128],
             ALU.mult, ['tmx%d' % ti, 'uT'], ['catA'])
    b.reset(m2_)
    out_proj(b, d, c, catA, catH)
    b.reset(m0)


LAYER_SHAPES = {
    'gmix': [128, 8], 'w_in': [128, 8, 1792], 'w_perm': [128, 8, 640], 'w_outA': [128, 4, 1024], 'w_outH': [64, 8, 1024],
    'gffn': [128, 8], 'wq': [128, 8, 1024], 'skbd': [128, 8, 256], 'uth': [128, 128, 1024], 'vbh': [128, 128, 1024],
}
EVEN_SHAPES = {'sink': [8], 'cw': [128, 4, 31], 'cvec': [128, 3, 4]}
ODD_SHAPES = {'gqk': [64, 4], 'sgu_g': [512], 'sgu_b': [512], 'wsT': [128, 4, 128], 'sgu_bs': [512]}


def build_fused_program(nlayers=4):
    nc = bass.Bass("TRN2", target_bir_lowering=False)

    def inp(name, shape, dt=F32):
        return nc.dram_tensor(name, list(shape), dt, kind="ExternalInput").ap()
    x_in = inp('x', [SEQ, D])
    rope_e = inp('rope_e', [2, 64, 2, NT + 256])
    rope_o = inp('rope_o', [2, 64, 2, 2 * NT])
    mask3 = inp('mask3', [2, 128, 3, 384], BF16)
    zrow = inp('zrow', [128, D])
    gfin = inp('gfin', [D])
    out = nc.dram_tensor('out', [SEQ, D], F32, kind="ExternalOutput").ap()
    xa = nc.dram_tensor('xa', [SEQ, D], F32).ap()
    xb = nc.dram_tensor('xb', [SEQ, D], F32).ap()
    x1 = nc.dram_tensor('x1', [SEQ, D], F32).ap()
    uts = nc.dram_tensor('uts', [128 // JG, 128, JG, 1024], BF16).ap()
    vbs = nc.dram_tensor('vbs', [128 // JG, 128, JG, 1024], BF16).ap()
    with ExitStack() as st:
        b = Bld(nc, st)
        b.consts()
        cur = x_in
        for L in range(nlayers):
            even = L % 2 == 0
            last = L == nlayers - 1
            d = {'uts': uts, 'vbs': vbs, 'gfin': gfin}
            shapes = dict(LAYER_SHAPES)
            shapes.update(EVEN_SHAPES if even else ODD_SHAPES)
            for k, shp in shapes.items():
                d[k] = inp('L%d_%s' % (L, k), shp)
            nxt = out if last else (xa if L % 2 == 0 else xb)
            for f in prep_dma_list(b, d):
                b.S.bg_dma('pool', f)
            for half in range(2):
                o0 = half * NT
                o1 = (1 - half) * NT
                c = {'x_own': cur[o0:o0 + NT, :], 'x1': x1[o0:o0 + NT, :]}
                if even:
                    c['ctxL'] = cur[o0 - 128:o0, :] if half == 1 else zrow
                    c['ctxR'] = cur[o0 + NT:o0 + NT + 128, :] if half == 0 else zrow
                    c['rope'] = rope_e[half]
                    c['mask3'] = mask3[half]
                    emit_mixer_even(b, d, c)
                else:
                    c['x_ctx'] = cur[o1:o1 + NT, :]
                    c['rope'] = rope_o[half]
                    emit_mixer_odd(b, d, c)
            emit_peer(b, d, last, x1, nxt, SEQ)
            cur = nxt
            print('layer', L, 'instructions', b.S.n_instr, {e: b.S.cnt[e] for e in ENGS}, 'dmas', b.S.ndma, flush=True)
            if not last:
                b.S.new_epoch(st)
        b.S.emit_all()
    return nc


PERM_1D = np.concatenate([np.arange(32, 64), np.arange(0, 32)])
PERM_AX = np.concatenate([np.arange(16, 32), np.arange(0, 16), np.arange(48, 64), np.arange(32, 48)])
_f = np.float32


def _fm(v, c):
    return np.ascontiguousarray(v.reshape(c, 128).T)


def host_layer_inputs(layer, inp):
    i = layer // 2
    even = layer % 2 == 0
    o = {}
    o['gmix'] = _fm(inp['mix_norm_g'][layer], 8)
    o['gffn'] = _fm(inp['ffn_norm_g'][layer], 8)
    Win = inp['even_w_in'][i] if even else inp['odd_w_in'][i]
    Wout = inp['even_w_out'][i] if even else inp['odd_w_out'][i]
    o['w_in'] = np.ascontiguousarray(Win.reshape(8, 128, 1792).transpose(1, 0, 2))
    qk0 = 1024 if even else 0
    perm = PERM_1D if even else PERM_AX
    qk = Win[:, qk0:qk0 + 640].reshape(1024, 10, 64)[:, :, perm].reshape(1024, 640)
    o['w_perm'] = np.ascontiguousarray(qk.reshape(8, 128, 640).transpose(1, 0, 2))
    A0, H0 = (0, 512) if even else (512, 0)
    o['w_outA'] = np.ascontiguousarray(Wout[A0:A0 + 512].reshape(4, 128, 1024).transpose(1, 0, 2))
    o['w_outH'] = np.ascontiguousarray(Wout[H0:H0 + 512].reshape(8, 64, 1024).transpose(1, 0, 2))
    if even:
        o['sink'] = np.ascontiguousarray(inp['sink_logits'][i])
        o['cw'] = np.ascontiguousarray(inp['conv_w'][i][:, 0, :].reshape(31, 4, 128).transpose(2, 1, 0))
        o['cvec'] = np.ascontiguousarray(np.stack([_fm(inp['conv_b'][i], 4), _fm(inp['conv_ln_g'][i], 4), _fm(inp['conv_ln_b'][i], 4)], axis=1))
    else:
        qg, kg = inp['q_norm_g'][i], inp['k_norm_g'][i]
        o['gqk'] = np.ascontiguousarray(np.stack([qg, qg[PERM_AX], kg, kg[PERM_AX]], axis=1))
        o['sgu_g'] = np.ascontiguousarray(inp['sgu_ln_g'][i])
        o['sgu_b'] = np.ascontiguousarray(inp['sgu_ln_b'][i])
        o['wsT'] = np.ascontiguousarray(inp['sgu_w'][i].transpose(2, 0, 1))
        o['sgu_bs'] = np.ascontiguousarray(inp['sgu_b'][i].reshape(512))
    o['wq'] = np.ascontiguousarray(inp['peer_wq'][layer].reshape(8, 128, 1024).transpose(1, 0, 2))
    sk = inp['peer_subkeys'][layer]
    skbd = np.zeros((128, 8, 256), _f)
    skbd[0:64, :, 0:128] = sk[:, 0].transpose(2, 0, 1)
    skbd[64:128, :, 128:256] = sk[:, 1].transpose(2, 0, 1)
    o['skbd'] = skbd
    U = inp['peer_u'][layer]
    V = inp['peer_v'][layer]
    o['uth'] = np.ascontiguousarray(U.reshape(128, 128, 8, 128).transpose(1, 3, 2, 0)).reshape(128, 128, 1024)
    o['vbh'] = np.ascontiguousarray(V.reshape(128, 128, 1024).transpose(1, 0, 2))
    return o


def rope_tables_even(pos):
    inv = (_f(10000.0) ** (-np.arange(0, 64, 2, dtype=_f) / _f(64))).astype(_f)
    ang = pos.astype(_f)[None, :] * inv[:, None]
    ang = np.concatenate([ang, ang], axis=0)
    sgn = np.concatenate([-np.ones(32, _f), np.ones(32, _f)])[:, None]
    return np.stack([np.cos(ang), np.sin(ang) * sgn], axis=1).astype(_f)


def rope_tables_axial(pos):
    inv = (_f(10000.0) ** (-np.arange(0, 32, 2, dtype=_f) / _f(32))).astype(_f)
    rows = (pos // 64).astype(_f)
    cols = (pos % 64).astype(_f)
    ar = rows[None, :] * inv[:, None]
    ac = cols[None, :] * inv[:, None]
    ang = np.concatenate([ar, ar, ac, ac], axis=0)
    sgn = np.concatenate([-np.ones(16, _f), np.ones(16, _f), -np.ones(16, _f), np.ones(16, _f)])[:, None]
    return np.stack([np.cos(ang), np.sin(ang) * sgn], axis=1).astype(_f)


def window_masks(half):
    k = np.arange(128)[:, None]
    q = np.arange(128)[None, :]
    left = np.where(k >= q, 0.0, NEGM)
    mid = np.zeros((128, 128))
    right = np.where(k <= q, 0.0, NEGM)
    full = np.full((128, 128), NEGM)
    m_mid = np.concatenate([left, mid, right], axis=1)
    m_first = np.concatenate([full if half == 0 else left, mid, right], axis=1)
    m_last = np.concatenate([left, mid, full if half == 1 else right], axis=1)
    return np.stack([m_mid, m_first, m_last], axis=1).astype(ml_dtypes.bfloat16)


def host_const_inputs():
    o = {}
    re, ro, mk = [], [], []
    for half in range(2):
        o0 = half * NT
        o1 = (1 - half) * NT
        pos = np.concatenate([o0 + np.arange(NT), o0 - 128 + np.arange(128), o0 + NT + np.arange(128)])
        re.append(rope_tables_even(pos))
        pos = np.concatenate([o0 + np.arange(NT), o1 + np.arange(NT)])
        ro.append(rope_tables_axial(pos))
        mk.append(window_masks(half))
    o['rope_e'] = np.stack(re)
    o['rope_o'] = np.stack(ro)
    o['mask3'] = np.stack(mk)
    o['zrow'] = np.zeros((128, D), _f)
    return o


_PROG = {}


def kernel(**inputs):
    inp = {k: np.asarray(v) for k, v in inputs.items()}
    x = np.ascontiguousarray(inp['x'], dtype=_f)
    if 'nc' not in _PROG:
        _PROG['nc'] = build_fused_program()
    nc = _PROG['nc']
    shared = host_const_inputs()
    shared['gfin'] = np.ascontiguousarray(inp['final_norm_g'])
    for L in range(4):
        for k, v in host_layer_inputs(L, inp).items():
            shared['L%d_%s' % (L, k)] = v
    in_maps = []
    for c in range(8):
        m = dict(shared)
        m['x'] = np.ascontiguousarray(x[c % 4])
        in_maps.append(m)
    res = run_bass_kernel_spmd(nc, in_maps, core_ids=list(range(8)))
    return np.stack([res.results[c]['out'] for c in range(4)], axis=0)
```
